# Optimizing a Trainium2 kernel written in Bass

```python
import math
import jax, jax.numpy as jnp
from jax import lax
import numpy as np

D_MODEL = 1024
BATCH = 8
SEQ = 4096
DEPTH = 1

GRID_W = 64
BLOCK = 128
WINDOW = 128
A_HEADS = 8
A_KV_HEADS = 2
A_HEAD_DIM = 64
B_HEADS = 4
B_KV_HEADS = 2
B_HEAD_DIM = 128
ROPE_THETA = 10000.0
D_FF = 2816
CONV_W = 3
LN_EPS = 1e-5
RMS_EPS = 1e-6
DEEPNORM_ALPHA = float((2 * DEPTH) ** 0.25)
DEEPNORM_BETA = float((8 * DEPTH) ** -0.25)

A_Q = A_HEADS * A_HEAD_DIM
A_KV = A_KV_HEADS * A_HEAD_DIM
B_Q = B_HEADS * B_HEAD_DIM
B_KV = B_KV_HEADS * B_HEAD_DIM
IN_WIDTHS = (A_Q, A_KV, A_KV, B_Q, B_KV, B_KV, D_MODEL, D_MODEL)
IN_SPLIT_POINTS = tuple(int(v) for v in np.cumsum(IN_WIDTHS)[:-1])
IN_TOTAL = int(sum(IN_WIDTHS))

kernel_name = "hybrid_gated_window_axial_gqa_convglu_deepnorm"


def layer_norm(x, g, b):
    xf = x.astype(jnp.float32)
    mu = xf.mean(-1, keepdims=True)
    var = jnp.square(xf - mu).mean(-1, keepdims=True)
    return ((xf - mu) * lax.rsqrt(var + LN_EPS) * g.astype(jnp.float32) + b.astype(jnp.float32)).astype(x.dtype)


def rms_norm(x, scale):
    xf = x.astype(jnp.float32)
    y = xf * lax.rsqrt(jnp.mean(xf * xf, axis=-1, keepdims=True) + RMS_EPS) * scale.astype(jnp.float32)
    return y.astype(x.dtype)


def alibi_slopes(n_heads):
    return jnp.exp2(-8.0 * jnp.arange(1, n_heads + 1, dtype=jnp.float32) / n_heads)


def rotate_axis(xs, pos):
    n = xs.shape[-1] // 2
    freqs = ROPE_THETA ** (-jnp.arange(n, dtype=jnp.float32) / n)
    ang = pos.astype(jnp.float32)[:, None] * freqs[None, :]
    cos = jnp.cos(ang)[None, :, None, :]
    sin = jnp.sin(ang)[None, :, None, :]
    x1 = xs[..., :n].astype(jnp.float32)
    x2 = xs[..., n:].astype(jnp.float32)
    return jnp.concatenate([x1 * cos - x2 * sin, x1 * sin + x2 * cos], axis=-1)


def axial_rope(x, rows, cols):
    half = x.shape[-1] // 2
    y = jnp.concatenate([rotate_axis(x[..., :half], rows), rotate_axis(x[..., half:], cols)], axis=-1)
    return y.astype(x.dtype)


def window_attention(q, k, v, sinks):
    bsz, s_len, hq, dh = q.shape
    hkv = k.shape[2]
    grp = hq // hkv
    nb = s_len // BLOCK
    qb = q.reshape(bsz, nb, BLOCK, hkv, grp, dh)
    pad = ((0, 0), (BLOCK, BLOCK), (0, 0), (0, 0))

    def band(t):
        tp = jnp.pad(t, pad).reshape(bsz, nb + 2, BLOCK, hkv, dh)
        return jnp.concatenate([tp[:, :-2], tp[:, 1:-1], tp[:, 2:]], axis=2)

    kb, vb = band(k), band(v)
    scores = jnp.einsum('bnqkgd,bnskd->bnkgqs', qb, kb).astype(jnp.float32) * (dh ** -0.5)

    blk = jnp.arange(nb)
    qpos = blk[:, None] * BLOCK + jnp.arange(BLOCK)[None, :]
    kpos = (blk[:, None] - 1) * BLOCK + jnp.arange(3 * BLOCK)[None, :]
    dist = jnp.abs(qpos[:, :, None] - kpos[:, None, :])
    valid = (dist <= WINDOW) & (kpos[:, None, :] >= 0) & (kpos[:, None, :] < s_len)
    slopes = alibi_slopes(hq).reshape(hkv, grp)
    bias = -slopes[None, :, :, None, None] * dist.astype(jnp.float32)[:, None, None, :, :]
    neg = jnp.finfo(jnp.float32).min
    scores = jnp.where(valid[:, None, None, :, :], scores + bias, neg)

    sink = sinks.astype(jnp.float32).reshape(hkv, grp)[:, :, None, None]
    m = jnp.maximum(scores.max(-1, keepdims=True), sink)
    p = jnp.exp(scores - m)
    p = p / (p.sum(-1, keepdims=True) + jnp.exp(sink - m))
    o = jnp.einsum('bnkgqs,bnskd->bnqkgd', p.astype(v.dtype), vb)
    return o.reshape(bsz, s_len, hq * dh)


def global_attention(q, k, v):
    bsz, s_len, hq, dh = q.shape
    hkv = k.shape[2]
    grp = hq // hkv
    nb = s_len // BLOCK
    qb = q.reshape(bsz, nb, BLOCK, hkv, grp, dh).transpose(1, 0, 2, 3, 4, 5)
    scale = dh ** -0.5

    def one_block(qi):
        s = jnp.einsum('bqkgd,bskd->bkgqs', qi, k).astype(jnp.float32) * scale
        p = jax.nn.softmax(s, axis=-1).astype(v.dtype)
        return jnp.einsum('bkgqs,bskd->bqkgd', p, v)

    o = lax.map(one_block, qb)
    return o.transpose(1, 0, 2, 3, 4, 5).reshape(bsz, s_len, hq * dh)


def token_mixer(h, w_in, b_in, a_sinks, b_q_norm, b_k_norm, w_o_a, w_o_b, w_out, rows, cols):
    bsz, s_len, _ = h.shape
    u = h @ w_in + b_in
    qa, ka, va, qb, kb, vb, ga, gb = jnp.split(u, IN_SPLIT_POINTS, axis=-1)
    qa = qa.reshape(bsz, s_len, A_HEADS, A_HEAD_DIM)
    ka = ka.reshape(bsz, s_len, A_KV_HEADS, A_HEAD_DIM)
    va = va.reshape(bsz, s_len, A_KV_HEADS, A_HEAD_DIM)
    oa = window_attention(qa, ka, va, a_sinks)
    qb = axial_rope(rms_norm(qb.reshape(bsz, s_len, B_HEADS, B_HEAD_DIM), b_q_norm), rows, cols)
    kb = axial_rope(rms_norm(kb.reshape(bsz, s_len, B_KV_HEADS, B_HEAD_DIM), b_k_norm), rows, cols)
    vb = vb.reshape(bsz, s_len, B_KV_HEADS, B_HEAD_DIM)
    ob = global_attention(qb, kb, vb)
    merged = jax.nn.sigmoid(ga) * (oa @ w_o_a) + jax.nn.sigmoid(gb) * (ob @ w_o_b)
    return merged @ w_out


def conv_glu(h, w_gate, w_val, conv_w, conv_b, w_down):
    g = h @ w_gate
    g = lax.conv_general_dilated(
        g, conv_w[:, None, :].astype(g.dtype), window_strides=(1,),
        padding=((CONV_W // 2, CONV_W // 2),),
        dimension_numbers=('NWC', 'WIO', 'NWC'),
        feature_group_count=g.shape[-1]) + conv_b
    act = jax.nn.gelu(g, approximate=False) * (h @ w_val)
    return act @ w_down


def setup_inputs(seed: int = 0) -> dict:
    key = jax.random.key(seed)
    ks = jax.random.split(key, 24)
    f32 = jnp.float32

    def nrm(k, shape, scale):
        return jax.random.normal(k, shape, f32) * scale

    x = jax.random.normal(ks[0], (BATCH, SEQ, D_MODEL), f32)
    ln_in_g = 1.0 + nrm(ks[1], (D_MODEL,), 0.02)
    ln_in_b = nrm(ks[2], (D_MODEL,), 0.02)
    col_scale = jnp.concatenate([
        jnp.full((w,), DEEPNORM_BETA if i in (2, 5) else 1.0, f32) for i, w in enumerate(IN_WIDTHS)])
    w_in = nrm(ks[3], (DEPTH, D_MODEL, IN_TOTAL), D_MODEL ** -0.5) * col_scale
    b_in = nrm(ks[4], (DEPTH, IN_TOTAL), 0.02)
    a_sinks = nrm(ks[5], (DEPTH, A_HEADS), 0.5)
    b_q_norm = 1.0 + nrm(ks[6], (DEPTH, B_HEAD_DIM), 0.02)
    b_k_norm = 1.0 + nrm(ks[7], (DEPTH, B_HEAD_DIM), 0.02)
    w_o_a = nrm(ks[8], (DEPTH, A_Q, D_MODEL), A_Q ** -0.5 * DEEPNORM_BETA)
    w_o_b = nrm(ks[9], (DEPTH, B_Q, D_MODEL), B_Q ** -0.5 * DEEPNORM_BETA)
    w_out = nrm(ks[10], (DEPTH, D_MODEL, D_MODEL), D_MODEL ** -0.5 * DEEPNORM_BETA)
    ln1_g = 1.0 + nrm(ks[11], (DEPTH, D_MODEL), 0.02)
    ln1_b = nrm(ks[12], (DEPTH, D_MODEL), 0.02)
    w_ffn_gate = nrm(ks[13], (DEPTH, D_MODEL, D_FF), D_MODEL ** -0.5 * DEEPNORM_BETA)
    w_ffn_val = nrm(ks[14], (DEPTH, D_MODEL, D_FF), D_MODEL ** -0.5 * DEEPNORM_BETA)
    ffn_conv_w = nrm(ks[15], (DEPTH, CONV_W, D_FF), CONV_W ** -0.5)
    ffn_conv_b = nrm(ks[16], (DEPTH, D_FF), 0.02)
    w_ffn_down = nrm(ks[17], (DEPTH, D_FF, D_MODEL), D_FF ** -0.5 * DEEPNORM_BETA)
    ln2_g = 1.0 + nrm(ks[18], (DEPTH, D_MODEL), 0.02)
    ln2_b = nrm(ks[19], (DEPTH, D_MODEL), 0.02)
    return {"x": x, "ln_in_g": ln_in_g, "ln_in_b": ln_in_b, "w_in": w_in, "b_in": b_in,
            "a_sinks": a_sinks, "b_q_norm": b_q_norm, "b_k_norm": b_k_norm,
            "w_o_a": w_o_a, "w_o_b": w_o_b, "w_out": w_out, "ln1_g": ln1_g, "ln1_b": ln1_b,
            "w_ffn_gate": w_ffn_gate, "w_ffn_val": w_ffn_val, "ffn_conv_w": ffn_conv_w,
            "ffn_conv_b": ffn_conv_b, "w_ffn_down": w_ffn_down, "ln2_g": ln2_g, "ln2_b": ln2_b}


def reference(x, ln_in_g, ln_in_b, w_in, b_in, a_sinks, b_q_norm, b_k_norm, w_o_a, w_o_b, w_out,
              ln1_g, ln1_b, w_ffn_gate, w_ffn_val, ffn_conv_w, ffn_conv_b, w_ffn_down, ln2_g, ln2_b):
    s_len = x.shape[1]
    n_rows = s_len // GRID_W
    rows = jnp.repeat(jnp.arange(n_rows), GRID_W)
    cols = jnp.tile(jnp.arange(GRID_W), n_rows)
    h = layer_norm(x, ln_in_g, ln_in_b)
    for l in range(DEPTH):
        y = token_mixer(h, w_in[l], b_in[l], a_sinks[l], b_q_norm[l], b_k_norm[l],
                        w_o_a[l], w_o_b[l], w_out[l], rows, cols)
        h = layer_norm(DEEPNORM_ALPHA * h + y, ln1_g[l], ln1_b[l])
        y = conv_glu(h, w_ffn_gate[l], w_ffn_val[l], ffn_conv_w[l], ffn_conv_b[l], w_ffn_down[l])
        h = layer_norm(DEEPNORM_ALPHA * h + y, ln2_g[l], ln2_b[l])
    return h
```

```python
import numpy as np
import ml_dtypes
from contextlib import ExitStack
import concourse.bass as bass
import concourse.mybir as mybir
from concourse.bass_utils import run_bass_kernel_spmd

F32 = mybir.dt.float32
BF16 = mybir.dt.bfloat16
AF = mybir.ActivationFunctionType
ALU = mybir.AluOpType

S_LEN = 4096
D = 1024
NT = 32
DFF = 2816
NFC = 22
ALPHA = float(2.0 ** 0.25)
LN_EPS = 1e-5
RMS_EPS = 1e-6
CH2 = 256
NCH2 = S_LEN // CH2
CH3 = 512
NCH3 = S_LEN // CH3
C_QA, C_KA, C_VA, C_QB, C_KB, C_VB, C_GA, C_GB = 0, 512, 640, 768, 1280, 1536, 1792, 2816


class Buf:
    __slots__ = ("name", "last_w", "readers", "is_bank")

    def __init__(self, name, is_bank=False):
        self.name = name
        self.last_w = None
        self.readers = {}
        self.is_bank = is_bank


class Sem:
    def __init__(self, h):
        self.h = h
        self.count = 0


class Sched:
    ENG = ("pe", "act", "dve", "pool", "sp")

    def __init__(self, nc, es):
        self.nc = nc
        self.es = es
        self.ops = {e: [] for e in self.ENG}
        self.esem = {e: Sem(es.enter_context(nc.semaphore("sem_" + e))) for e in self.ENG if e != "sp"}
        self.seen = {e: {} for e in self.ENG}
        self.nsem = 0
        self.bankof = {}

    def new_sem(self, name):
        self.nsem += 1
        return Sem(self.es.enter_context(self.nc.semaphore("d_%s_%d" % (name, self.nsem))))

    def _waits(self, eng, r, w):
        need = {}

        def add(dep, war, bank=False):
            if dep is None:
                return
            sem, val = dep
            if war and sem is self.esem.get(eng) and (eng == "pe" or bank):
                return
            if need.get(sem, 0) < val:
                need[sem] = val
        for b in r:
            add(b.last_w, False)
        for b in w:
            add(b.last_w, True, b.is_bank)
            for sem, val in b.readers.items():
                add((sem, val), True, b.is_bank)
        out = []
        seen = self.seen[eng]
        for sem, val in need.items():
            if seen.get(sem, 0) < val:
                seen[sem] = val
                out.append((sem.h, val))
        return out

    def op(self, eng, fn, r=(), w=(), track=True):
        banks = []
        for b in list(r) + list(w):
            bk = self.bankof.get(b)
            if bk is not None and bk not in banks:
                banks.append(bk)
        if banks:
            w = list(w) + banks
        waits = self._waits(eng, r, w)
        sem = self.esem[eng]
        val = sem.count + 1
        if eng != "pe":
            track = True
        if track:
            sem.count = val
        self.ops[eng].append((waits, fn, (sem.h, 1) if track else None))
        for b in w:
            b.last_w = (sem, val)
            b.readers = {}
        for b in r:
            if b.readers.get(sem, 0) < val:
                b.readers[sem] = val

    def dma(self, q, out, in_, sem, r=(), w=(), **kw):
        waits = self._waits(q, r, w)
        sem.count += 16
        val = sem.count
        self.ops[q].append((waits, lambda e, o=out, i=in_: e.dma_start(out=o, in_=i, **kw), (sem.h, 16)))
        for b in w:
            b.last_w = (sem, val)
            b.readers = {}
        for b in r:
            if b.readers.get(sem, 0) < val:
                b.readers[sem] = val

    def wait_all(self, eng, sems):
        waits = []
        for s in sems:
            if s.count > 0 and self.seen[eng].get(s, 0) < s.count:
                self.seen[eng][s] = s.count
                waits.append((s.h, s.count))
        self.ops[eng].append((waits, None, None))

    def barrier(self, dsems):
        allsems = list(self.esem.values()) + list(dsems)
        for e in self.ENG:
            self.wait_all(e, allsems)

    def emit(self):
        nc = self.nc
        block = self.es.enter_context(nc.Block())

        def run(eng_name):
            def f(e):
                for waits, fn, inc in self.ops[eng_name]:
                    for h, v in waits:
                        e.wait_ge(h, v)
                    if fn is not None:
                        ins = fn(e)
                        if inc is not None:
                            ins.then_inc(inc[0], inc[1])
            return f
        block.tensor(run("pe"))
        block.scalar(run("act"))
        block.vector(run("dve"))
        block.gpsimd(run("pool"))
        block.sync(run("sp"))


def build_nc(stage=3, sub=9, nch=NCH2, asub=9):
    nc = bass.Bass("TRN2", target_bir_lowering=False)
    es = ExitStack()
    dram = lambda n, s, dt=F32, kind="ExternalInput": nc.dram_tensor(n, list(s), dt, kind=kind).ap()
    x_d = dram("x", [S_LEN, D])
    win_d = dram("w_in", [D, 3840])
    woa_d = dram("w_o_a", [512, D])
    wob_d = dram("w_o_b", [512, D])
    wout_d = dram("w_out", [D, D])
    wg_d = dram("w_g", [D, DFF])
    wv_d = dram("w_v", [D, DFF])
    wd_d = dram("w_d", [DFF, D])
    lnp_d = dram("lnp", [128, 6, D])
    binfm_d = dram("bin_fm", [128, 30])
    bka_d = dram("bka_dup", [128, 2])
    bv_d = dram("bv_bc", [128, 384])
    g1fm_d = dram("ln1_fm", [128, 2, 8])
    cw_d = dram("cw_fm", [128, NFC, 4])
    sink_d = dram("sinks_bc", [128, 8])
    gqk_d = dram("gqk_fm", [128, 2])
    ident_d = dram("ident", [128, 128], BF16)
    ones_d = dram("ones", [128, 128], BF16)
    rt_d = dram("rt", [128, 128], BF16)
    rope_d = dram("rope_rc", [128, 2, 64])
    idents_d = dram("ident_s", [128, 8, 128], BF16)
    distt_d = dram("dist_t", [128, 3, 128], BF16)
    h1_d = dram("h1_scratch", [S_LEN, D], F32, kind="Internal")
    out_d = dram("out", [S_LEN, D], F32, kind="ExternalOutput")

    S = Sched(nc, es)
    SBYTES = 212800
    sb = es.enter_context(nc.sbuf_tensor("SB", [128, SBYTES // 2], BF16))
    psum = [es.enter_context(nc.psum_tensor("ps%d" % i, [128, 512], F32)) for i in range(8)]
    PQ = [[Buf("ps%d_%d" % (b, q)) for q in range(4)] for b in range(8)]
    for b_ in range(8):
        bk_ = Buf("bank%d" % b_, is_bank=True)
        for q_ in PQ[b_]:
            S.bankof[q_] = bk_

    def pbank(b):
        return psum[b][:, :], PQ[b]

    def phalf(b, s):
        return psum[b][:, s * 256:(s + 1) * 256], PQ[b][2 * s:2 * s + 2]

    class Alloc:
        def __init__(self, base, limit):
            self.off = base
            self.limit = limit

        def get(self, shape, dt):
            esz = 4 if dt == F32 else 2
            n = 1
            for s in shape[1:]:
                n *= s
            nb = (n * esz + 63) // 64 * 64
            assert self.off + nb <= self.limit, ("SBUF overflow", self.off, nb, self.limit)
            ap = sb[:, self.off // 2:(self.off + nb) // 2]
            if dt == F32:
                ap = ap.bitcast(F32)
            ap = ap[:, 0:n]
            if len(shape) == 3:
                ap = ap.rearrange("p (a b) -> p a b", a=shape[1])
            elif len(shape) == 4:
                ap = ap.rearrange("p (a b c) -> p a b c", a=shape[1], b=shape[2])
            self.off += nb
            return ap

    A0 = Alloc(0, SBYTES)
    ident = A0.get([128, 128], BF16)
    ones = A0.get([128, 128], BF16)
    rt = A0.get([128, 128], BF16)
    binfm = A0.get([128, 30], F32)
    binh = A0.get([128, 30], F32)
    bka = A0.get([128, 2], F32)
    g1fm = A0.get([128, 2, 8], F32)
    cwfm = A0.get([128, NFC, 4], F32)
    esink = A0.get([128, 8], F32)
    esink_hi = A0.get([128, 8], BF16)
    esink_lo = A0.get([128, 8], BF16)
    sinkL = A0.get([128, 128], BF16)
    gqk = A0.get([128, 2], F32)
    roperc = A0.get([128, 2, 64], F32)
    neghalf = A0.get([128, 3], F32)
    epsln = neghalf[:, 0:1]
    epsrms = neghalf[:, 1:2]
    nhalf = neghalf[:, 2:3]
    stat = A0.get([128, 48], F32)
    mv = A0.get([128, 4, 2], F32)
    rstd = A0.get([128, 4], F32)
    B_const = Buf("const")
    B_stat = [Buf("stat%d" % j) for j in range(4)]
    B_mv = [Buf("mv%d" % j) for j in range(4)]
    B_rstd = [Buf("rstd%d" % j) for j in range(4)]
    dsems = []

    def newsem(n):
        s = S.new_sem(n)
        dsems.append(s)
        return s
    sem_c = newsem("const")
    B_c2 = Buf("neghalf")
    B_binh = Buf("binh")
    B_esink = Buf("esink")
    B_eshl = Buf("esink_hl")
    S.op("pool", lambda e: e.memset(epsln, LN_EPS), w=[B_c2], track=False)
    S.op("pool", lambda e: e.memset(nhalf, -0.5), w=[B_c2], track=False)
    S.op("pool", lambda e: e.memset(epsrms, RMS_EPS), w=[B_c2])

    def issue_consts():
        for ap, d in ((ident, ident_d), (ones, ones_d), (rt, rt_d), (binfm, binfm_d), (bka, bka_d),
                      (g1fm, g1fm_d), (cwfm, cw_d), (esink, sink_d), (gqk, gqk_d), (roperc, rope_d)):
            S.dma("sp", ap, d, sem_c, w=[B_const])

    def const_ops():
        S.op("dve", lambda e: e.tensor_scalar(out=binh, in0=binfm, scalar1=0.5, scalar2=None, op0=ALU.mult),
             r=[B_const], w=[B_binh])
        S.op("act", lambda e: e.activation(out=esink, in_=esink, func=AF.Exp), r=[B_const], w=[B_esink])
        S.op("dve", lambda e: e.tensor_copy(out=esink_hi, in_=esink), r=[B_esink], w=[B_eshl])
        S.op("dve", lambda e: e.tensor_tensor(out=esink_lo, in0=esink, in1=esink_hi, op=ALU.subtract), r=[B_esink, B_eshl], w=[B_eshl])
        S.op("dve", lambda e: e.memset(sinkL[:, 0:64], 0.0), w=[B_eshl], track=False)
        S.op("dve", lambda e: e.memset(sinkL[:, 64:128], 1.0), w=[B_eshl])
    CONST = [B_const, B_binh, B_esink]
    P12_BASE = A0.off

    def ln_stats(xt, j, B_x):
        for hh in range(2):
            S.op("dve", lambda e, hh=hh: e.bn_stats(out=stat[:, (2 * j + hh) * 6:(2 * j + hh + 1) * 6], in_=xt[:, hh * 512:(hh + 1) * 512]),
                 r=[B_x], w=[B_stat[j]], track=(hh == 1))
        S.op("dve", lambda e: e.bn_aggr(out=mv[:, j, :], in_=stat[:, 12 * j:12 * j + 12]), r=[B_stat[j]], w=[B_mv[j]])
        S.op("dve", lambda e: e.tensor_scalar(out=rstd[:, j:j + 1], in0=mv[:, j, 1:2], scalar1=LN_EPS, scalar2=None,
                                              op0=ALU.add), r=[B_mv[j]], w=[B_rstd[j]])
        S.op("pool", lambda e: e.tensor_tensor(out=rstd[:, j:j + 1], in0=rstd[:, j:j + 1], in1=nhalf, op=ALU.pow),
             r=[B_rstd[j], B_c2], w=[B_rstd[j]])

    A = Alloc(P12_BASE, SBYTES)
    kbT = A.get([128, 2, S_LEN], BF16)
    vb = A.get([128, NT, 256], BF16)
    kaT = A.get([128, 2, S_LEN], BF16)
    va = A.get([128, NT, 2, 128], BF16)
    w_qg = A.get([128, 8, 3072], BF16)
    w_oa = A.get([128, 4, D], BF16)
    w_ob = A.get([128, 4, D], BF16)
    w_out = A.get([128, 8, D], BF16)
    lnp0 = A.get([128, 2, D], F32)
    xh = [A.get([128, 2, D], F32) for _ in range(2)]
    h0T_ = [A.get([128, 8, CH2], BF16) for _ in range(2)]
    rAA = A.get([128, 512], F32)
    rBB = A.get([128, 512], F32)
    rA = [rAA[:, 0:256], rAA[:, 256:512]]
    rB = [rBB[:, 0:256], rBB[:, 256:512]]
    s5ab = A.get([128, 512], F32)
    s5a = s5ab[:, 0:256]
    s5b = s5ab[:, 256:512]
    hb = s5ab.bitcast(BF16)
    aden = s5a[:, 0:128]
    arec = s5a[:, 128:256]
    costab = A.get([128, 256], F32)
    sintab = A.get([128, 256], F32)
    P2ONLY = A.off
    qT = A.get([128, 4, CH2], BF16)
    qbd = A.get([128, 4, 2, 256], BF16)
    PTT = A.get([128, 2048], BF16)
    PTU = [PTT[:, i * 512:(i + 1) * 512] for i in range(4)]
    oTa = A.get([128, 4, CH2], BF16)
    oTb = A.get([128, 4, CH2], BF16)
    identS = A.get([128, 8, 128], BF16)
    distT = A.get([128, 3, 128], BF16)
    mT = A.get([128, 8, CH2], BF16)
    Akv = Alloc(P2ONLY, A.off)
    print('phase12 sbuf used', A.off)
    w_kv = Akv.get([128, 8, 896], BF16)
    bvbc = Akv.get([128, 384], F32)

    B_kbT = [Buf("kbT%d" % c) for c in range(NCH2)]
    B_kaT = [Buf("kaT%d" % c) for c in range(NCH2)]
    B_vb = [Buf("vb%d" % t) for t in range(NT)]
    B_va = [Buf("va%d" % t) for t in range(NT)]
    B_vaones = Buf("vaones")
    B_wqg, B_woa, B_wob, B_wout, B_wkv = Buf("wqg"), Buf("woa"), Buf("wob"), Buf("wout"), Buf("wkv")
    B_lnp0 = Buf("lnp0")
    B_xh = [[Buf("xh%d_%d" % (i, j)) for j in range(2)] for i in range(2)]
    B_h0T_ = [[Buf("h0T%d_%d" % (p_, j)) for j in range(2)] for p_ in range(2)]
    B_qT = [Buf("qT%d" % i) for i in range(4)]
    B_qbd = [Buf("qbd%d" % i) for i in range(4)]
    B_PTU = [Buf("PTU%d" % i) for i in range(4)]
    B_oTa = [Buf("oTa%d" % i) for i in range(4)]
    B_oTb = [Buf("oTb%d" % i) for i in range(4)]
    B_mT = [Buf("mT%d" % i) for i in range(8)]
    B_rA = [Buf("rA0"), Buf("rA1")]
    B_rB = [Buf("rB0"), Buf("rB1")]
    B_s5b = Buf("s5b")
    B_aden, B_arec, B_aotmp = Buf("aden"), Buf("arec"), Buf("aotmp")
    HB = [B_aden, B_arec, B_s5b]
    B_tab = Buf("tab")
    B_tabhi = Buf("tabhi")
    B_biasA = Buf("biasA")
    B_bvbc = Buf("bvbc")

    sem_xh = [newsem("xh%d" % i) for i in range(2)]
    win_v = win_d.rearrange("(k p) n -> p k n", p=128)
    s_wkv = newsem("wkv")
    S.dma("pool", w_kv[:, :, 0:64], win_v[:, :, C_KA:C_KA + 64], s_wkv, w=[B_wkv])
    S.dma("pool", w_kv[:, :, 64:128], win_v[:, :, C_KA:C_KA + 64], s_wkv, w=[B_wkv])
    S.dma("pool", w_kv[:, :, 128:192], win_v[:, :, C_KA + 64:C_KA + 128], s_wkv, w=[B_wkv])
    S.dma("pool", w_kv[:, :, 192:256], win_v[:, :, C_KA + 64:C_KA + 128], s_wkv, w=[B_wkv])
    S.dma("pool", w_kv[:, :, 256:384], win_v[:, :, C_VA:C_VA + 128], s_wkv, w=[B_wkv])
    S.dma("pool", w_kv[:, :, 384:896], win_v[:, :, C_KB:C_KB + 512], s_wkv, w=[B_wkv])
    s_misc = newsem("misc")
    s_bv = newsem("bv")

    def issue_setup():
        load_x(0, 0)
        S.dma("sp", lnp0, lnp_d[:, 0:2, :], s_misc, w=[B_lnp0])
        issue_consts()
        S.dma("sp", bvbc, bv_d, s_bv, w=[B_bvbc])
        S.op("pool", lambda e: e.memset(va[:, :, :, 64:128], 1.0), w=[B_vaones])
        for tab, k in ((costab, 0), (sintab, 1)):
            S.op("pool", lambda e, tab=tab, k=k: e.tensor_copy(
                out=tab[64:128, :].rearrange("p (r c) -> p r c", r=4),
                in_=roperc[64:128, k, :].unsqueeze(1).broadcast_to([64, 4, 64])), r=[B_const], w=[B_tabhi])

    def issue_w2():
        s_w2 = newsem("w2")
        for (dst, c0, n) in ((0, C_QA, 512), (512, C_QB, 512), (1024, C_GA, 1024), (2048, C_GB, 1024)):
            S.dma("pool", w_qg[:, :, dst:dst + n], win_v[:, :, c0:c0 + n], s_w2, w=[B_wqg])
        s_woa, s_wob, s_wout = newsem("woa"), newsem("wob"), newsem("wout")
        S.dma("pool", w_oa, woa_d.rearrange("(k p) n -> p k n", p=128), s_woa, w=[B_woa])
        S.dma("pool", w_ob, wob_d.rearrange("(k p) n -> p k n", p=128), s_wob, w=[B_wob])
        S.dma("pool", w_out, wout_d.rearrange("(k p) n -> p k n", p=128), s_wout, w=[B_wout])

    def load_x(c, slot):
        src = x_d[c * CH2:(c + 1) * CH2, :].rearrange("(j p) d -> p j d", p=128)
        S.dma("sp", xh[slot], src, sem_xh[slot], w=B_xh[slot])

    def ln_in_tile(slot, j):
        xt = xh[slot][:, j, :]
        Bx = B_xh[slot][j]
        ln_stats(xt, j, Bx)
        S.op("dve", lambda e: e.scalar_tensor_tensor(
            out=xt, in0=xt, scalar=mv[:, j, 0:1], in1=lnp0[:, 0, :], op0=ALU.subtract, op1=ALU.mult),
            r=[Bx, B_mv[j], B_lnp0], w=[Bx])
        S.op("dve", lambda e: e.scalar_tensor_tensor(
            out=xt, in0=xt, scalar=rstd[:, j:j + 1], in1=lnp0[:, 1, :], op0=ALU.mult, op1=ALU.add),
            r=[Bx, B_rstd[j], B_lnp0], w=[Bx])

    def transpose_tile(slot, j, par, bank):
        xt = xh[slot][:, j, :]
        Bx = B_xh[slot][j]
        S.op("act", lambda e: e.activation(out=hb, in_=xt, func=AF.Copy), r=[Bx], w=HB)
        pb, pq = pbank(bank)
        pbb = pb.bitcast(BF16)
        for kc in range(8):
            S.op("pe", lambda e, kc=kc: e.transpose(out=pbb[:, kc * 128:(kc + 1) * 128],
                                                    in_=hb[:, kc * 128:(kc + 1) * 128], identity=ident),
                 r=HB + [B_const], w=pq, track=(kc == 7))
        S.op("act", lambda e: e.activation(
            out=h0T_[par][:, :, j * 128:(j + 1) * 128], in_=pbb.rearrange("p (k t) -> p k t", k=8), func=AF.Copy),
            r=pq, w=[B_h0T_[par][j]])

    def front_stages(slot, j, par, bank, offload=False):
        xt = xh[slot][:, j, :]
        Bx = B_xh[slot][j]
        pb, pq = pbank(bank)
        pbb = pb.bitcast(BF16)

        def f0():
            for hh in range(2):
                S.op("dve", lambda e, hh=hh: e.bn_stats(out=stat[:, (2 * j + hh) * 6:(2 * j + hh + 1) * 6],
                                                        in_=xt[:, hh * 512:(hh + 1) * 512]),
                     r=[Bx], w=[B_stat[j]], track=(hh == 1))
            S.op("dve", lambda e: e.bn_aggr(out=mv[:, j, :], in_=stat[:, 12 * j:12 * j + 12]), r=[B_stat[j]], w=[B_mv[j]])
            S.op("dve", lambda e: e.tensor_scalar(out=rstd[:, j:j + 1], in0=mv[:, j, 1:2], scalar1=LN_EPS, scalar2=None,
                                                  op0=ALU.add), r=[B_mv[j]], w=[B_rstd[j]])

        def f1():
            S.op("pool", lambda e: e.tensor_tensor(out=rstd[:, j:j + 1], in0=rstd[:, j:j + 1], in1=nhalf, op=ALU.pow),
                 r=[B_rstd[j], B_c2], w=[B_rstd[j]])

        def f2():
            S.op("dve", lambda e: e.scalar_tensor_tensor(
                out=xt, in0=xt, scalar=mv[:, j, 0:1], in1=lnp0[:, 0, :], op0=ALU.subtract, op1=ALU.mult),
                r=[Bx, B_mv[j], B_lnp0], w=[Bx])
            S.op("dve", lambda e: e.scalar_tensor_tensor(
                out=xt, in0=xt, scalar=rstd[:, j:j + 1], in1=lnp0[:, 1, :], op0=ALU.mult, op1=ALU.add),
                r=[Bx, B_rstd[j], B_lnp0], w=[Bx])

        def f3():
            if offload:
                S.op("dve", lambda e: e.tensor_copy(out=hb, in_=xt), r=[Bx], w=HB)
            else:
                S.op("act", lambda e: e.activation(out=hb, in_=xt, func=AF.Copy), r=[Bx], w=HB)

        def f4():
            for kc in range(8):
                S.op("pe", lambda e, kc=kc: e.transpose(out=pbb[:, kc * 128:(kc + 1) * 128],
                                                        in_=hb[:, kc * 128:(kc + 1) * 128], identity=ident),
                     r=HB + [B_const], w=pq, track=(kc == 7))

        def f5():
            if offload:
                S.op("dve", lambda e: e.tensor_copy(
                    out=h0T_[par][:, :, j * 128:(j + 1) * 128], in_=pbb.rearrange("p (k t) -> p k t", k=8)),
                    r=pq, w=[B_h0T_[par][j]])
            else:
                S.op("act", lambda e: e.activation(
                    out=h0T_[par][:, :, j * 128:(j + 1) * 128], in_=pbb.rearrange("p (k t) -> p k t", k=8), func=AF.Copy),
                    r=pq, w=[B_h0T_[par][j]])
        return [f0, f1, f2, f3, f4, f5]

    def ln_in_and_transpose(c, slot, trb):
        for j in range(2):
            ln_in_tile(slot, j)
            transpose_tile(slot, j, c % 2, trb[j])

    class BG:
        def __init__(self):
            self.q = []

        def push(self, stages, stride=1, offset=0):
            for k_, f_ in enumerate(stages):
                pos = offset + k_ * stride
                while len(self.q) <= pos:
                    self.q.append([])
                self.q[pos].append(f_)

        def push_at(self, pos, f_):
            while len(self.q) <= pos:
                self.q.append([])
            self.q[pos].append(f_)

        def tick(self):
            if self.q:
                for f_ in self.q.pop(0):
                    f_()

        def flush(self):
            while self.q:
                self.tick()
    bg = BG()

    def mm_group(out_ap, pq, pairs, r):
        n = len(pairs)
        for i, (l, rr) in enumerate(pairs):
            S.op("pe", lambda e, l=l, rr=rr, i=i: e.matmul(out_ap, lhsT=l, rhs=rr, start=(i == 0), stop=(i == n - 1)),
                 r=r, w=pq, track=(i == n - 1))

    def update_rope_tab(c):
        for tab, k in ((costab, 0), (sintab, 1)):
            S.op("pool", lambda e, tab=tab, k=k: e.tensor_copy(
                out=tab[0:64, :].rearrange("p (r c) -> p r c", r=4),
                in_=roperc[0:64, k, c * 4:c * 4 + 4].unsqueeze(2).broadcast_to([64, 4, 64])),
                r=[B_const], w=[B_tab])

    def rope_stages(st, u_ps, u_pq, bias_ap, g_ap, dst_ap, B_dst, ss_slot, rq_slot):
        tA, tB, tC = rA[st], rB[st], dst_ap
        BA, BB, BC = B_rA[st], B_rB[st], B_dst
        ss_ap, ss_pq = ss_slot
        rq_ap, rq_pq = rq_slot

        def t1():
            S.op("act", lambda e: e.activation(out=tA, in_=u_ps, func=AF.Identity, bias=bias_ap, scale=1.0),
                 r=u_pq + CONST, w=[BA])
            S.op("act", lambda e: e.activation(out=tC, in_=u_ps, func=AF.Square, bias=bias_ap, scale=1.0),
                 r=u_pq + CONST, w=[BC])

        def t2():
            mm_group(ss_ap, ss_pq, [(ones, tC)], r=[BC, B_const])

        def t3():
            S.op("act", lambda e: e.activation(out=tB, in_=ss_ap, func=AF.Ln, bias=epsrms, scale=1.0 / 128.0),
                 r=ss_pq + [B_c2], w=[BB])
            S.op("act", lambda e: e.activation(out=tB, in_=tB, func=AF.Exp, scale=-0.5), r=[BB], w=[BB])

        def t4():
            S.op("dve", lambda e: e.scalar_tensor_tensor(out=tA, in0=tA, scalar=g_ap, in1=tB, op0=ALU.mult, op1=ALU.mult),
                 r=[BA, BB] + CONST, w=[BA])
            S.op("dve", lambda e: e.tensor_copy(out=tC, in_=tA), r=[BA], w=[BC])

        def t5():
            mm_group(rq_ap, rq_pq, [(rt, tC)], r=[BC, B_const])

        def t6():
            S.op("dve", lambda e: e.tensor_tensor(out=tB, in0=rq_ap, in1=sintab, op=ALU.mult),
                 r=rq_pq + [B_tab, B_tabhi], w=[BB])
            S.op("dve", lambda e: e.tensor_tensor(out=tA, in0=tA, in1=costab, op=ALU.mult),
                 r=[BA, B_tab, B_tabhi], w=[BA])
            S.op("dve", lambda e: e.tensor_tensor(out=dst_ap, in0=tA, in1=tB, op=ALU.add),
                 r=[BA, BB], w=[B_dst])
        return [t1, t2, t3, t4, t5, t6]

    def rms_rope(u_ps, u_pq, bias_ap, g_ap, dst_ap, B_dst, ss_slot, rq_slot, st=0):
        for f_ in rope_stages(st, u_ps, u_pq, bias_ap, g_ap, dst_ap, B_dst, ss_slot, rq_slot):
            f_()

    def push_front_p1(c):
        slot, par = c % 2, c % 2
        s0 = front_stages(slot, 0, par, 6)
        s1 = front_stages(slot, 1, par, 7)
        bg.push(s0, stride=1, offset=0)
        for k_, f_ in enumerate(s1):
            bg.push_at([1, 2, 3, 5, 6, 7][k_], f_)

    def phase1_chunk(c):
        slot, par = c % 2, c % 2
        if c == 0:
            s0_ = front_stages(slot, 0, par, 6)
            s1_ = front_stages(slot, 1, par, 7)
            for f_ in s0_[:3] + s1_[:3]:
                f_()
            const_ops()
            for f_ in s0_[3:] + s1_[3:]:
                f_()
        if c + 1 < NCH2:
            load_x(c + 1, 1 - slot)
            push_front_p1(c + 1)
        h0T = h0T_[par]
        hr = [B_h0T_[par][0], B_h0T_[par][1], B_wkv]
        for g in range(2):
            bk = 2 + 2 * par + g
            ap, pq = phalf(bk, 0)
            mm_group(ap, pq, [(w_kv[:, kc, 384 + g * 128:384 + (g + 1) * 128], h0T[:, kc, :]) for kc in range(8)], r=hr)
            stages = rope_stages(g, ap, pq, binfm[:, 10 + g:11 + g], gqk[:, 1:2], kbT[:, g, c * CH2:(c + 1) * CH2],
                                 B_kbT[c], phalf(bk, 1), phalf(bk, 0))
            if g == 0:
                stages = stages[:5] + [lambda: update_rope_tab(c)] + stages[5:]
                pos = [1, 2, 3, 4, 6, 7, 8]
            else:
                pos = [1, 2, 3, 4, 6, 8]
            for k_, f_ in enumerate(stages):
                bg.push_at(pos[k_], f_)
            bg.tick()
        for g in range(2):
            ap, pq = phalf(g, 0)
            mm_group(ap, pq, [(w_kv[:, kc, g * 128:(g + 1) * 128], h0T[:, kc, :]) for kc in range(8)], r=hr)
            S.op("act", lambda e, ap=ap, g=g: e.activation(out=kaT[:, g, c * CH2:(c + 1) * CH2], in_=ap,
                                                           func=AF.Identity, bias=bka[:, g:g + 1], scale=1.0),
                 r=pq + CONST, w=[B_kaT[c]])
            bg.tick()
        for j in range(2):
            t = 2 * c + j
            ap, pq = pbank(j)
            for kc in range(8):
                S.op("pe", lambda e, kc=kc, j=j, ap=ap: e.matmul(ap[:, 0:128], lhsT=h0T[:, kc, j * 128:(j + 1) * 128],
                                                                 rhs=w_kv[:, kc, 256:384], start=(kc == 0), stop=(kc == 7)),
                     r=hr, w=pq, track=(kc == 7))
            bg.tick()
            for kc in range(8):
                S.op("pe", lambda e, kc=kc, j=j, ap=ap: e.matmul(ap[:, 128:384], lhsT=h0T[:, kc, j * 128:(j + 1) * 128],
                                                                 rhs=w_kv[:, kc, 640:896], start=(kc == 0), stop=(kc == 7)),
                     r=hr, w=pq, track=(kc == 7))
            bg.tick()
            S.op("dve", lambda e, t=t, ap=ap: e.tensor_tensor(
                out=va[:, t, :, 0:64], in0=ap[:, 0:128].rearrange("p (g d) -> p g d", g=2),
                in1=bvbc[:, 0:128].rearrange("p (g d) -> p g d", g=2), op=ALU.add),
                r=pq + [B_bvbc], w=[B_va[t]])
            S.op("dve", lambda e, t=t, ap=ap: e.tensor_tensor(out=vb[:, t, :], in0=ap[:, 128:384], in1=bvbc[:, 128:384],
                                                              op=ALU.add), r=pq + [B_bvbc], w=[B_vb[t]])

    issue_setup()
    for c_ in range(NCH2):
        phase1_chunk(c_)
        if c_ == 1:
            issue_w2()
    bg.flush()
    S.barrier(dsems)
    if stage == 1:
        sdbg = newsem("dbg")
        for nm, ap_, n_ in (("dbg_kbT", kbT, 2 * S_LEN), ("dbg_vb", vb, NT * 256), ("dbg_kaT", kaT, 2 * S_LEN), ("dbg_va", va, NT * 256)):
            d_ = nc.dram_tensor(nm, [128, n_], BF16, kind="ExternalOutput").ap()
            flat = ap_.rearrange("p a b -> p (a b)") if len(ap_.shape) == 3 else ap_.rearrange("p a b c -> p (a b c)")
            S.dma("sp", d_, flat, sdbg)
        S.wait_all("sp", dsems)
        S.emit()
        es.close()
        return nc
    S.op("pool", lambda e: e.memset(qbd, 0.0), w=B_qbd)
    s_bias = newsem("biasA")
    S.dma("sp", identS, idents_d, s_bias, w=[B_biasA])
    S.dma("sp", distT, distt_d, s_bias, w=[B_biasA])

    KV_ALL_B = B_kbT + B_vb
    sem_h1 = [newsem("h1st%d" % i) for i in range(2)]
    SC_B = 1.0 / float(np.sqrt(128.0))
    def s5_s6(c):
        slot = c % 2
        h0T = h0T_[c % 2]
        hr = [B_h0T_[c % 2][0], B_h0T_[c % 2][1], B_wqg]
        tmp0, tmp1 = s5a, s5b
        Bt0, Bt1 = [B_aden, B_arec], [B_s5b]
        for fo in range(8):
            p_ = fo % 2
            pa, pa_q = phalf(2 * p_, 0)
            pb_, pb_q = phalf(2 * p_, 1)
            ga, ga_q = phalf(2 * p_ + 1, 0)
            gb, gb_q = phalf(2 * p_ + 1, 1)
            fs = slice(fo * 128, (fo + 1) * 128)
            mm_group(ga, ga_q, [(w_qg[:, kc, 1024 + fo * 128:1024 + (fo + 1) * 128], h0T[:, kc, :]) for kc in range(8)], r=hr)
            bg.tick()
            mm_group(gb, gb_q, [(w_qg[:, kc, 2048 + fo * 128:2048 + (fo + 1) * 128], h0T[:, kc, :]) for kc in range(8)], r=hr)
            bg.tick()
            mm_group(pa, pa_q, [(w_oa[:, k, fs], oTa[:, k, :]) for k in range(4)], r=B_oTa + [B_woa])
            bg.tick()
            mm_group(pb_, pb_q, [(w_ob[:, k, fs], oTb[:, k, :]) for k in range(4)], r=B_oTb + [B_wob])
            bg.tick()
            S.op("act", lambda e, ga=ga, fo=fo: e.activation(out=tmp0, in_=ga, func=AF.Tanh,
                                                             bias=binh[:, 14 + fo:15 + fo], scale=0.5),
                 r=ga_q + CONST, w=Bt0)
            S.op("act", lambda e, gb=gb, fo=fo: e.activation(out=tmp1, in_=gb, func=AF.Tanh,
                                                             bias=binh[:, 22 + fo:23 + fo], scale=0.5),
                 r=gb_q + CONST, w=Bt1)
            S.op("dve", lambda e, pa=pa: e.scalar_tensor_tensor(out=tmp0, in0=tmp0, scalar=1.0, in1=pa,
                                                                op0=ALU.add, op1=ALU.mult),
                 r=pa_q + Bt0, w=Bt0)
            S.op("dve", lambda e, pb_=pb_: e.scalar_tensor_tensor(out=tmp1, in0=tmp1, scalar=1.0, in1=pb_,
                                                                  op0=ALU.add, op1=ALU.mult),
                 r=pb_q + Bt1, w=Bt1)
            S.op("dve", lambda e, fo=fo: e.tensor_tensor(out=mT[:, fo, :], in0=tmp0, in1=tmp1, op=ALU.add),
                 r=Bt0 + Bt1, w=[B_mT[fo]])
        for j in range(2):
            xt = xh[slot][:, j, :]
            Bx = B_xh[slot][j]
            for nh in range(2):
                yp, yq = pbank((2 * j + nh) % 4)
                for kc in range(8):
                    S.op("pe", lambda e, kc=kc, yp=yp, j=j, nh=nh: e.matmul(
                        yp, lhsT=mT[:, kc, j * 128:(j + 1) * 128], rhs=w_out[:, kc, nh * 512:(nh + 1) * 512],
                        start=(kc == 0), stop=(kc == 7)), r=[B_mT[kc], B_wout], w=yq, track=(kc == 7))
                bg.tick()
                S.op("dve", lambda e, xt=xt, yp=yp, nh=nh: e.scalar_tensor_tensor(
                    out=xt[:, nh * 512:(nh + 1) * 512], in0=xt[:, nh * 512:(nh + 1) * 512], scalar=ALPHA, in1=yp,
                    op0=ALU.mult, op1=ALU.add), r=yq + [Bx], w=[Bx])
            ln_stats(xt, j, Bx)
            S.op("dve", lambda e, xt=xt, j=j: e.tensor_scalar(out=xt, in0=xt, scalar1=mv[:, j, 0:1],
                                                              scalar2=rstd[:, j:j + 1], op0=ALU.subtract, op1=ALU.mult),
                 r=[Bx, B_mv[j], B_rstd[j]], w=[Bx])
        dst = h1_d[c * CH2:(c + 1) * CH2, :].rearrange("(j p) d -> p j d", p=128)
        S.dma("sp", dst, xh[slot], sem_xh[slot], r=B_xh[slot])

    def push_rope(c):
        par = c % 2
        h0T = h0T_[par]
        hr = [B_h0T_[par][0], B_h0T_[par][1], B_wqg]
        bg.push_at(0, lambda: update_rope_tab(c))
        for h in range(4):
            u_ps, u_pq = phalf(4 + h, 0)

            def t0(h=h, u_ps=u_ps, u_pq=u_pq):
                mm_group(u_ps, u_pq, [(w_qg[:, kc, 512 + h * 128:512 + (h + 1) * 128], h0T[:, kc, :]) for kc in range(8)], r=hr)
            stages = [t0] + rope_stages(h % 2, u_ps, u_pq, binfm[:, 6 + h:7 + h], gqk[:, 0:1], qT[:, h, :],
                                        B_qT[h], phalf(4 + h, 1), phalf(4 + h, 0))
            base = 1 + (h % 2) + 17 * (h // 2)
            for k_, f_ in enumerate(stages):
                bg.push_at(base + [0, 2, 4, 6, 8, 12, 14][k_], f_)

    def push_front(c):
        slot, par = c % 2, c % 2
        s0 = front_stages(slot, 0, par, 6, offload=True)
        s1 = front_stages(slot, 1, par, 7, offload=True)
        bg.push(s0, stride=4, offset=2)
        for k_, f_ in enumerate(s1):
            bg.push_at([4, 8, 12, 20, 24, 28][k_], f_)

    def phase2_chunk(c, first, last):
        slot = c % 2
        par = c % 2
        h0T = h0T_[par]
        if first:
            push_front(c)
            bg.flush()
            push_rope(c)
            bg.flush()
        if not last:
            load_x(c + 1, 1 - slot)
        hr = [B_h0T_[par][0], B_h0T_[par][1], B_wqg]
        for fo in range(4):
            ap, pq = phalf(fo, 0)
            mm_group(ap, pq, [(w_qg[:, kc, fo * 128:(fo + 1) * 128], h0T[:, kc, :]) for kc in range(8)], r=hr)
            bg.tick()
            S.op("act", lambda e, ap=ap, fo=fo: e.activation(
                out=qbd[0:64, fo, :, 0:128], in_=ap[0:64, :].rearrange("p (j t) -> p j t", j=2), func=AF.Identity,
                bias=binfm[0:64, fo:fo + 1], scale=1.0), r=pq + CONST, w=[B_qbd[fo]])
            S.op("act", lambda e, ap=ap, fo=fo: e.activation(
                out=qbd[64:128, fo, :, 128:256], in_=ap[64:128, :].rearrange("p (j t) -> p j t", j=2), func=AF.Identity,
                bias=binfm[64:128, fo:fo + 1], scale=1.0), r=pq + CONST, w=[B_qbd[fo]])
        bg.flush()
        units = []
        for j in range(2):
            i = 2 * c + j
            rels = [r_ for r_ in range(3) if 0 <= i + r_ - 1 < NT]
            for cc in range(4):
                units.append((j, i, rels, cc))
        NU = len(units)

        def a_banks(u):
            bx, bxq = pbank(4 + 2 * (u % 2))
            by, byq = pbank(5 + 2 * (u % 2))
            return bx, bxq, by, byq

        def a_qk(u):
            j, i, rels, cc = units[u]
            g = cc // 2
            bx, bxq, by, byq = a_banks(u)
            kdeps = [B_kaT[(i + r_ - 1) // 2] for r_ in rels]
            for r_ in rels:
                kb = i + r_ - 1
                if r_ < 2:
                    o_ap, oq = bx[:, r_ * 256:(r_ + 1) * 256], bxq
                else:
                    o_ap, oq = by[:, 0:256], byq
                S.op("pe", lambda e, o_ap=o_ap, kb=kb: e.matmul(
                    o_ap, lhsT=kaT[:, g, kb * 128:(kb + 1) * 128], rhs=qbd[:, cc, j, :],
                    start=True, stop=False), r=kdeps + [B_qbd[cc]], w=oq, track=False)
                for hh in range(2):
                    S.op("pe", lambda e, o_ap=o_ap, r_=r_, hh=hh: e.matmul(
                        o_ap[:, hh * 128:(hh + 1) * 128], lhsT=identS[:, 2 * cc + hh, :], rhs=distT[:, r_, :],
                        start=False, stop=(hh == 1)), r=[B_biasA], w=oq, track=(hh == 1))

        def a_exp(u):
            j, i, rels, cc = units[u]
            bx, bxq, by, byq = a_banks(u)
            pt = PTT[:, (u % 2) * 1024:(u % 2) * 1024 + 768]
            bp = [B_PTU[2 * (u % 2)], B_PTU[2 * (u % 2) + 1]]
            rx = [r_ for r_ in rels if r_ < 2]
            lo, hi = rx[0] * 256, (rx[-1] + 1) * 256
            S.op("act", lambda e: e.activation(out=pt[:, lo:hi], in_=bx[:, lo:hi], func=AF.Exp, scale=0.125),
                 r=bxq, w=bp)
            if 2 in rels:
                S.op("act", lambda e: e.activation(out=pt[:, 512:768], in_=by[:, 0:256], func=AF.Exp, scale=0.125),
                     r=byq, w=bp)

        def a_pv(u):
            j, i, rels, cc = units[u]
            g = cc // 2
            pt = PTT[:, (u % 2) * 1024:(u % 2) * 1024 + 768]
            bp = [B_PTU[2 * (u % 2)], B_PTU[2 * (u % 2) + 1]]
            ob, opq_all = pbank(u % 4)
            o_ap = ob[:, 0:256]
            opq = opq_all[0:2]
            vdeps = [B_va[i + r_ - 1] for r_ in rels] + [B_vaones]
            nr = len(rels)
            for n_, r_ in enumerate(rels):
                kb = i + r_ - 1
                S.op("pe", lambda e, kb=kb, r_=r_, n_=n_: e.matmul(
                    o_ap, lhsT=va[:, kb, g, :], rhs=pt[:, r_ * 256:(r_ + 1) * 256],
                    start=(n_ == 0), stop=False),
                    r=vdeps + bp, w=opq, track=False)
            for hh in range(2):
                for part, es in enumerate((esink_hi, esink_lo)):
                    lastm = (hh == 1 and part == 1)
                    S.op("pe", lambda e, hh=hh, es=es, lastm=lastm: e.matmul(
                        o_ap[:, hh * 128:(hh + 1) * 128], lhsT=sinkL[0:1, :],
                        rhs=es[0:1, 2 * cc + hh:2 * cc + hh + 1].broadcast_to([1, 128]),
                        start=False, stop=lastm), r=[B_eshl], w=opq, track=lastm)

        def a_norm(u):
            j, i, rels, cc = units[u]
            ob, opq_all = pbank(u % 4)
            o_ap = ob[:, 0:256]
            opq = opq_all[0:2]
            rec0 = s5b[0:64, :]
            tln = s5a[64:128, :]
            if u % 2 == 0:
                S.op("act", lambda e: e.activation(out=tln, in_=o_ap[64:128, :], func=AF.Ln), r=opq, w=[B_aden, B_arec])
                S.op("act", lambda e: e.activation(out=tln, in_=tln, func=AF.Exp, scale=-1.0), r=[B_aden, B_arec], w=[B_aden, B_arec])
                S.op("dve", lambda e: e.tensor_copy(out=rec0, in_=tln), r=[B_aden, B_arec], w=[B_s5b])
            else:
                S.op("dve", lambda e: e.reciprocal(out=rec0, in_=o_ap[64:128, :]), r=opq, w=[B_s5b])
            S.op("dve", lambda e: e.scalar_tensor_tensor(
                out=oTa[0:64, cc, j * 128:(j + 1) * 128], in0=o_ap[0:64, 0:128], scalar=0.5, in1=rec0[:, 0:128],
                op0=ALU.mult, op1=ALU.mult), r=opq + [B_s5b], w=[B_oTa[cc]])
            S.op("dve", lambda e: e.scalar_tensor_tensor(
                out=oTa[64:128, cc, j * 128:(j + 1) * 128], in0=o_ap[0:64, 128:256], scalar=0.5, in1=rec0[:, 128:256],
                op0=ALU.mult, op1=ALU.mult), r=opq + [B_s5b], w=[B_oTa[cc]])

        a_qk(0)
        for u in range(NU):
            a_exp(u)
            if u + 1 < NU:
                a_qk(u + 1)
            a_pv(u)
            if u >= 1:
                a_norm(u - 1)
        a_norm(NU - 1)
        if not last:
            push_front(c + 1)
        for g in range(2):
            o_ap, o_pq = pbank(4 + 2 * g)
            s_ap, s_pq = pbank(5 + 2 * g)
            qrhs = qT[:, 2 * g:2 * g + 2, :]

            def qk(kt, g=g, qrhs=qrhs):
                ap, pq = pbank(kt % 4)
                S.op("pe", lambda e: e.matmul(ap, lhsT=kbT[:, g, kt * 128:(kt + 1) * 128], rhs=qrhs,
                                              start=True, stop=True),
                     r=[B_kbT[kt // 2], B_qT[2 * g], B_qT[2 * g + 1]], w=pq)

            def expo(kt):
                ap, pq = pbank(kt % 4)
                S.op("act", lambda e: e.activation(out=PTU[kt % 4], in_=ap, func=AF.Exp, scale=SC_B),
                     r=pq, w=[B_PTU[kt % 4]])

            def pv(kt, g=g, o_ap=o_ap, s_ap=s_ap, o_pq=o_pq, s_pq=s_pq):
                S.op("pe", lambda e: e.matmul(o_ap, lhsT=vb[:, kt, g * 128:(g + 1) * 128], rhs=PTU[kt % 4],
                                              start=(kt == 0), stop=(kt == NT - 1)),
                     r=[B_vb[kt], B_PTU[kt % 4]], w=o_pq, track=False)
                S.op("pe", lambda e: e.matmul(s_ap, lhsT=ones, rhs=PTU[kt % 4],
                                              start=(kt == 0), stop=(kt == NT - 1)),
                     r=[B_PTU[kt % 4], B_const], w=s_pq)
            qk(0)
            qk(1)
            for kt in range(NT):
                expo(kt)
                if kt + 2 < NT:
                    qk(kt + 2)
                pv(kt)
                bg.tick()
            rec = rAA
            S.op("act", lambda e, s_ap=s_ap: e.activation(out=rec, in_=s_ap, func=AF.Ln), r=s_pq, w=B_rA)
            S.op("act", lambda e: e.activation(out=rec, in_=rec, func=AF.Exp, scale=-1.0), r=B_rA, w=B_rA)
            S.op("dve", lambda e, o_ap=o_ap, g=g: e.scalar_tensor_tensor(
                out=oTb[:, 2 * g:2 * g + 2, :], in0=o_ap.rearrange("p (h t) -> p h t", h=2), scalar=0.5,
                in1=rec.rearrange("p (h t) -> p h t", h=2), op0=ALU.mult, op1=ALU.mult),
                r=o_pq + B_rA, w=[B_oTb[2 * g], B_oTb[2 * g + 1]])
        bg.flush()
        if not last:
            push_rope(c + 1)
        s5_s6(c)

    load_x(0, 0)
    for c_ in range(nch):
        phase2_chunk(c_, c_ == 0, c_ == nch - 1)
    bg.flush()

    S.barrier(dsems)
    if stage == 2:
        sdbg = newsem("dbg")
        d_ = nc.dram_tensor("dbg_h1", [S_LEN, D], F32, kind="ExternalOutput").ap()
        if sub >= 5:
            for i_ in range(nch):
                S.dma("sp", d_[i_ * 256:(i_ + 1) * 256, :], h1_d[i_ * 256:(i_ + 1) * 256, :], sdbg)
        for nm, ap_, n_ in (("dbg_qT", qT, 4 * CH2), ("dbg_oTa", oTa, 4 * CH2), ("dbg_oTb", oTb, 4 * CH2)):
            dd_ = nc.dram_tensor(nm, [128, n_], BF16, kind="ExternalOutput").ap()
            S.dma("sp", dd_, ap_.rearrange("p a b -> p (a b)"), sdbg)
        S.wait_all("sp", dsems)
        S.emit()
        es.close()
        return nc
    A3 = Alloc(P12_BASE, SBYTES)
    w_g = A3.get([128, 8, DFF], BF16)
    w_v = A3.get([128, 8, DFF], BF16)
    w_dn = A3.get([128, NFC, D], BF16)
    lnp3 = A3.get([128, 4, D], F32)
    hh = A3.get([128, 4, D], F32)
    h1T = A3.get([128, 8, CH3 + 2], BF16)
    halo = A3.get([128, 2, 8], F32)
    gext_ = [A3.get([128, CH3 + 2], F32) for _ in range(2)]
    t3a_ = [A3.get([128, CH3], F32) for _ in range(2)]
    hb3_off = A3.off
    t3b_ = [A3.get([128, CH3], F32)] * 2
    hb3 = Alloc(hb3_off, A3.off).get([128, D], BF16)
    actT = A3.get([128, NFC, CH3], BF16)
    B_wg, B_wv, B_wd, B_lnp3 = Buf("wg"), Buf("wv"), Buf("wd"), Buf("lnp3")
    B_hh = [Buf("hh%d" % j) for j in range(4)]
    B_h1Th, B_halo = Buf("h1Th"), Buf("halo")
    B_gext_ = [Buf("gext0"), Buf("gext1")]
    B_gexth_ = [Buf("gexth0"), Buf("gexth1")]
    B_t3a_ = [Buf("t3a0"), Buf("t3a1")]
    B_t3b_ = [Buf("t3b0")] * 2
    B_hb3 = B_t3b_[0]
    B_actT = [Buf("actT%d" % i) for i in range(NFC)]
    wg_v = wg_d.rearrange("(k p) n -> p k n", p=128)
    wv_v = wv_d.rearrange("(k p) n -> p k n", p=128)
    WPC = [(0, 256), (256, 768), (768, 1408), (1408, 2176), (2176, 2816)]
    B_wgp = [Buf("wg%d" % i) for i in range(len(WPC))]
    B_wvp = [Buf("wv%d" % i) for i in range(len(WPC))]
    for i_, (c0, c1) in enumerate(WPC):
        S.dma("pool", w_g[:, :, c0:c1], wg_v[:, :, c0:c1], newsem("wg%d" % i_), w=[B_wgp[i_]])
        S.dma("pool", w_v[:, :, c0:c1], wv_v[:, :, c0:c1], newsem("wv%d" % i_), w=[B_wvp[i_]])

    def wpiece(fc):
        for i_, (c0, c1) in enumerate(WPC):
            if c0 <= fc * 128 < c1:
                return i_
    wd_v = wd_d.rearrange("(k p) n -> p k n", p=128)
    s_w3d = newsem("w3d")
    for k0 in range(0, NFC, 11):
        S.dma("pool", w_dn[:, k0:k0 + 11, :], wd_v[:, k0:k0 + 11, :], s_w3d, w=[B_wd])
    s_m3 = newsem("m3")
    S.dma("sp", lnp3, lnp_d[:, 2:6, :], s_m3, w=[B_lnp3])
    sem_halo = newsem("halo")
    sem_hhj = [newsem("hh%d" % j) for j in range(4)]
    B_h1Tj = [Buf("h1T%d" % j) for j in range(4)]

    def prep_stages(c, j):
        t0 = c * CH3
        ht = hh[:, j, :]
        pb, pq = pbank(6 + j % 2)
        pbb = pb.bitcast(BF16)

        def p0():
            S.dma("sp", ht, h1_d[t0 + j * 128:t0 + (j + 1) * 128, :], sem_hhj[j], w=[B_hh[j]])
            if j == 0:
                for side, tk in ((0, t0 - 1), (1, t0 + CH3)):
                    if 0 <= tk < S_LEN:
                        S.dma("sp", halo[:, side, :], h1_d[tk:tk + 1, :].rearrange("o (k p) -> p (o k)", p=128),
                              sem_halo, w=[B_halo], allow_slow_non_contiguous=True)

        def p1():
            S.op("dve", lambda e: e.tensor_tensor(out=ht, in0=ht, in1=lnp3[:, 0, :], op=ALU.mult),
                 r=[B_hh[j], B_lnp3], w=[B_hh[j]])

        def p2():
            S.op("dve", lambda e: e.tensor_tensor(out=ht, in0=ht, in1=lnp3[:, 1, :], op=ALU.add),
                 r=[B_hh[j], B_lnp3], w=[B_hh[j]])

        def p3():
            S.op("act", lambda e: e.activation(out=hb3, in_=ht, func=AF.Copy), r=[B_hh[j]], w=[B_hb3])

        def p4():
            for kc in range(8):
                S.op("pe", lambda e, kc=kc: e.transpose(out=pbb[:, kc * 128:(kc + 1) * 128],
                                                        in_=hb3[:, kc * 128:(kc + 1) * 128], identity=ident),
                     r=[B_hb3, B_const], w=pq, track=(kc == 7))

        def p5():
            S.op("act", lambda e: e.activation(
                out=h1T[:, :, 1 + j * 128:1 + (j + 1) * 128], in_=pbb.rearrange("p (k t) -> p k t", k=8), func=AF.Copy),
                r=pq, w=[B_h1Tj[j]])
        return [p0, p1, p2, p3, p4, p5]

    def halo_finish(c):
        t0 = c * CH3
        for side, tk, col in ((0, t0 - 1, 0), (1, t0 + CH3, CH3 + 1)):
            if 0 <= tk < S_LEN:
                S.op("dve", lambda e, side=side: e.tensor_tensor(out=halo[:, side, :], in0=halo[:, side, :],
                                                                 in1=g1fm[:, 0, :], op=ALU.mult),
                     r=[B_halo, B_const], w=[B_halo])
                S.op("dve", lambda e, side=side, col=col: e.tensor_tensor(out=h1T[:, :, col], in0=halo[:, side, :],
                                                                          in1=g1fm[:, 1, :], op=ALU.add),
                     r=[B_halo, B_const], w=[B_h1Th])
            else:
                S.op("dve", lambda e, col=col: e.memset(h1T[:, :, col], 0.0), w=[B_h1Th])

    def phase3_chunk(c):
        t0 = c * CH3
        if c == 0:
            for j in range(4):
                for f_ in prep_stages(c, j):
                    f_()
        bg.flush()
        halo_finish(c)
        hr3 = B_h1Tj + [B_h1Th]
        for fc in range(NFC):
            fs = slice(fc * 128, (fc + 1) * 128)
            gext, t3a, t3b = gext_[fc % 2], t3a_[fc % 2], t3b_[fc % 2]
            B_gext, B_gexth, B_t3a, B_t3b = B_gext_[fc % 2], B_gexth_[fc % 2], B_t3a_[fc % 2], B_t3b_[fc % 2]
            gp, gq_ = pbank(fc % 2)
            vp, vq_ = pbank(2 + fc % 2)
            ghp_full, ghq_full = pbank(4 + fc % 2)
            ghp = ghp_full[:, 0:2]
            for kc in range(8):
                S.op("pe", lambda e, kc=kc, gp=gp, fs=fs: e.matmul(gp, lhsT=w_g[:, kc, fs], rhs=h1T[:, kc, 1:CH3 + 1],
                                                                   start=(kc == 0), stop=(kc == 7)),
                     r=hr3 + [B_wgp[wpiece(fc)]], w=gq_, track=(kc == 7))
                S.op("pe", lambda e, kc=kc, ghp=ghp, fs=fs: e.matmul(ghp, lhsT=w_g[:, kc, fs],
                                                                     rhs=h1T[:, kc, 0:CH3 + 2:CH3 + 1],
                                                                     start=(kc == 0), stop=(kc == 7)),
                     r=hr3 + [B_wgp[wpiece(fc)]], w=[ghq_full[0]], track=(kc == 7))
            mm_group(vp, vq_, [(w_v[:, kc, fs], h1T[:, kc, 1:CH3 + 1]) for kc in range(8)], r=hr3 + [B_wvp[wpiece(fc)]])
            S.op("act", lambda e, gp=gp, gext=gext: e.activation(out=gext[:, 1:CH3 + 1], in_=gp, func=AF.Copy), r=gq_, w=[B_gext])
            S.op("dve", lambda e, ghp=ghp, gext=gext: e.tensor_copy(out=gext[:, 0:CH3 + 2:CH3 + 1], in_=ghp),
                 r=[ghq_full[0]], w=[B_gexth])
            ge = [B_gext, B_gexth]
            S.op("dve", lambda e, fc=fc, gext=gext, t3a=t3a, t3b=t3b: e.tensor_scalar(out=t3a, in0=gext[:, 0:CH3], scalar1=cwfm[:, fc, 0:1],
                                                         scalar2=None, op0=ALU.mult), r=ge + [B_const], w=[B_t3a])
            S.op("dve", lambda e, fc=fc, gext=gext, t3a=t3a, t3b=t3b: e.scalar_tensor_tensor(out=t3a, in0=gext[:, 1:CH3 + 1], scalar=cwfm[:, fc, 1:2],
                                                                in1=t3a, op0=ALU.mult, op1=ALU.add),
                 r=ge + [B_const, B_t3a], w=[B_t3a])
            S.op("dve", lambda e, fc=fc, gext=gext, t3a=t3a, t3b=t3b: e.scalar_tensor_tensor(out=t3a, in0=gext[:, 2:CH3 + 2], scalar=cwfm[:, fc, 2:3],
                                                                in1=t3a, op0=ALU.mult, op1=ALU.add),
                 r=ge + [B_const, B_t3a], w=[B_t3a])
            S.op("act", lambda e, fc=fc, gext=gext, t3a=t3a, t3b=t3b: e.activation(out=t3b, in_=t3a, func=AF.Gelu, bias=cwfm[:, fc, 3:4], scale=1.0),
                 r=[B_t3a, B_const], w=[B_t3b])
            S.op("dve", lambda e, fc=fc, vp=vp, t3b=t3b: e.tensor_tensor(out=actT[:, fc, :], in0=t3b, in1=vp, op=ALU.mult),
                 r=vq_ + [B_t3b], w=[B_actT[fc]])
        for j in range(4):
            ht = hh[:, j, :]
            for nh in range(2):
                yp, yq = pbank((2 * j + nh) % 4)
                for fc in range(NFC):
                    S.op("pe", lambda e, fc=fc, yp=yp, j=j, nh=nh: e.matmul(
                        yp, lhsT=actT[:, fc, j * 128:(j + 1) * 128], rhs=w_dn[:, fc, nh * 512:(nh + 1) * 512],
                        start=(fc == 0), stop=(fc == NFC - 1)), r=[B_actT[fc], B_wd], w=yq, track=(fc == NFC - 1))
                bg.tick()
                S.op("dve", lambda e, ht=ht, yp=yp, nh=nh: e.scalar_tensor_tensor(
                    out=ht[:, nh * 512:(nh + 1) * 512], in0=ht[:, nh * 512:(nh + 1) * 512], scalar=ALPHA, in1=yp,
                    op0=ALU.mult, op1=ALU.add), r=yq + [B_hh[j]], w=[B_hh[j]])
            ln_stats(ht, j, B_hh[j])
            S.op("dve", lambda e, ht=ht, j=j: e.scalar_tensor_tensor(
                out=ht, in0=ht, scalar=mv[:, j, 0:1], in1=lnp3[:, 2, :], op0=ALU.subtract, op1=ALU.mult),
                r=[B_hh[j], B_mv[j], B_lnp3], w=[B_hh[j]])
            S.op("dve", lambda e, ht=ht, j=j: e.scalar_tensor_tensor(
                out=ht, in0=ht, scalar=rstd[:, j:j + 1], in1=lnp3[:, 3, :], op0=ALU.mult, op1=ALU.add),
                r=[B_hh[j], B_rstd[j], B_lnp3], w=[B_hh[j]])
            S.dma("sp", out_d[t0 + j * 128:t0 + (j + 1) * 128, :], ht, sem_hhj[j], r=[B_hh[j]])
            if c + 1 < NCH3:
                bg.push(prep_stages(c + 1, j), stride=1, offset=1)
    for c_ in range(NCH3):
        phase3_chunk(c_)
    bg.flush()
    S.wait_all("sp", dsems)
    S.emit()
    es.close()
    return nc


def _host_consts():
    bf = ml_dtypes.bfloat16
    ident = np.eye(128, dtype=np.float32).astype(bf)
    ones = np.ones((128, 128), dtype=np.float32).astype(bf)
    R = np.zeros((128, 128), dtype=np.float32)
    for d in range(128):
        if d % 64 < 32:
            R[d, d + 32] = -1.0
        else:
            R[d, d - 32] = 1.0
    rt = np.ascontiguousarray(R.T).astype(bf)
    freqs = (np.float32(10000.0) ** (-(np.arange(32, dtype=np.float32) / np.float32(32)))).astype(np.float32)
    pos = np.arange(64, dtype=np.float32)
    ang = (pos[None, :] * freqs[:, None]).astype(np.float32)
    ang128 = np.tile(ang, (4, 1))
    rope = np.stack([np.cos(ang128), np.sin(ang128)], axis=1).astype(np.float32)
    a = np.arange(128)[:, None]
    b = np.arange(128)[None, :]
    dist_t = np.zeros((128, 3, 128), dtype=np.float32)
    for rel in range(3):
        dist = np.abs(b - a - (rel - 1) * 128)
        dist_t[:, rel, :] = np.where(dist <= 128, -dist, -30000.0)
    dist_t = dist_t.astype(bf)
    ident_s = np.zeros((128, 8, 128), dtype=np.float32)
    for h in range(8):
        ident_s[:, h, :] = np.eye(128, dtype=np.float32) * (8.0 * 2.0 ** (-(h + 1)))
    ident_s = ident_s.astype(bf)
    bias = (ident_s, dist_t)
    return ident, ones, rt, rope, bias


_NC_CACHE = {}


def make_shared(x, ln_in_g, ln_in_b, w_in, b_in, a_sinks, b_q_norm, b_k_norm, w_o_a, w_o_b, w_out,
           ln1_g, ln1_b, w_ffn_gate, w_ffn_val, ffn_conv_w, ffn_conv_b, w_ffn_down, ln2_g, ln2_b):
    f = lambda t: np.ascontiguousarray(np.asarray(t, dtype=np.float32))
    x = f(x)
    b_in0 = f(b_in)[0]
    ident, ones, rt, rope, bias = _host_consts()
    lnp = np.stack([f(ln_in_g), f(ln_in_b), f(ln1_g)[0], f(ln1_b)[0], f(ln2_g)[0], f(ln2_b)[0]], 0)
    lnp = np.ascontiguousarray(np.broadcast_to(lnp[None], (128, 6, D)))
    bin_fm = np.ascontiguousarray(b_in0.reshape(30, 128).T)
    bka = np.stack([np.tile(b_in0[C_KA:C_KA + 64], 2), np.tile(b_in0[C_KA + 64:C_KA + 128], 2)], 1)
    bv = np.concatenate([b_in0[C_VA:C_VA + 128], b_in0[C_VB:C_VB + 256]])
    bv_bc = np.ascontiguousarray(np.broadcast_to(bv[None], (128, 384)))
    ln1_fm = np.stack([f(ln1_g)[0].reshape(8, 128).T, f(ln1_b)[0].reshape(8, 128).T], 1)
    cw = np.concatenate([f(ffn_conv_w)[0], f(ffn_conv_b)], 0)
    cw_fm = np.ascontiguousarray(cw.reshape(4, NFC, 128).transpose(2, 1, 0))
    sinks_bc = np.ascontiguousarray(np.broadcast_to(f(a_sinks)[0][None], (128, 8)))
    gqk = np.stack([f(b_q_norm)[0], f(b_k_norm)[0]], 1)
    shared = {
        "w_in": f(w_in)[0], "w_o_a": f(w_o_a)[0], "w_o_b": f(w_o_b)[0], "w_out": f(w_out)[0],
        "w_g": f(w_ffn_gate)[0], "w_v": f(w_ffn_val)[0], "w_d": f(w_ffn_down)[0],
        "lnp": lnp, "bin_fm": bin_fm, "bka_dup": np.ascontiguousarray(bka), "bv_bc": bv_bc,
        "ln1_fm": np.ascontiguousarray(ln1_fm), "cw_fm": cw_fm, "sinks_bc": sinks_bc,
        "gqk_fm": np.ascontiguousarray(gqk), "ident": ident, "ones": ones, "rt": rt,
        "rope_rc": np.ascontiguousarray(rope), "ident_s": bias[0], "dist_t": bias[1],
    }
    return x, shared


def kernel(x, ln_in_g, ln_in_b, w_in, b_in, a_sinks, b_q_norm, b_k_norm, w_o_a, w_o_b, w_out,
           ln1_g, ln1_b, w_ffn_gate, w_ffn_val, ffn_conv_w, ffn_conv_b, w_ffn_down, ln2_g, ln2_b):
    x, shared = make_shared(x, ln_in_g, ln_in_b, w_in, b_in, a_sinks, b_q_norm, b_k_norm, w_o_a, w_o_b, w_out,
                            ln1_g, ln1_b, w_ffn_gate, w_ffn_val, ffn_conv_w, ffn_conv_b, w_ffn_down, ln2_g, ln2_b)
    if "nc" not in _NC_CACHE:
        _NC_CACHE["nc"] = build_nc()
    nc = _NC_CACHE["nc"]
    in_maps = []
    for b in range(8):
        m = dict(shared)
        m["x"] = x[b]
        in_maps.append(m)
    res = run_bass_kernel_spmd(nc, in_maps, core_ids=list(range(8)))
    return np.stack([np.asarray(r["out"], dtype=np.float32) for r in res.results], 0)
```

```python
import numpy as np
import ml_dtypes
from contextlib import ExitStack
import concourse.bass as bass
import concourse.mybir as mybir
from concourse.bass_utils import run_bass_kernel_spmd

F32 = mybir.dt.float32
BF16 = mybir.dt.bfloat16
AF = mybir.ActivationFunctionType
ALU = mybir.AluOpType

S_LEN = 4096
D = 1024
NT = 32
DFF = 2816
NFC = 22
ALPHA = float(2.0 ** 0.25)
LN_EPS = 1e-5
RMS_EPS = 1e-6
CH2 = 256
NCH2 = S_LEN // CH2
CH3 = 512
NCH3 = S_LEN // CH3
C_QA, C_KA, C_VA, C_QB, C_KB, C_VB, C_GA, C_GB = 0, 512, 640, 768, 1280, 1536, 1792, 2816


class Buf:
    __slots__ = ("name", "last_w", "readers", "is_bank")

    def __init__(self, name, is_bank=False):
        self.name = name
        self.last_w = None
        self.readers = {}
        self.is_bank = is_bank


class Sem:
    def __init__(self, h):
        self.h = h
        self.count = 0


class Sched:
    ENG = ("pe", "act", "dve", "pool", "sp")

    def __init__(self, nc, es):
        self.nc = nc
        self.es = es
        self.ops = {e: [] for e in self.ENG}
        self.esem = {e: Sem(es.enter_context(nc.semaphore("sem_" + e))) for e in self.ENG if e != "sp"}
        self.seen = {e: {} for e in self.ENG}
        self.nsem = 0
        self.bankof = {}

    def new_sem(self, name):
        self.nsem += 1
        return Sem(self.es.enter_context(self.nc.semaphore("d_%s_%d" % (name, self.nsem))))

    def _waits(self, eng, r, w):
        need = {}

        def add(dep, war, bank=False):
            if dep is None:
                return
            sem, val = dep
            if war and sem is self.esem.get(eng) and (eng == "pe" or bank):
                return
            if need.get(sem, 0) < val:
                need[sem] = val
        for b in r:
            add(b.last_w, False)
        for b in w:
            add(b.last_w, True, b.is_bank)
            for sem, val in b.readers.items():
                add((sem, val), True, b.is_bank)
        out = []
        seen = self.seen[eng]
        for sem, val in need.items():
            if seen.get(sem, 0) < val:
                seen[sem] = val
                out.append((sem.h, val))
        return out

    def op(self, eng, fn, r=(), w=(), track=True):
        banks = []
        for b in list(r) + list(w):
            bk = self.bankof.get(b)
            if bk is not None and bk not in banks:
                banks.append(bk)
        if banks:
            w = list(w) + banks
        waits = self._waits(eng, r, w)
        sem = self.esem[eng]
        val = sem.count + 1
        if eng != "pe":
            track = True
        if track:
            sem.count = val
        self.ops[eng].append((waits, fn, (sem.h, 1) if track else None))
        for b in w:
            b.last_w = (sem, val)
            b.readers = {}
        for b in r:
            if b.readers.get(sem, 0) < val:
                b.readers[sem] = val

    def dma(self, q, out, in_, sem, r=(), w=(), **kw):
        waits = self._waits(q, r, w)
        sem.count += 16
        val = sem.count
        self.ops[q].append((waits, lambda e, o=out, i=in_: e.dma_start(out=o, in_=i, **kw), (sem.h, 16)))
        for b in w:
            b.last_w = (sem, val)
            b.readers = {}
        for b in r:
            if b.readers.get(sem, 0) < val:
                b.readers[sem] = val

    def wait_all(self, eng, sems):
        waits = []
        for s in sems:
            if s.count > 0 and self.seen[eng].get(s, 0) < s.count:
                self.seen[eng][s] = s.count
                waits.append((s.h, s.count))
        self.ops[eng].append((waits, None, None))

    def barrier(self, dsems):
        allsems = list(self.esem.values()) + list(dsems)
        for e in self.ENG:
            self.wait_all(e, allsems)

    def emit(self):
        nc = self.nc
        block = self.es.enter_context(nc.Block())

        def run(eng_name):
            def f(e):
                for waits, fn, inc in self.ops[eng_name]:
                    for h, v in waits:
                        e.wait_ge(h, v)
                    if fn is not None:
                        ins = fn(e)
                        if inc is not None:
                            ins.then_inc(inc[0], inc[1])
            return f
        block.tensor(run("pe"))
        block.scalar(run("act"))
        block.vector(run("dve"))
        block.gpsimd(run("pool"))
        block.sync(run("sp"))


def build_nc(stage=3, sub=9, nch=NCH2, asub=9):
    nc = bass.Bass("TRN2", target_bir_lowering=False)
    es = ExitStack()
    dram = lambda n, s, dt=F32, kind="ExternalInput": nc.dram_tensor(n, list(s), dt, kind=kind).ap()
    x_d = dram("x", [S_LEN, D])
    win_d = dram("w_in", [D, 3840])
    woa_d = dram("w_o_a", [512, D])
    wob_d = dram("w_o_b", [512, D])
    wout_d = dram("w_out", [D, D])
    wg_d = dram("w_g", [D, DFF])
    wv_d = dram("w_v", [D, DFF])
    wd_d = dram("w_d", [DFF, D])
    lnp_d = dram("lnp", [128, 6, D])
    binfm_d = dram("bin_fm", [128, 30])
    bka_d = dram("bka_dup", [128, 2])
    bv_d = dram("bv_bc", [128, 384])
    g1fm_d = dram("ln1_fm", [128, 2, 8])
    cw_d = dram("cw_fm", [128, NFC, 4])
    sink_d = dram("sinks_bc", [128, 8])
    gqk_d = dram("gqk_fm", [128, 2])
    ident_d = dram("ident", [128, 128], BF16)
    ones_d = dram("ones", [128, 128], BF16)
    rt_d = dram("rt", [128, 128], BF16)
    rope_d = dram("rope_rc", [128, 2, 64])
    idents_d = dram("ident_s", [128, 8, 128], BF16)
    distt_d = dram("dist_t", [128, 3, 128], BF16)
    h1_d = dram("h1_scratch", [S_LEN, D], F32, kind="Internal")
    out_d = dram("out", [S_LEN, D], F32, kind="ExternalOutput")

    S = Sched(nc, es)
    SBYTES = 212800
    sb = es.enter_context(nc.sbuf_tensor("SB", [128, SBYTES // 2], BF16))
    psum = [es.enter_context(nc.psum_tensor("ps%d" % i, [128, 512], F32)) for i in range(8)]
    PQ = [[Buf("ps%d_%d" % (b, q)) for q in range(4)] for b in range(8)]
    for b_ in range(8):
        bk_ = Buf("bank%d" % b_, is_bank=True)
        for q_ in PQ[b_]:
            S.bankof[q_] = bk_

    def pbank(b):
        return psum[b][:, :], PQ[b]

    def phalf(b, s):
        return psum[b][:, s * 256:(s + 1) * 256], PQ[b][2 * s:2 * s + 2]

    class Alloc:
        def __init__(self, base, limit):
            self.off = base
            self.limit = limit

        def get(self, shape, dt):
            esz = 4 if dt == F32 else 2
            n = 1
            for s in shape[1:]:
                n *= s
            nb = (n * esz + 63) // 64 * 64
            assert self.off + nb <= self.limit, ("SBUF overflow", self.off, nb, self.limit)
            ap = sb[:, self.off // 2:(self.off + nb) // 2]
            if dt == F32:
                ap = ap.bitcast(F32)
            ap = ap[:, 0:n]
            if len(shape) == 3:
                ap = ap.rearrange("p (a b) -> p a b", a=shape[1])
            elif len(shape) == 4:
                ap = ap.rearrange("p (a b c) -> p a b c", a=shape[1], b=shape[2])
            self.off += nb
            return ap

    A0 = Alloc(0, SBYTES)
    ident = A0.get([128, 128], BF16)
    ones = A0.get([128, 128], BF16)
    rt = A0.get([128, 128], BF16)
    binfm = A0.get([128, 30], F32)
    binh = A0.get([128, 30], F32)
    bka = A0.get([128, 2], F32)
    g1fm = A0.get([128, 2, 8], F32)
    cwfm = A0.get([128, NFC, 4], F32)
    esink = A0.get([128, 8], F32)
    esink_hi = A0.get([128, 8], BF16)
    esink_lo = A0.get([128, 8], BF16)
    sinkL = A0.get([128, 128], BF16)
    gqk = A0.get([128, 2], F32)
    roperc = A0.get([128, 2, 64], F32)
    neghalf = A0.get([128, 3], F32)
    epsln = neghalf[:, 0:1]
    epsrms = neghalf[:, 1:2]
    nhalf = neghalf[:, 2:3]
    stat = A0.get([128, 48], F32)
    mv = A0.get([128, 4, 2], F32)
    rstd = A0.get([128, 4], F32)
    B_const = Buf("const")
    B_stat = [Buf("stat%d" % j) for j in range(4)]
    B_mv = [Buf("mv%d" % j) for j in range(4)]
    B_rstd = [Buf("rstd%d" % j) for j in range(4)]
    dsems = []

    def newsem(n):
        s = S.new_sem(n)
        dsems.append(s)
        return s
    sem_c = newsem("const")
    B_c2 = Buf("neghalf")
    B_binh = Buf("binh")
    B_esink = Buf("esink")
    B_eshl = Buf("esink_hl")
    S.op("pool", lambda e: e.memset(epsln, LN_EPS), w=[B_c2], track=False)
    S.op("pool", lambda e: e.memset(nhalf, -0.5), w=[B_c2], track=False)
    S.op("pool", lambda e: e.memset(epsrms, RMS_EPS), w=[B_c2])

    def issue_consts():
        for ap, d in ((ident, ident_d), (ones, ones_d), (rt, rt_d), (binfm, binfm_d), (bka, bka_d),
                      (g1fm, g1fm_d), (cwfm, cw_d), (esink, sink_d), (gqk, gqk_d), (roperc, rope_d)):
            S.dma("sp", ap, d, sem_c, w=[B_const])

    def const_ops():
        S.op("dve", lambda e: e.tensor_scalar(out=binh, in0=binfm, scalar1=0.5, scalar2=None, op0=ALU.mult),
             r=[B_const], w=[B_binh])
        S.op("act", lambda e: e.activation(out=esink, in_=esink, func=AF.Exp), r=[B_const], w=[B_esink])
        S.op("dve", lambda e: e.tensor_copy(out=esink_hi, in_=esink), r=[B_esink], w=[B_eshl])
        S.op("dve", lambda e: e.tensor_tensor(out=esink_lo, in0=esink, in1=esink_hi, op=ALU.subtract), r=[B_esink, B_eshl], w=[B_eshl])
        S.op("dve", lambda e: e.memset(sinkL[:, 0:64], 0.0), w=[B_eshl], track=False)
        S.op("dve", lambda e: e.memset(sinkL[:, 64:128], 1.0), w=[B_eshl])
    CONST = [B_const, B_binh, B_esink]
    P12_BASE = A0.off

    def ln_stats(xt, j, B_x):
        for hh in range(2):
            S.op("dve", lambda e, hh=hh: e.bn_stats(out=stat[:, (2 * j + hh) * 6:(2 * j + hh + 1) * 6], in_=xt[:, hh * 512:(hh + 1) * 512]),
                 r=[B_x], w=[B_stat[j]], track=(hh == 1))
        S.op("dve", lambda e: e.bn_aggr(out=mv[:, j, :], in_=stat[:, 12 * j:12 * j + 12]), r=[B_stat[j]], w=[B_mv[j]])
        S.op("dve", lambda e: e.tensor_scalar(out=rstd[:, j:j + 1], in0=mv[:, j, 1:2], scalar1=LN_EPS, scalar2=None,
                                              op0=ALU.add), r=[B_mv[j]], w=[B_rstd[j]])
        S.op("pool", lambda e: e.tensor_tensor(out=rstd[:, j:j + 1], in0=rstd[:, j:j + 1], in1=nhalf, op=ALU.pow),
             r=[B_rstd[j], B_c2], w=[B_rstd[j]])

    A = Alloc(P12_BASE, SBYTES)
    kbT = A.get([128, 2, S_LEN], BF16)
    vb = A.get([128, NT, 256], BF16)
    kaT = A.get([128, 2, S_LEN], BF16)
    va = A.get([128, NT, 2, 128], BF16)
    w_qg = A.get([128, 8, 3072], BF16)
    w_oa = A.get([128, 4, D], BF16)
    w_ob = A.get([128, 4, D], BF16)
    w_out = A.get([128, 8, D], BF16)
    lnp0 = A.get([128, 2, D], F32)
    xh = [A.get([128, 2, D], F32) for _ in range(2)]
    h0T_ = [A.get([128, 8, CH2], BF16) for _ in range(2)]
    rAA = A.get([128, 512], F32)
    rBB = A.get([128, 512], F32)
    rA = [rAA[:, 0:256], rAA[:, 256:512]]
    rB = [rBB[:, 0:256], rBB[:, 256:512]]
    s5ab = A.get([128, 512], F32)
    s5a = s5ab[:, 0:256]
    s5b = s5ab[:, 256:512]
    hb = s5ab.bitcast(BF16)
    aden = s5a[:, 0:128]
    arec = s5a[:, 128:256]
    costab = A.get([128, 256], F32)
    sintab = A.get([128, 256], F32)
    P2ONLY = A.off
    qT = A.get([128, 4, CH2], BF16)
    qbd = A.get([128, 4, 2, 256], BF16)
    PTT = A.get([128, 2048], BF16)
    PTU = [PTT[:, i * 512:(i + 1) * 512] for i in range(4)]
    oTa = A.get([128, 4, CH2], BF16)
    oTb = A.get([128, 4, CH2], BF16)
    identS = A.get([128, 8, 128], BF16)
    distT = A.get([128, 3, 128], BF16)
    mT = A.get([128, 8, CH2], BF16)
    Akv = Alloc(P2ONLY, A.off)
    print('phase12 sbuf used', A.off)
    w_kv = Akv.get([128, 8, 896], BF16)
    bvbc = Akv.get([128, 384], F32)

    B_kbT = [Buf("kbT%d" % c) for c in range(NCH2)]
    B_kaT = [Buf("kaT%d" % c) for c in range(NCH2)]
    B_vb = [Buf("vb%d" % t) for t in range(NT)]
    B_va = [Buf("va%d" % t) for t in range(NT)]
    B_vaones = Buf("vaones")
    B_wqg, B_woa, B_wob, B_wout, B_wkv = Buf("wqg"), Buf("woa"), Buf("wob"), Buf("wout"), Buf("wkv")
    B_lnp0 = Buf("lnp0")
    B_xh = [[Buf("xh%d_%d" % (i, j)) for j in range(2)] for i in range(2)]
    B_h0T_ = [[Buf("h0T%d_%d" % (p_, j)) for j in range(2)] for p_ in range(2)]
    B_qT = [Buf("qT%d" % i) for i in range(4)]
    B_qbd = [Buf("qbd%d" % i) for i in range(4)]
    B_PTU = [Buf("PTU%d" % i) for i in range(4)]
    B_oTa = [Buf("oTa%d" % i) for i in range(4)]
    B_oTb = [Buf("oTb%d" % i) for i in range(4)]
    B_mT = [Buf("mT%d" % i) for i in range(8)]
    B_rA = [Buf("rA0"), Buf("rA1")]
    B_rB = [Buf("rB0"), Buf("rB1")]
    B_s5b = Buf("s5b")
    B_aden, B_arec, B_aotmp = Buf("aden"), Buf("arec"), Buf("aotmp")
    HB = [B_aden, B_arec, B_s5b]
    B_tab = Buf("tab")
    B_tabhi = Buf("tabhi")
    B_biasA = Buf("biasA")
    B_bvbc = Buf("bvbc")

    sem_xh = [newsem("xh%d" % i) for i in range(2)]
    win_v = win_d.rearrange("(k p) n -> p k n", p=128)
    s_wkv = newsem("wkv")
    S.dma("pool", w_kv[:, :, 0:64], win_v[:, :, C_KA:C_KA + 64], s_wkv, w=[B_wkv])
    S.dma("pool", w_kv[:, :, 64:128], win_v[:, :, C_KA:C_KA + 64], s_wkv, w=[B_wkv])
    S.dma("pool", w_kv[:, :, 128:192], win_v[:, :, C_KA + 64:C_KA + 128], s_wkv, w=[B_wkv])
    S.dma("pool", w_kv[:, :, 192:256], win_v[:, :, C_KA + 64:C_KA + 128], s_wkv, w=[B_wkv])
    S.dma("pool", w_kv[:, :, 256:384], win_v[:, :, C_VA:C_VA + 128], s_wkv, w=[B_wkv])
    S.dma("pool", w_kv[:, :, 384:896], win_v[:, :, C_KB:C_KB + 512], s_wkv, w=[B_wkv])
    s_misc = newsem("misc")
    s_bv = newsem("bv")

    def issue_setup():
        load_x(0, 0)
        S.dma("sp", lnp0, lnp_d[:, 0:2, :], s_misc, w=[B_lnp0])
        issue_consts()
        S.dma("sp", bvbc, bv_d, s_bv, w=[B_bvbc])
        S.op("pool", lambda e: e.memset(va[:, :, :, 64:128], 1.0), w=[B_vaones])
        for tab, k in ((costab, 0), (sintab, 1)):
            S.op("pool", lambda e, tab=tab, k=k: e.tensor_copy(
                out=tab[64:128, :].rearrange("p (r c) -> p r c", r=4),
                in_=roperc[64:128, k, :].unsqueeze(1).broadcast_to([64, 4, 64])), r=[B_const], w=[B_tabhi])

    def issue_w2():
        s_w2 = newsem("w2")
        for (dst, c0, n) in ((0, C_QA, 512), (512, C_QB, 512), (1024, C_GA, 1024), (2048, C_GB, 1024)):
            S.dma("pool", w_qg[:, :, dst:dst + n], win_v[:, :, c0:c0 + n], s_w2, w=[B_wqg])
        s_woa, s_wob, s_wout = newsem("woa"), newsem("wob"), newsem("wout")
        S.dma("pool", w_oa, woa_d.rearrange("(k p) n -> p k n", p=128), s_woa, w=[B_woa])
        S.dma("pool", w_ob, wob_d.rearrange("(k p) n -> p k n", p=128), s_wob, w=[B_wob])
        S.dma("pool", w_out, wout_d.rearrange("(k p) n -> p k n", p=128), s_wout, w=[B_wout])

    def load_x(c, slot):
        src = x_d[c * CH2:(c + 1) * CH2, :].rearrange("(j p) d -> p j d", p=128)
        S.dma("sp", xh[slot], src, sem_xh[slot], w=B_xh[slot])

    def ln_in_tile(slot, j):
        xt = xh[slot][:, j, :]
        Bx = B_xh[slot][j]
        ln_stats(xt, j, Bx)
        S.op("dve", lambda e: e.scalar_tensor_tensor(
            out=xt, in0=xt, scalar=mv[:, j, 0:1], in1=lnp0[:, 0, :], op0=ALU.subtract, op1=ALU.mult),
            r=[Bx, B_mv[j], B_lnp0], w=[Bx])
        S.op("dve", lambda e: e.scalar_tensor_tensor(
            out=xt, in0=xt, scalar=rstd[:, j:j + 1], in1=lnp0[:, 1, :], op0=ALU.mult, op1=ALU.add),
            r=[Bx, B_rstd[j], B_lnp0], w=[Bx])

    def transpose_tile(slot, j, par, bank):
        xt = xh[slot][:, j, :]
        Bx = B_xh[slot][j]
        S.op("act", lambda e: e.activation(out=hb, in_=xt, func=AF.Copy), r=[Bx], w=HB)
        pb, pq = pbank(bank)
        pbb = pb.bitcast(BF16)
        for kc in range(8):
            S.op("pe", lambda e, kc=kc: e.transpose(out=pbb[:, kc * 128:(kc + 1) * 128],
                                                    in_=hb[:, kc * 128:(kc + 1) * 128], identity=ident),
                 r=HB + [B_const], w=pq, track=(kc == 7))
        S.op("act", lambda e: e.activation(
            out=h0T_[par][:, :, j * 128:(j + 1) * 128], in_=pbb.rearrange("p (k t) -> p k t", k=8), func=AF.Copy),
            r=pq, w=[B_h0T_[par][j]])

    def front_stages(slot, j, par, bank, offload=False):
        xt = xh[slot][:, j, :]
        Bx = B_xh[slot][j]
        pb, pq = pbank(bank)
        pbb = pb.bitcast(BF16)

        def f0():
            for hh in range(2):
                S.op("dve", lambda e, hh=hh: e.bn_stats(out=stat[:, (2 * j + hh) * 6:(2 * j + hh + 1) * 6],
                                                        in_=xt[:, hh * 512:(hh + 1) * 512]),
                     r=[Bx], w=[B_stat[j]], track=(hh == 1))
            S.op("dve", lambda e: e.bn_aggr(out=mv[:, j, :], in_=stat[:, 12 * j:12 * j + 12]), r=[B_stat[j]], w=[B_mv[j]])
            S.op("dve", lambda e: e.tensor_scalar(out=rstd[:, j:j + 1], in0=mv[:, j, 1:2], scalar1=LN_EPS, scalar2=None,
                                                  op0=ALU.add), r=[B_mv[j]], w=[B_rstd[j]])

        def f1():
            S.op("pool", lambda e: e.tensor_tensor(out=rstd[:, j:j + 1], in0=rstd[:, j:j + 1], in1=nhalf, op=ALU.pow),
                 r=[B_rstd[j], B_c2], w=[B_rstd[j]])

        def f2():
            S.op("dve", lambda e: e.scalar_tensor_tensor(
                out=xt, in0=xt, scalar=mv[:, j, 0:1], in1=lnp0[:, 0, :], op0=ALU.subtract, op1=ALU.mult),
                r=[Bx, B_mv[j], B_lnp0], w=[Bx])
            S.op("dve", lambda e: e.scalar_tensor_tensor(
                out=xt, in0=xt, scalar=rstd[:, j:j + 1], in1=lnp0[:, 1, :], op0=ALU.mult, op1=ALU.add),
                r=[Bx, B_rstd[j], B_lnp0], w=[Bx])

        def f3():
            if offload:
                S.op("dve", lambda e: e.tensor_copy(out=hb, in_=xt), r=[Bx], w=HB)
            else:
                S.op("act", lambda e: e.activation(out=hb, in_=xt, func=AF.Copy), r=[Bx], w=HB)

        def f4():
            for kc in range(8):
                S.op("pe", lambda e, kc=kc: e.transpose(out=pbb[:, kc * 128:(kc + 1) * 128],
                                                        in_=hb[:, kc * 128:(kc + 1) * 128], identity=ident),
                     r=HB + [B_const], w=pq, track=(kc == 7))

        def f5():
            if offload:
                S.op("dve", lambda e: e.tensor_copy(
                    out=h0T_[par][:, :, j * 128:(j + 1) * 128], in_=pbb.rearrange("p (k t) -> p k t", k=8)),
                    r=pq, w=[B_h0T_[par][j]])
            else:
                S.op("act", lambda e: e.activation(
                    out=h0T_[par][:, :, j * 128:(j + 1) * 128], in_=pbb.rearrange("p (k t) -> p k t", k=8), func=AF.Copy),
                    r=pq, w=[B_h0T_[par][j]])
        return [f0, f1, f2, f3, f4, f5]

    def ln_in_and_transpose(c, slot, trb):
        for j in range(2):
            ln_in_tile(slot, j)
            transpose_tile(slot, j, c % 2, trb[j])

    class BG:
        def __init__(self):
            self.q = []

        def push(self, stages, stride=1, offset=0):
            for k_, f_ in enumerate(stages):
                pos = offset + k_ * stride
                while len(self.q) <= pos:
                    self.q.append([])
                self.q[pos].append(f_)

        def push_at(self, pos, f_):
            while len(self.q) <= pos:
                self.q.append([])
            self.q[pos].append(f_)

        def tick(self):
            if self.q:
                for f_ in self.q.pop(0):
                    f_()

        def flush(self):
            while self.q:
                self.tick()
    bg = BG()

    def mm_group(out_ap, pq, pairs, r):
        n = len(pairs)
        for i, (l, rr) in enumerate(pairs):
            S.op("pe", lambda e, l=l, rr=rr, i=i: e.matmul(out_ap, lhsT=l, rhs=rr, start=(i == 0), stop=(i == n - 1)),
                 r=r, w=pq, track=(i == n - 1))

    def update_rope_tab(c):
        for tab, k in ((costab, 0), (sintab, 1)):
            S.op("pool", lambda e, tab=tab, k=k: e.tensor_copy(
                out=tab[0:64, :].rearrange("p (r c) -> p r c", r=4),
                in_=roperc[0:64, k, c * 4:c * 4 + 4].unsqueeze(2).broadcast_to([64, 4, 64])),
                r=[B_const], w=[B_tab])

    def rope_stages(st, u_ps, u_pq, bias_ap, g_ap, dst_ap, B_dst, ss_slot, rq_slot):
        tA, tB, tC = rA[st], rB[st], dst_ap
        BA, BB, BC = B_rA[st], B_rB[st], B_dst
        ss_ap, ss_pq = ss_slot
        rq_ap, rq_pq = rq_slot

        def t1():
            S.op("act", lambda e: e.activation(out=tA, in_=u_ps, func=AF.Identity, bias=bias_ap, scale=1.0),
                 r=u_pq + CONST, w=[BA])
            S.op("act", lambda e: e.activation(out=tC, in_=u_ps, func=AF.Square, bias=bias_ap, scale=1.0),
                 r=u_pq + CONST, w=[BC])

        def t2():
            mm_group(ss_ap, ss_pq, [(ones, tC)], r=[BC, B_const])

        def t3():
            S.op("act", lambda e: e.activation(out=tB, in_=ss_ap, func=AF.Ln, bias=epsrms, scale=1.0 / 128.0),
                 r=ss_pq + [B_c2], w=[BB])
            S.op("act", lambda e: e.activation(out=tB, in_=tB, func=AF.Exp, scale=-0.5), r=[BB], w=[BB])

        def t4():
            S.op("dve", lambda e: e.scalar_tensor_tensor(out=tA, in0=tA, scalar=g_ap, in1=tB, op0=ALU.mult, op1=ALU.mult),
                 r=[BA, BB] + CONST, w=[BA])
            S.op("dve", lambda e: e.tensor_copy(out=tC, in_=tA), r=[BA], w=[BC])

        def t5():
            mm_group(rq_ap, rq_pq, [(rt, tC)], r=[BC, B_const])

        def t6():
            S.op("dve", lambda e: e.tensor_tensor(out=tB, in0=rq_ap, in1=sintab, op=ALU.mult),
                 r=rq_pq + [B_tab, B_tabhi], w=[BB])
            S.op("dve", lambda e: e.tensor_tensor(out=tA, in0=tA, in1=costab, op=ALU.mult),
                 r=[BA, B_tab, B_tabhi], w=[BA])
            S.op("dve", lambda e: e.tensor_tensor(out=dst_ap, in0=tA, in1=tB, op=ALU.add),
                 r=[BA, BB], w=[B_dst])
        return [t1, t2, t3, t4, t5, t6]

    def rms_rope(u_ps, u_pq, bias_ap, g_ap, dst_ap, B_dst, ss_slot, rq_slot, st=0):
        for f_ in rope_stages(st, u_ps, u_pq, bias_ap, g_ap, dst_ap, B_dst, ss_slot, rq_slot):
            f_()

    def push_front_p1(c):
        slot, par = c % 2, c % 2
        s0 = front_stages(slot, 0, par, 6)
        s1 = front_stages(slot, 1, par, 7)
        bg.push(s0, stride=1, offset=0)
        for k_, f_ in enumerate(s1):
            bg.push_at([1, 2, 3, 5, 6, 7][k_], f_)

    def phase1_chunk(c):
        slot, par = c % 2, c % 2
        if c == 0:
            s0_ = front_stages(slot, 0, par, 6)
            s1_ = front_stages(slot, 1, par, 7)
            for f_ in s0_[:3] + s1_[:3]:
                f_()
            const_ops()
            for f_ in s0_[3:] + s1_[3:]:
                f_()
        if c + 1 < NCH2:
            load_x(c + 1, 1 - slot)
            push_front_p1(c + 1)
        h0T = h0T_[par]
        hr = [B_h0T_[par][0], B_h0T_[par][1], B_wkv]
        for g in range(2):
            bk = 2 + 2 * par + g
            ap, pq = phalf(bk, 0)
            mm_group(ap, pq, [(w_kv[:, kc, 384 + g * 128:384 + (g + 1) * 128], h0T[:, kc, :]) for kc in range(8)], r=hr)
            stages = rope_stages(g, ap, pq, binfm[:, 10 + g:11 + g], gqk[:, 1:2], kbT[:, g, c * CH2:(c + 1) * CH2],
                                 B_kbT[c], phalf(bk, 1), phalf(bk, 0))
            if g == 0:
                stages = stages[:5] + [lambda: update_rope_tab(c)] + stages[5:]
                pos = [1, 2, 3, 4, 6, 7, 8]
            else:
                pos = [1, 2, 3, 4, 6, 8]
            for k_, f_ in enumerate(stages):
                bg.push_at(pos[k_], f_)
            bg.tick()
        for g in range(2):
            ap, pq = phalf(g, 0)
            mm_group(ap, pq, [(w_kv[:, kc, g * 128:(g + 1) * 128], h0T[:, kc, :]) for kc in range(8)], r=hr)
            S.op("act", lambda e, ap=ap, g=g: e.activation(out=kaT[:, g, c * CH2:(c + 1) * CH2], in_=ap,
                                                           func=AF.Identity, bias=bka[:, g:g + 1], scale=1.0),
                 r=pq + CONST, w=[B_kaT[c]])
            bg.tick()
        for j in range(2):
            t = 2 * c + j
            ap, pq = pbank(j)
            for kc in range(8):
                S.op("pe", lambda e, kc=kc, j=j, ap=ap: e.matmul(ap[:, 0:128], lhsT=h0T[:, kc, j * 128:(j + 1) * 128],
                                                                 rhs=w_kv[:, kc, 256:384], start=(kc == 0), stop=(kc == 7)),
                     r=hr, w=pq, track=(kc == 7))
            bg.tick()
            for kc in range(8):
                S.op("pe", lambda e, kc=kc, j=j, ap=ap: e.matmul(ap[:, 128:384], lhsT=h0T[:, kc, j * 128:(j + 1) * 128],
                                                                 rhs=w_kv[:, kc, 640:896], start=(kc == 0), stop=(kc == 7)),
                     r=hr, w=pq, track=(kc == 7))
            bg.tick()
            S.op("dve", lambda e, t=t, ap=ap: e.tensor_tensor(
                out=va[:, t, :, 0:64], in0=ap[:, 0:128].rearrange("p (g d) -> p g d", g=2),
                in1=bvbc[:, 0:128].rearrange("p (g d) -> p g d", g=2), op=ALU.add),
                r=pq + [B_bvbc], w=[B_va[t]])
            S.op("dve", lambda e, t=t, ap=ap: e.tensor_tensor(out=vb[:, t, :], in0=ap[:, 128:384], in1=bvbc[:, 128:384],
                                                              op=ALU.add), r=pq + [B_bvbc], w=[B_vb[t]])

    issue_setup()
    for c_ in range(NCH2):
        phase1_chunk(c_)
        if c_ == 1:
            issue_w2()
    bg.flush()
    S.barrier(dsems)
    if stage == 1:
        sdbg = newsem("dbg")
        for nm, ap_, n_ in (("dbg_kbT", kbT, 2 * S_LEN), ("dbg_vb", vb, NT * 256), ("dbg_kaT", kaT, 2 * S_LEN), ("dbg_va", va, NT * 256)):
            d_ = nc.dram_tensor(nm, [128, n_], BF16, kind="ExternalOutput").ap()
            flat = ap_.rearrange("p a b -> p (a b)") if len(ap_.shape) == 3 else ap_.rearrange("p a b c -> p (a b c)")
            S.dma("sp", d_, flat, sdbg)
        S.wait_all("sp", dsems)
        S.emit()
        es.close()
        return nc
    S.op("pool", lambda e: e.memset(qbd, 0.0), w=B_qbd)
    s_bias = newsem("biasA")
    S.dma("sp", identS, idents_d, s_bias, w=[B_biasA])
    S.dma("sp", distT, distt_d, s_bias, w=[B_biasA])

    KV_ALL_B = B_kbT + B_vb
    sem_h1 = [newsem("h1st%d" % i) for i in range(2)]
    SC_B = 1.0 / float(np.sqrt(128.0))
    def s5_s6(c):
        slot = c % 2
        h0T = h0T_[c % 2]
        hr = [B_h0T_[c % 2][0], B_h0T_[c % 2][1], B_wqg]
        tmp0, tmp1 = s5a, s5b
        Bt0, Bt1 = [B_aden, B_arec], [B_s5b]
        for fo in range(8):
            p_ = fo % 2
            pa, pa_q = phalf(2 * p_, 0)
            pb_, pb_q = phalf(2 * p_, 1)
            ga, ga_q = phalf(2 * p_ + 1, 0)
            gb, gb_q = phalf(2 * p_ + 1, 1)
            fs = slice(fo * 128, (fo + 1) * 128)
            mm_group(ga, ga_q, [(w_qg[:, kc, 1024 + fo * 128:1024 + (fo + 1) * 128], h0T[:, kc, :]) for kc in range(8)], r=hr)
            bg.tick()
            mm_group(gb, gb_q, [(w_qg[:, kc, 2048 + fo * 128:2048 + (fo + 1) * 128], h0T[:, kc, :]) for kc in range(8)], r=hr)
            bg.tick()
            mm_group(pa, pa_q, [(w_oa[:, k, fs], oTa[:, k, :]) for k in range(4)], r=B_oTa + [B_woa])
            bg.tick()
            mm_group(pb_, pb_q, [(w_ob[:, k, fs], oTb[:, k, :]) for k in range(4)], r=B_oTb + [B_wob])
            bg.tick()
            S.op("act", lambda e, ga=ga, fo=fo: e.activation(out=tmp0, in_=ga, func=AF.Tanh,
                                                             bias=binh[:, 14 + fo:15 + fo], scale=0.5),
                 r=ga_q + CONST, w=Bt0)
            S.op("act", lambda e, gb=gb, fo=fo: e.activation(out=tmp1, in_=gb, func=AF.Tanh,
                                                             bias=binh[:, 22 + fo:23 + fo], scale=0.5),
                 r=gb_q + CONST, w=Bt1)
            S.op("dve", lambda e, pa=pa: e.scalar_tensor_tensor(out=tmp0, in0=tmp0, scalar=1.0, in1=pa,
                                                                op0=ALU.add, op1=ALU.mult),
                 r=pa_q + Bt0, w=Bt0)
            S.op("dve", lambda e, pb_=pb_: e.scalar_tensor_tensor(out=tmp1, in0=tmp1, scalar=1.0, in1=pb_,
                                                                  op0=ALU.add, op1=ALU.mult),
                 r=pb_q + Bt1, w=Bt1)
            S.op("dve", lambda e, fo=fo: e.tensor_tensor(out=mT[:, fo, :], in0=tmp0, in1=tmp1, op=ALU.add),
                 r=Bt0 + Bt1, w=[B_mT[fo]])
        for j in range(2):
            xt = xh[slot][:, j, :]
            Bx = B_xh[slot][j]
            for nh in range(2):
                yp, yq = pbank((2 * j + nh) % 4)
                for kc in range(8):
                    S.op("pe", lambda e, kc=kc, yp=yp, j=j, nh=nh: e.matmul(
                        yp, lhsT=mT[:, kc, j * 128:(j + 1) * 128], rhs=w_out[:, kc, nh * 512:(nh + 1) * 512],
                        start=(kc == 0), stop=(kc == 7)), r=[B_mT[kc], B_wout], w=yq, track=(kc == 7))
                bg.tick()
                S.op("dve", lambda e, xt=xt, yp=yp, nh=nh: e.scalar_tensor_tensor(
                    out=xt[:, nh * 512:(nh + 1) * 512], in0=xt[:, nh * 512:(nh + 1) * 512], scalar=ALPHA, in1=yp,
                    op0=ALU.mult, op1=ALU.add), r=yq + [Bx], w=[Bx])
            ln_stats(xt, j, Bx)
            S.op("dve", lambda e, xt=xt, j=j: e.tensor_scalar(out=xt, in0=xt, scalar1=mv[:, j, 0:1],
                                                              scalar2=rstd[:, j:j + 1], op0=ALU.subtract, op1=ALU.mult),
                 r=[Bx, B_mv[j], B_rstd[j]], w=[Bx])
        dst = h1_d[c * CH2:(c + 1) * CH2, :].rearrange("(j p) d -> p j d", p=128)
        S.dma("sp", dst, xh[slot], sem_xh[slot], r=B_xh[slot])

    def push_rope(c):
        par = c % 2
        h0T = h0T_[par]
        hr = [B_h0T_[par][0], B_h0T_[par][1], B_wqg]
        bg.push_at(0, lambda: update_rope_tab(c))
        for h in range(4):
            u_ps, u_pq = phalf(4 + h, 0)

            def t0(h=h, u_ps=u_ps, u_pq=u_pq):
                mm_group(u_ps, u_pq, [(w_qg[:, kc, 512 + h * 128:512 + (h + 1) * 128], h0T[:, kc, :]) for kc in range(8)], r=hr)
            stages = [t0] + rope_stages(h % 2, u_ps, u_pq, binfm[:, 6 + h:7 + h], gqk[:, 0:1], qT[:, h, :],
                                        B_qT[h], phalf(4 + h, 1), phalf(4 + h, 0))
            base = 1 + (h % 2) + 17 * (h // 2)
            for k_, f_ in enumerate(stages):
                bg.push_at(base + [0, 2, 4, 6, 8, 12, 14][k_], f_)

    def push_front(c):
        slot, par = c % 2, c % 2
        s0 = front_stages(slot, 0, par, 6, offload=True)
        s1 = front_stages(slot, 1, par, 7, offload=True)
        bg.push(s0, stride=4, offset=2)
        for k_, f_ in enumerate(s1):
            bg.push_at([4, 8, 12, 20, 24, 28][k_], f_)

    def phase2_chunk(c, first, last):
        slot = c % 2
        par = c % 2
        h0T = h0T_[par]
        if first:
            push_front(c)
            bg.flush()
            push_rope(c)
            bg.flush()
        if not last:
            load_x(c + 1, 1 - slot)
        hr = [B_h0T_[par][0], B_h0T_[par][1], B_wqg]
        for fo in range(4):
            ap, pq = phalf(fo, 0)
            mm_group(ap, pq, [(w_qg[:, kc, fo * 128:(fo + 1) * 128], h0T[:, kc, :]) for kc in range(8)], r=hr)
            bg.tick()
            S.op("act", lambda e, ap=ap, fo=fo: e.activation(
                out=qbd[0:64, fo, :, 0:128], in_=ap[0:64, :].rearrange("p (j t) -> p j t", j=2), func=AF.Identity,
                bias=binfm[0:64, fo:fo + 1], scale=1.0), r=pq + CONST, w=[B_qbd[fo]])
            S.op("act", lambda e, ap=ap, fo=fo: e.activation(
                out=qbd[64:128, fo, :, 128:256], in_=ap[64:128, :].rearrange("p (j t) -> p j t", j=2), func=AF.Identity,
                bias=binfm[64:128, fo:fo + 1], scale=1.0), r=pq + CONST, w=[B_qbd[fo]])
        bg.flush()
        units = []
        for j in range(2):
            i = 2 * c + j
            rels = [r_ for r_ in range(3) if 0 <= i + r_ - 1 < NT]
            for cc in range(4):
                units.append((j, i, rels, cc))
        NU = len(units)

        def a_banks(u):
            bx, bxq = pbank(4 + 2 * (u % 2))
            by, byq = pbank(5 + 2 * (u % 2))
            return bx, bxq, by, byq

        def a_qk(u):
            j, i, rels, cc = units[u]
            g = cc // 2
            bx, bxq, by, byq = a_banks(u)
            kdeps = [B_kaT[(i + r_ - 1) // 2] for r_ in rels]
            for r_ in rels:
                kb = i + r_ - 1
                if r_ < 2:
                    o_ap, oq = bx[:, r_ * 256:(r_ + 1) * 256], bxq
                else:
                    o_ap, oq = by[:, 0:256], byq
                S.op("pe", lambda e, o_ap=o_ap, kb=kb: e.matmul(
                    o_ap, lhsT=kaT[:, g, kb * 128:(kb + 1) * 128], rhs=qbd[:, cc, j, :],
                    start=True, stop=False), r=kdeps + [B_qbd[cc]], w=oq, track=False)
                for hh in range(2):
                    S.op("pe", lambda e, o_ap=o_ap, r_=r_, hh=hh: e.matmul(
                        o_ap[:, hh * 128:(hh + 1) * 128], lhsT=identS[:, 2 * cc + hh, :], rhs=distT[:, r_, :],
                        start=False, stop=(hh == 1)), r=[B_biasA], w=oq, track=(hh == 1))

        def a_exp(u):
            j, i, rels, cc = units[u]
            bx, bxq, by, byq = a_banks(u)
            pt = PTT[:, (u % 2) * 1024:(u % 2) * 1024 + 768]
            bp = [B_PTU[2 * (u % 2)], B_PTU[2 * (u % 2) + 1]]
            rx = [r_ for r_ in rels if r_ < 2]
            lo, hi = rx[0] * 256, (rx[-1] + 1) * 256
            S.op("act", lambda e: e.activation(out=pt[:, lo:hi], in_=bx[:, lo:hi], func=AF.Exp, scale=0.125),
                 r=bxq, w=bp)
            if 2 in rels:
                S.op("act", lambda e: e.activation(out=pt[:, 512:768], in_=by[:, 0:256], func=AF.Exp, scale=0.125),
                     r=byq, w=bp)

        def a_pv(u):
            j, i, rels, cc = units[u]
            g = cc // 2
            pt = PTT[:, (u % 2) * 1024:(u % 2) * 1024 + 768]
            bp = [B_PTU[2 * (u % 2)], B_PTU[2 * (u % 2) + 1]]
            ob, opq_all = pbank(u % 4)
            o_ap = ob[:, 0:256]
            opq = opq_all[0:2]
            vdeps = [B_va[i + r_ - 1] for r_ in rels] + [B_vaones]
            nr = len(rels)
            for n_, r_ in enumerate(rels):
                kb = i + r_ - 1
                S.op("pe", lambda e, kb=kb, r_=r_, n_=n_: e.matmul(
                    o_ap, lhsT=va[:, kb, g, :], rhs=pt[:, r_ * 256:(r_ + 1) * 256],
                    start=(n_ == 0), stop=False),
                    r=vdeps + bp, w=opq, track=False)
            for hh in range(2):
                for part, es in enumerate((esink_hi, esink_lo)):
                    lastm = (hh == 1 and part == 1)
                    S.op("pe", lambda e, hh=hh, es=es, lastm=lastm: e.matmul(
                        o_ap[:, hh * 128:(hh + 1) * 128], lhsT=sinkL[0:1, :],
                        rhs=es[0:1, 2 * cc + hh:2 * cc + hh + 1].broadcast_to([1, 128]),
                        start=False, stop=lastm), r=[B_eshl], w=opq, track=lastm)

        def a_norm(u):
            j, i, rels, cc = units[u]
            ob, opq_all = pbank(u % 4)
            o_ap = ob[:, 0:256]
            opq = opq_all[0:2]
            rec0 = s5b[0:64, :]
            tln = s5a[64:128, :]
            S.op("act", lambda e: e.activation(out=tln, in_=o_ap[64:128, :], func=AF.Ln), r=opq, w=[B_aden, B_arec])
            S.op("act", lambda e: e.activation(out=tln, in_=tln, func=AF.Exp, scale=-1.0), r=[B_aden, B_arec], w=[B_aden, B_arec])
            S.op("dve", lambda e: e.tensor_copy(out=rec0, in_=tln), r=[B_aden, B_arec], w=[B_s5b])
            S.op("dve", lambda e: e.scalar_tensor_tensor(
                out=oTa[0:64, cc, j * 128:(j + 1) * 128], in0=o_ap[0:64, 0:128], scalar=0.5, in1=rec0[:, 0:128],
                op0=ALU.mult, op1=ALU.mult), r=opq + [B_s5b], w=[B_oTa[cc]])
            S.op("dve", lambda e: e.scalar_tensor_tensor(
                out=oTa[64:128, cc, j * 128:(j + 1) * 128], in0=o_ap[0:64, 128:256], scalar=0.5, in1=rec0[:, 128:256],
                op0=ALU.mult, op1=ALU.mult), r=opq + [B_s5b], w=[B_oTa[cc]])

        a_qk(0)
        for u in range(NU):
            a_exp(u)
            if u + 1 < NU:
                a_qk(u + 1)
            a_pv(u)
            if u >= 1:
                a_norm(u - 1)
        a_norm(NU - 1)
        if not last:
            push_front(c + 1)
        for g in range(2):
            o_ap, o_pq = pbank(4 + 2 * g)
            s_ap, s_pq = pbank(5 + 2 * g)
            qrhs = qT[:, 2 * g:2 * g + 2, :]

            def qk(kt, g=g, qrhs=qrhs):
                ap, pq = pbank(kt % 4)
                S.op("pe", lambda e: e.matmul(ap, lhsT=kbT[:, g, kt * 128:(kt + 1) * 128], rhs=qrhs,
                                              start=True, stop=True),
                     r=[B_kbT[kt // 2], B_qT[2 * g], B_qT[2 * g + 1]], w=pq)

            def expo(kt):
                ap, pq = pbank(kt % 4)
                S.op("act", lambda e: e.activation(out=PTU[kt % 4], in_=ap, func=AF.Exp, scale=SC_B),
                     r=pq, w=[B_PTU[kt % 4]])

            def pv(kt, g=g, o_ap=o_ap, s_ap=s_ap, o_pq=o_pq, s_pq=s_pq):
                S.op("pe", lambda e: e.matmul(o_ap, lhsT=vb[:, kt, g * 128:(g + 1) * 128], rhs=PTU[kt % 4],
                                              start=(kt == 0), stop=(kt == NT - 1)),
                     r=[B_vb[kt], B_PTU[kt % 4]], w=o_pq, track=False)
                S.op("pe", lambda e: e.matmul(s_ap, lhsT=ones, rhs=PTU[kt % 4],
                                              start=(kt == 0), stop=(kt == NT - 1)),
                     r=[B_PTU[kt % 4], B_const], w=s_pq)
            qk(0)
            qk(1)
            for kt in range(NT):
                expo(kt)
                if kt + 2 < NT:
                    qk(kt + 2)
                pv(kt)
                bg.tick()
            rec = rAA
            S.op("act", lambda e, s_ap=s_ap: e.activation(out=rec, in_=s_ap, func=AF.Ln), r=s_pq, w=B_rA)
            S.op("act", lambda e: e.activation(out=rec, in_=rec, func=AF.Exp, scale=-1.0), r=B_rA, w=B_rA)
            S.op("dve", lambda e, o_ap=o_ap, g=g: e.scalar_tensor_tensor(
                out=oTb[:, 2 * g:2 * g + 2, :], in0=o_ap.rearrange("p (h t) -> p h t", h=2), scalar=0.5,
                in1=rec.rearrange("p (h t) -> p h t", h=2), op0=ALU.mult, op1=ALU.mult),
                r=o_pq + B_rA, w=[B_oTb[2 * g], B_oTb[2 * g + 1]])
        bg.flush()
        if not last:
            push_rope(c + 1)
        s5_s6(c)

    load_x(0, 0)
    for c_ in range(nch):
        phase2_chunk(c_, c_ == 0, c_ == nch - 1)
    bg.flush()

    S.barrier(dsems)
    if stage == 2:
        sdbg = newsem("dbg")
        d_ = nc.dram_tensor("dbg_h1", [S_LEN, D], F32, kind="ExternalOutput").ap()
        if sub >= 5:
            for i_ in range(nch):
                S.dma("sp", d_[i_ * 256:(i_ + 1) * 256, :], h1_d[i_ * 256:(i_ + 1) * 256, :], sdbg)
        for nm, ap_, n_ in (("dbg_qT", qT, 4 * CH2), ("dbg_oTa", oTa, 4 * CH2), ("dbg_oTb", oTb, 4 * CH2)):
            dd_ = nc.dram_tensor(nm, [128, n_], BF16, kind="ExternalOutput").ap()
            S.dma("sp", dd_, ap_.rearrange("p a b -> p (a b)"), sdbg)
        S.wait_all("sp", dsems)
        S.emit()
        es.close()
        return nc
    A3 = Alloc(P12_BASE, SBYTES)
    w_g = A3.get([128, 8, DFF], BF16)
    w_v = A3.get([128, 8, DFF], BF16)
    w_dn = A3.get([128, NFC, D], BF16)
    lnp3 = A3.get([128, 4, D], F32)
    hh = A3.get([128, 4, D], F32)
    h1T = A3.get([128, 8, CH3 + 2], BF16)
    halo = A3.get([128, 2, 8], F32)
    gext_ = [A3.get([128, CH3 + 2], F32) for _ in range(2)]
    t3aa = A3.get([128, 2 * CH3], F32)
    t3a_ = [t3aa[:, 0:CH3], t3aa[:, CH3:2 * CH3]]
    hb3_off = A3.off
    t3b_ = [A3.get([128, CH3], F32)] * 2
    hb3 = Alloc(hb3_off, A3.off).get([128, D], BF16)
    actT = A3.get([128, NFC, CH3], BF16)
    B_wg, B_wv, B_wd, B_lnp3 = Buf("wg"), Buf("wv"), Buf("wd"), Buf("lnp3")
    B_hh = [Buf("hh%d" % j) for j in range(4)]
    B_h1Th, B_halo = Buf("h1Th"), Buf("halo")
    B_gext_ = [Buf("gext0"), Buf("gext1")]
    B_gexth_ = [Buf("gexth0"), Buf("gexth1")]
    B_t3a_ = [Buf("t3a0"), Buf("t3a1")]
    B_t3b_ = [Buf("t3b0")] * 2
    B_hb3 = B_t3b_[0]
    B_actT = [Buf("actT%d" % i) for i in range(NFC)]
    wg_v = wg_d.rearrange("(k p) n -> p k n", p=128)
    wv_v = wv_d.rearrange("(k p) n -> p k n", p=128)
    WPC = [(0, 256), (256, 768), (768, 1408), (1408, 2176), (2176, 2816)]
    B_wgp = [Buf("wg%d" % i) for i in range(len(WPC))]
    B_wvp = [Buf("wv%d" % i) for i in range(len(WPC))]
    for i_, (c0, c1) in enumerate(WPC):
        S.dma("pool", w_g[:, :, c0:c1], wg_v[:, :, c0:c1], newsem("wg%d" % i_), w=[B_wgp[i_]])
        S.dma("pool", w_v[:, :, c0:c1], wv_v[:, :, c0:c1], newsem("wv%d" % i_), w=[B_wvp[i_]])

    def wpiece(fc):
        for i_, (c0, c1) in enumerate(WPC):
            if c0 <= fc * 128 < c1:
                return i_
    wd_v = wd_d.rearrange("(k p) n -> p k n", p=128)
    s_w3d = newsem("w3d")
    for k0 in range(0, NFC, 11):
        S.dma("pool", w_dn[:, k0:k0 + 11, :], wd_v[:, k0:k0 + 11, :], s_w3d, w=[B_wd])
    s_m3 = newsem("m3")
    S.dma("sp", lnp3, lnp_d[:, 2:6, :], s_m3, w=[B_lnp3])
    sem_halo = newsem("halo")
    sem_hhj = [newsem("hh%d" % j) for j in range(4)]
    B_h1Tj = [Buf("h1T%d" % j) for j in range(4)]

    sem_alt = newsem("hhalt")

    def prep_stages(c, j, alt=False):
        t0 = c * CH3
        ht = t3aa if alt else hh[:, j, :]
        Bh = [B_t3a_[0], B_t3a_[1]] if alt else [B_hh[j]]
        pb, pq = pbank(6 + j % 2)
        pbb = pb.bitcast(BF16)

        def p0():
            S.dma("sp", ht, h1_d[t0 + j * 128:t0 + (j + 1) * 128, :], sem_alt if alt else sem_hhj[j], w=Bh)
            if j == 0:
                for side, tk in ((0, t0 - 1), (1, t0 + CH3)):
                    if 0 <= tk < S_LEN:
                        S.dma("sp", halo[:, side, :], h1_d[tk:tk + 1, :].rearrange("o (k p) -> p (o k)", p=128),
                              sem_halo, w=[B_halo], allow_slow_non_contiguous=True)

        def p1():
            S.op("dve", lambda e: e.tensor_tensor(out=ht, in0=ht, in1=lnp3[:, 0, :], op=ALU.mult),
                 r=Bh + [B_lnp3], w=Bh)

        def p2():
            S.op("dve", lambda e: e.tensor_tensor(out=ht, in0=ht, in1=lnp3[:, 1, :], op=ALU.add),
                 r=Bh + [B_lnp3], w=Bh)

        def p3():
            S.op("act", lambda e: e.activation(out=hb3, in_=ht, func=AF.Copy), r=Bh, w=[B_hb3])

        def p4():
            for kc in range(8):
                S.op("pe", lambda e, kc=kc: e.transpose(out=pbb[:, kc * 128:(kc + 1) * 128],
                                                        in_=hb3[:, kc * 128:(kc + 1) * 128], identity=ident),
                     r=[B_hb3, B_const], w=pq, track=(kc == 7))

        def p5():
            S.op("act", lambda e: e.activation(
                out=h1T[:, :, 1 + j * 128:1 + (j + 1) * 128], in_=pbb.rearrange("p (k t) -> p k t", k=8), func=AF.Copy),
                r=pq, w=[B_h1Tj[j]])
        return [p0, p1, p2, p3, p4, p5]

    def halo_finish(c):
        t0 = c * CH3
        for side, tk, col in ((0, t0 - 1, 0), (1, t0 + CH3, CH3 + 1)):
            if 0 <= tk < S_LEN:
                S.op("dve", lambda e, side=side: e.tensor_tensor(out=halo[:, side, :], in0=halo[:, side, :],
                                                                 in1=g1fm[:, 0, :], op=ALU.mult),
                     r=[B_halo, B_const], w=[B_halo])
                S.op("dve", lambda e, side=side, col=col: e.tensor_tensor(out=h1T[:, :, col], in0=halo[:, side, :],
                                                                          in1=g1fm[:, 1, :], op=ALU.add),
                     r=[B_halo, B_const], w=[B_h1Th])
            else:
                S.op("dve", lambda e, col=col: e.memset(h1T[:, :, col], 0.0), w=[B_h1Th])

    def phase3_chunk(c):
        t0 = c * CH3
        if c == 0:
            for j in range(4):
                for f_ in prep_stages(c, j):
                    f_()
        bg.flush()
        halo_finish(c)
        hr3 = B_h1Tj + [B_h1Th]
        for fc in range(NFC):
            fs = slice(fc * 128, (fc + 1) * 128)
            gext, t3a, t3b = gext_[fc % 2], t3a_[fc % 2], t3b_[fc % 2]
            B_gext, B_gexth, B_t3a, B_t3b = B_gext_[fc % 2], B_gexth_[fc % 2], B_t3a_[fc % 2], B_t3b_[fc % 2]
            gp, gq_ = pbank(fc % 2)
            vp, vq_ = pbank(2 + fc % 2)
            ghp_full, ghq_full = pbank(4 + fc % 2)
            ghp = ghp_full[:, 0:2]
            for kc in range(8):
                S.op("pe", lambda e, kc=kc, gp=gp, fs=fs: e.matmul(gp, lhsT=w_g[:, kc, fs], rhs=h1T[:, kc, 1:CH3 + 1],
                                                                   start=(kc == 0), stop=(kc == 7)),
                     r=hr3 + [B_wgp[wpiece(fc)]], w=gq_, track=(kc == 7))
                S.op("pe", lambda e, kc=kc, ghp=ghp, fs=fs: e.matmul(ghp, lhsT=w_g[:, kc, fs],
                                                                     rhs=h1T[:, kc, 0:CH3 + 2:CH3 + 1],
                                                                     start=(kc == 0), stop=(kc == 7)),
                     r=hr3 + [B_wgp[wpiece(fc)]], w=[ghq_full[0]], track=(kc == 7))
            mm_group(vp, vq_, [(w_v[:, kc, fs], h1T[:, kc, 1:CH3 + 1]) for kc in range(8)], r=hr3 + [B_wvp[wpiece(fc)]])
            S.op("act", lambda e, gp=gp, gext=gext: e.activation(out=gext[:, 1:CH3 + 1], in_=gp, func=AF.Copy), r=gq_, w=[B_gext])
            S.op("dve", lambda e, ghp=ghp, gext=gext: e.tensor_copy(out=gext[:, 0:CH3 + 2:CH3 + 1], in_=ghp),
                 r=[ghq_full[0]], w=[B_gexth])
            ge = [B_gext, B_gexth]
            S.op("dve", lambda e, fc=fc, gext=gext, t3a=t3a, t3b=t3b: e.tensor_scalar(out=t3a, in0=gext[:, 0:CH3], scalar1=cwfm[:, fc, 0:1],
                                                         scalar2=None, op0=ALU.mult), r=ge + [B_const], w=[B_t3a])
            S.op("dve", lambda e, fc=fc, gext=gext, t3a=t3a, t3b=t3b: e.scalar_tensor_tensor(out=t3a, in0=gext[:, 1:CH3 + 1], scalar=cwfm[:, fc, 1:2],
                                                                in1=t3a, op0=ALU.mult, op1=ALU.add),
                 r=ge + [B_const, B_t3a], w=[B_t3a])
            S.op("dve", lambda e, fc=fc, gext=gext, t3a=t3a, t3b=t3b: e.scalar_tensor_tensor(out=t3a, in0=gext[:, 2:CH3 + 2], scalar=cwfm[:, fc, 2:3],
                                                                in1=t3a, op0=ALU.mult, op1=ALU.add),
                 r=ge + [B_const, B_t3a], w=[B_t3a])
            S.op("act", lambda e, fc=fc, gext=gext, t3a=t3a, t3b=t3b: e.activation(out=t3b, in_=t3a, func=AF.Gelu, bias=cwfm[:, fc, 3:4], scale=1.0),
                 r=[B_t3a, B_const], w=[B_t3b])
            S.op("dve", lambda e, fc=fc, vp=vp, t3b=t3b: e.tensor_tensor(out=actT[:, fc, :], in0=t3b, in1=vp, op=ALU.mult),
                 r=vq_ + [B_t3b], w=[B_actT[fc]])
        if c + 1 < NCH3:
            bg.push(prep_stages(c + 1, 3, alt=True), stride=1, offset=0)
        for j in range(4):
            ht = hh[:, j, :]
            for nh in range(2):
                yp, yq = pbank((2 * j + nh) % 4)
                for fc in range(NFC):
                    S.op("pe", lambda e, fc=fc, yp=yp, j=j, nh=nh: e.matmul(
                        yp, lhsT=actT[:, fc, j * 128:(j + 1) * 128], rhs=w_dn[:, fc, nh * 512:(nh + 1) * 512],
                        start=(fc == 0), stop=(fc == NFC - 1)), r=[B_actT[fc], B_wd], w=yq, track=(fc == NFC - 1))
                bg.tick()
                S.op("dve", lambda e, ht=ht, yp=yp, nh=nh: e.scalar_tensor_tensor(
                    out=ht[:, nh * 512:(nh + 1) * 512], in0=ht[:, nh * 512:(nh + 1) * 512], scalar=ALPHA, in1=yp,
                    op0=ALU.mult, op1=ALU.add), r=yq + [B_hh[j]], w=[B_hh[j]])
            ln_stats(ht, j, B_hh[j])
            S.op("dve", lambda e, ht=ht, j=j: e.scalar_tensor_tensor(
                out=ht, in0=ht, scalar=mv[:, j, 0:1], in1=lnp3[:, 2, :], op0=ALU.subtract, op1=ALU.mult),
                r=[B_hh[j], B_mv[j], B_lnp3], w=[B_hh[j]])
            S.op("dve", lambda e, ht=ht, j=j: e.scalar_tensor_tensor(
                out=ht, in0=ht, scalar=rstd[:, j:j + 1], in1=lnp3[:, 3, :], op0=ALU.mult, op1=ALU.add),
                r=[B_hh[j], B_rstd[j], B_lnp3], w=[B_hh[j]])
            S.dma("sp", out_d[t0 + j * 128:t0 + (j + 1) * 128, :], ht, sem_hhj[j], r=[B_hh[j]])
            if c + 1 < NCH3:
                if j < 3:
                    bg.push(prep_stages(c + 1, j), stride=1, offset=1)
                else:
                    bg.push_at(1, lambda: S.op("dve", lambda e: e.tensor_copy(out=hh[:, 3, :], in_=t3aa),
                                               r=[B_t3a_[0], B_t3a_[1]], w=[B_hh[3]]))
    for c_ in range(NCH3):
        phase3_chunk(c_)
    bg.flush()
    S.wait_all("sp", dsems)
    S.emit()
    es.close()
    return nc


def _host_consts():
    bf = ml_dtypes.bfloat16
    ident = np.eye(128, dtype=np.float32).astype(bf)
    ones = np.ones((128, 128), dtype=np.float32).astype(bf)
    R = np.zeros((128, 128), dtype=np.float32)
    for d in range(128):
        if d % 64 < 32:
            R[d, d + 32] = -1.0
        else:
            R[d, d - 32] = 1.0
    rt = np.ascontiguousarray(R.T).astype(bf)
    freqs = (np.float32(10000.0) ** (-(np.arange(32, dtype=np.float32) / np.float32(32)))).astype(np.float32)
    pos = np.arange(64, dtype=np.float32)
    ang = (pos[None, :] * freqs[:, None]).astype(np.float32)
    ang128 = np.tile(ang, (4, 1))
    rope = np.stack([np.cos(ang128), np.sin(ang128)], axis=1).astype(np.float32)
    a = np.arange(128)[:, None]
    b = np.arange(128)[None, :]
    dist_t = np.zeros((128, 3, 128), dtype=np.float32)
    for rel in range(3):
        dist = np.abs(b - a - (rel - 1) * 128)
        dist_t[:, rel, :] = np.where(dist <= 128, -dist, -30000.0)
    dist_t = dist_t.astype(bf)
    ident_s = np.zeros((128, 8, 128), dtype=np.float32)
    for h in range(8):
        ident_s[:, h, :] = np.eye(128, dtype=np.float32) * (8.0 * 2.0 ** (-(h + 1)))
    ident_s = ident_s.astype(bf)
    bias = (ident_s, dist_t)
    return ident, ones, rt, rope, bias


_NC_CACHE = {}


def make_shared(x, ln_in_g, ln_in_b, w_in, b_in, a_sinks, b_q_norm, b_k_norm, w_o_a, w_o_b, w_out,
           ln1_g, ln1_b, w_ffn_gate, w_ffn_val, ffn_conv_w, ffn_conv_b, w_ffn_down, ln2_g, ln2_b):
    f = lambda t: np.ascontiguousarray(np.asarray(t, dtype=np.float32))
    x = f(x)
    b_in0 = f(b_in)[0]
    ident, ones, rt, rope, bias = _host_consts()
    lnp = np.stack([f(ln_in_g), f(ln_in_b), f(ln1_g)[0], f(ln1_b)[0], f(ln2_g)[0], f(ln2_b)[0]], 0)
    lnp = np.ascontiguousarray(np.broadcast_to(lnp[None], (128, 6, D)))
    bin_fm = np.ascontiguousarray(b_in0.reshape(30, 128).T)
    bka = np.stack([np.tile(b_in0[C_KA:C_KA + 64], 2), np.tile(b_in0[C_KA + 64:C_KA + 128], 2)], 1)
    bv = np.concatenate([b_in0[C_VA:C_VA + 128], b_in0[C_VB:C_VB + 256]])
    bv_bc = np.ascontiguousarray(np.broadcast_to(bv[None], (128, 384)))
    ln1_fm = np.stack([f(ln1_g)[0].reshape(8, 128).T, f(ln1_b)[0].reshape(8, 128).T], 1)
    cw = np.concatenate([f(ffn_conv_w)[0], f(ffn_conv_b)], 0)
    cw_fm = np.ascontiguousarray(cw.reshape(4, NFC, 128).transpose(2, 1, 0))
    sinks_bc = np.ascontiguousarray(np.broadcast_to(f(a_sinks)[0][None], (128, 8)))
    gqk = np.stack([f(b_q_norm)[0], f(b_k_norm)[0]], 1)
    shared = {
        "w_in": f(w_in)[0], "w_o_a": f(w_o_a)[0], "w_o_b": f(w_o_b)[0], "w_out": f(w_out)[0],
        "w_g": f(w_ffn_gate)[0], "w_v": f(w_ffn_val)[0], "w_d": f(w_ffn_down)[0],
        "lnp": lnp, "bin_fm": bin_fm, "bka_dup": np.ascontiguousarray(bka), "bv_bc": bv_bc,
        "ln1_fm": np.ascontiguousarray(ln1_fm), "cw_fm": cw_fm, "sinks_bc": sinks_bc,
        "gqk_fm": np.ascontiguousarray(gqk), "ident": ident, "ones": ones, "rt": rt,
        "rope_rc": np.ascontiguousarray(rope), "ident_s": bias[0], "dist_t": bias[1],
    }
    return x, shared


def kernel(x, ln_in_g, ln_in_b, w_in, b_in, a_sinks, b_q_norm, b_k_norm, w_o_a, w_o_b, w_out,
           ln1_g, ln1_b, w_ffn_gate, w_ffn_val, ffn_conv_w, ffn_conv_b, w_ffn_down, ln2_g, ln2_b):
    x, shared = make_shared(x, ln_in_g, ln_in_b, w_in, b_in, a_sinks, b_q_norm, b_k_norm, w_o_a, w_o_b, w_out,
                            ln1_g, ln1_b, w_ffn_gate, w_ffn_val, ffn_conv_w, ffn_conv_b, w_ffn_down, ln2_g, ln2_b)
    if "nc" not in _NC_CACHE:
        _NC_CACHE["nc"] = build_nc()
    nc = _NC_CACHE["nc"]
    in_maps = []
    for b in range(8):
        m = dict(shared)
        m["x"] = x[b]
        in_maps.append(m)
    res = run_bass_kernel_spmd(nc, in_maps, core_ids=list(range(8)))
    return np.stack([np.asarray(r["out"], dtype=np.float32) for r in res.results], 0)
```

```python
import numpy as np
import ml_dtypes
from contextlib import ExitStack
import concourse.bass as bass
import concourse.mybir as mybir
from concourse.bass_utils import run_bass_kernel_spmd

F32 = mybir.dt.float32
BF16 = mybir.dt.bfloat16
AF = mybir.ActivationFunctionType
ALU = mybir.AluOpType

S_LEN = 4096
D = 1024
NT = 32
DFF = 2816
NFC = 22
ALPHA = float(2.0 ** 0.25)
LN_EPS = 1e-5
RMS_EPS = 1e-6
CH2 = 256
NCH2 = S_LEN // CH2
CH3 = 512
NCH3 = S_LEN // CH3
C_QA, C_KA, C_VA, C_QB, C_KB, C_VB, C_GA, C_GB = 0, 512, 640, 768, 1280, 1536, 1792, 2816


class Buf:
    __slots__ = ("name", "last_w", "readers", "is_bank")

    def __init__(self, name, is_bank=False):
        self.name = name
        self.last_w = None
        self.readers = {}
        self.is_bank = is_bank


class Sem:
    def __init__(self, h):
        self.h = h
        self.count = 0


class Sched:
    ENG = ("pe", "act", "dve", "pool", "sp")

    def __init__(self, nc, es):
        self.nc = nc
        self.es = es
        self.ops = {e: [] for e in self.ENG}
        self.esem = {e: Sem(es.enter_context(nc.semaphore("sem_" + e))) for e in self.ENG if e != "sp"}
        self.seen = {e: {} for e in self.ENG}
        self.nsem = 0
        self.bankof = {}

    def new_sem(self, name):
        self.nsem += 1
        return Sem(self.es.enter_context(self.nc.semaphore("d_%s_%d" % (name, self.nsem))))

    def _waits(self, eng, r, w):
        need = {}

        def add(dep, war, bank=False):
            if dep is None:
                return
            sem, val = dep
            if war and sem is self.esem.get(eng) and (eng == "pe" or bank):
                return
            if need.get(sem, 0) < val:
                need[sem] = val
        for b in r:
            add(b.last_w, False)
        for b in w:
            add(b.last_w, True, b.is_bank)
            for sem, val in b.readers.items():
                add((sem, val), True, b.is_bank)
        out = []
        seen = self.seen[eng]
        for sem, val in need.items():
            if seen.get(sem, 0) < val:
                seen[sem] = val
                out.append((sem.h, val))
        return out

    def op(self, eng, fn, r=(), w=(), track=True):
        banks = []
        for b in list(r) + list(w):
            bk = self.bankof.get(b)
            if bk is not None and bk not in banks:
                banks.append(bk)
        if banks:
            w = list(w) + banks
        waits = self._waits(eng, r, w)
        sem = self.esem[eng]
        val = sem.count + 1
        if eng != "pe":
            track = True
        if track:
            sem.count = val
        self.ops[eng].append((waits, fn, (sem.h, 1) if track else None))
        for b in w:
            b.last_w = (sem, val)
            b.readers = {}
        for b in r:
            if b.readers.get(sem, 0) < val:
                b.readers[sem] = val

    def dma(self, q, out, in_, sem, r=(), w=(), **kw):
        waits = self._waits(q, r, w)
        sem.count += 16
        val = sem.count
        self.ops[q].append((waits, lambda e, o=out, i=in_: e.dma_start(out=o, in_=i, **kw), (sem.h, 16)))
        for b in w:
            b.last_w = (sem, val)
            b.readers = {}
        for b in r:
            if b.readers.get(sem, 0) < val:
                b.readers[sem] = val

    def wait_all(self, eng, sems):
        waits = []
        for s in sems:
            if s.count > 0 and self.seen[eng].get(s, 0) < s.count:
                self.seen[eng][s] = s.count
                waits.append((s.h, s.count))
        self.ops[eng].append((waits, None, None))

    def barrier(self, dsems):
        allsems = list(self.esem.values()) + list(dsems)
        for e in self.ENG:
            self.wait_all(e, allsems)

    def emit(self):
        nc = self.nc
        block = self.es.enter_context(nc.Block())

        def run(eng_name):
            def f(e):
                for waits, fn, inc in self.ops[eng_name]:
                    for h, v in waits:
                        e.wait_ge(h, v)
                    if fn is not None:
                        ins = fn(e)
                        if inc is not None:
                            ins.then_inc(inc[0], inc[1])
            return f
        block.tensor(run("pe"))
        block.scalar(run("act"))
        block.vector(run("dve"))
        block.gpsimd(run("pool"))
        block.sync(run("sp"))


def build_nc(stage=3, sub=9, nch=NCH2, asub=9):
    nc = bass.Bass("TRN2", target_bir_lowering=False)
    es = ExitStack()
    dram = lambda n, s, dt=F32, kind="ExternalInput": nc.dram_tensor(n, list(s), dt, kind=kind).ap()
    x_d = dram("x", [S_LEN, D])
    win_d = dram("w_in", [D, 3840])
    woa_d = dram("w_o_a", [512, D])
    wob_d = dram("w_o_b", [512, D])
    wout_d = dram("w_out", [D, D])
    wg_d = dram("w_g", [D, DFF])
    wv_d = dram("w_v", [D, DFF])
    wd_d = dram("w_d", [DFF, D])
    lnp_d = dram("lnp", [128, 6, D])
    binfm_d = dram("bin_fm", [128, 30])
    bka_d = dram("bka_dup", [128, 2])
    bv_d = dram("bv_bc", [128, 384])
    g1fm_d = dram("ln1_fm", [128, 2, 8])
    cw_d = dram("cw_fm", [128, NFC, 4])
    sink_d = dram("sinks_bc", [128, 8])
    gqk_d = dram("gqk_fm", [128, 2])
    ident_d = dram("ident", [128, 128], BF16)
    ones_d = dram("ones", [128, 128], BF16)
    rt_d = dram("rt", [128, 128], BF16)
    rope_d = dram("rope_rc", [128, 2, 64])
    idents_d = dram("ident_s", [128, 8, 128], BF16)
    distt_d = dram("dist_t", [128, 3, 128], BF16)
    h1_d = dram("h1_scratch", [S_LEN, D], F32, kind="Internal")
    out_d = dram("out", [S_LEN, D], F32, kind="ExternalOutput")

    S = Sched(nc, es)
    SBYTES = 212800
    sb = es.enter_context(nc.sbuf_tensor("SB", [128, SBYTES // 2], BF16))
    psum = [es.enter_context(nc.psum_tensor("ps%d" % i, [128, 512], F32)) for i in range(8)]
    PQ = [[Buf("ps%d_%d" % (b, q)) for q in range(4)] for b in range(8)]
    for b_ in range(8):
        bk_ = Buf("bank%d" % b_, is_bank=True)
        for q_ in PQ[b_]:
            S.bankof[q_] = bk_

    def pbank(b):
        return psum[b][:, :], PQ[b]

    def phalf(b, s):
        return psum[b][:, s * 256:(s + 1) * 256], PQ[b][2 * s:2 * s + 2]

    class Alloc:
        def __init__(self, base, limit):
            self.off = base
            self.limit = limit

        def get(self, shape, dt):
            esz = 4 if dt == F32 else 2
            n = 1
            for s in shape[1:]:
                n *= s
            nb = (n * esz + 63) // 64 * 64
            assert self.off + nb <= self.limit, ("SBUF overflow", self.off, nb, self.limit)
            ap = sb[:, self.off // 2:(self.off + nb) // 2]
            if dt == F32:
                ap = ap.bitcast(F32)
            ap = ap[:, 0:n]
            if len(shape) == 3:
                ap = ap.rearrange("p (a b) -> p a b", a=shape[1])
            elif len(shape) == 4:
                ap = ap.rearrange("p (a b c) -> p a b c", a=shape[1], b=shape[2])
            self.off += nb
            return ap

    A0 = Alloc(0, SBYTES)
    ident = A0.get([128, 128], BF16)
    ones = A0.get([128, 128], BF16)
    rt = A0.get([128, 128], BF16)
    binfm = A0.get([128, 30], F32)
    binh = A0.get([128, 30], F32)
    bka = A0.get([128, 2], F32)
    g1fm = A0.get([128, 2, 8], F32)
    cwfm = A0.get([128, NFC, 4], F32)
    esink = A0.get([128, 8], F32)
    esink_hi = A0.get([128, 8], BF16)
    esink_lo = A0.get([128, 8], BF16)
    sinkL = A0.get([128, 128], BF16)
    gqk = A0.get([128, 2], F32)
    roperc = A0.get([128, 2, 64], F32)
    neghalf = A0.get([128, 3], F32)
    epsln = neghalf[:, 0:1]
    epsrms = neghalf[:, 1:2]
    nhalf = neghalf[:, 2:3]
    stat = A0.get([128, 48], F32)
    mv = A0.get([128, 4, 2], F32)
    rstd = A0.get([128, 4], F32)
    B_const = Buf("const")
    B_stat = [Buf("stat%d" % j) for j in range(4)]
    B_mv = [Buf("mv%d" % j) for j in range(4)]
    B_rstd = [Buf("rstd%d" % j) for j in range(4)]
    dsems = []

    def newsem(n):
        s = S.new_sem(n)
        dsems.append(s)
        return s
    sem_c = newsem("const")
    B_c2 = Buf("neghalf")
    B_binh = Buf("binh")
    B_esink = Buf("esink")
    B_eshl = Buf("esink_hl")
    S.op("pool", lambda e: e.memset(epsln, LN_EPS), w=[B_c2], track=False)
    S.op("pool", lambda e: e.memset(nhalf, -0.5), w=[B_c2], track=False)
    S.op("pool", lambda e: e.memset(epsrms, RMS_EPS), w=[B_c2])

    def issue_consts():
        for ap, d in ((ident, ident_d), (ones, ones_d), (rt, rt_d), (binfm, binfm_d), (bka, bka_d),
                      (g1fm, g1fm_d), (cwfm, cw_d), (esink, sink_d), (gqk, gqk_d), (roperc, rope_d)):
            S.dma("sp", ap, d, sem_c, w=[B_const])

    def const_ops():
        S.op("dve", lambda e: e.tensor_scalar(out=binh, in0=binfm, scalar1=0.5, scalar2=None, op0=ALU.mult),
             r=[B_const], w=[B_binh])
        S.op("act", lambda e: e.activation(out=esink, in_=esink, func=AF.Exp), r=[B_const], w=[B_esink])
        S.op("dve", lambda e: e.tensor_copy(out=esink_hi, in_=esink), r=[B_esink], w=[B_eshl])
        S.op("dve", lambda e: e.tensor_tensor(out=esink_lo, in0=esink, in1=esink_hi, op=ALU.subtract), r=[B_esink, B_eshl], w=[B_eshl])
        S.op("dve", lambda e: e.memset(sinkL[:, 0:64], 0.0), w=[B_eshl], track=False)
        S.op("dve", lambda e: e.memset(sinkL[:, 64:128], 1.0), w=[B_eshl])
    CONST = [B_const, B_binh, B_esink]
    P12_BASE = A0.off

    def ln_stats(xt, j, B_x):
        for hh in range(2):
            S.op("dve", lambda e, hh=hh: e.bn_stats(out=stat[:, (2 * j + hh) * 6:(2 * j + hh + 1) * 6], in_=xt[:, hh * 512:(hh + 1) * 512]),
                 r=[B_x], w=[B_stat[j]], track=(hh == 1))
        S.op("dve", lambda e: e.bn_aggr(out=mv[:, j, :], in_=stat[:, 12 * j:12 * j + 12]), r=[B_stat[j]], w=[B_mv[j]])
        S.op("dve", lambda e: e.tensor_scalar(out=rstd[:, j:j + 1], in0=mv[:, j, 1:2], scalar1=LN_EPS, scalar2=None,
                                              op0=ALU.add), r=[B_mv[j]], w=[B_rstd[j]])
        S.op("pool", lambda e: e.tensor_tensor(out=rstd[:, j:j + 1], in0=rstd[:, j:j + 1], in1=nhalf, op=ALU.pow),
             r=[B_rstd[j], B_c2], w=[B_rstd[j]])

    A = Alloc(P12_BASE, SBYTES)
    kbT = A.get([128, 2, S_LEN], BF16)
    vb = A.get([128, NT, 256], BF16)
    kaT = A.get([128, 2, S_LEN], BF16)
    va = A.get([128, NT, 2, 128], BF16)
    w_qg = A.get([128, 8, 3072], BF16)
    w_oa = A.get([128, 4, D], BF16)
    w_ob = A.get([128, 4, D], BF16)
    w_out = A.get([128, 8, D], BF16)
    lnp0 = A.get([128, 2, D], F32)
    xh = [A.get([128, 2, D], F32) for _ in range(2)]
    h0T_ = [A.get([128, 8, CH2], BF16) for _ in range(2)]
    rAA = A.get([128, 512], F32)
    rBB = A.get([128, 512], F32)
    rA = [rAA[:, 0:256], rAA[:, 256:512]]
    rB = [rBB[:, 0:256], rBB[:, 256:512]]
    s5ab = A.get([128, 512], F32)
    s5a = s5ab[:, 0:256]
    s5b = s5ab[:, 256:512]
    hb = s5ab.bitcast(BF16)
    aden = s5a[:, 0:128]
    arec = s5a[:, 128:256]
    costab = A.get([128, 256], F32)
    sintab = A.get([128, 256], F32)
    P2ONLY = A.off
    qT = A.get([128, 4, CH2], BF16)
    qbd = A.get([128, 4, 2, 256], BF16)
    PTT = A.get([128, 2048], BF16)
    PTU = [PTT[:, i * 512:(i + 1) * 512] for i in range(4)]
    oTa = A.get([128, 4, CH2], BF16)
    oTb = A.get([128, 4, CH2], BF16)
    identS = A.get([128, 8, 128], BF16)
    distT = A.get([128, 3, 128], BF16)
    mT = A.get([128, 8, CH2], BF16)
    Akv = Alloc(P2ONLY, A.off)
    print('phase12 sbuf used', A.off)
    w_kv = Akv.get([128, 8, 896], BF16)
    bvbc = Akv.get([128, 384], F32)

    B_kbT = [Buf("kbT%d" % c) for c in range(NCH2)]
    B_kaT = [Buf("kaT%d" % c) for c in range(NCH2)]
    B_vb = [Buf("vb%d" % t) for t in range(NT)]
    B_va = [Buf("va%d" % t) for t in range(NT)]
    B_vaones = Buf("vaones")
    B_wqg, B_woa, B_wob, B_wout, B_wkv = Buf("wqg"), Buf("woa"), Buf("wob"), Buf("wout"), Buf("wkv")
    B_lnp0 = Buf("lnp0")
    B_xh = [[Buf("xh%d_%d" % (i, j)) for j in range(2)] for i in range(2)]
    B_h0T_ = [[Buf("h0T%d_%d" % (p_, j)) for j in range(2)] for p_ in range(2)]
    B_qT = [Buf("qT%d" % i) for i in range(4)]
    B_qbd = [Buf("qbd%d" % i) for i in range(4)]
    B_PTU = [Buf("PTU%d" % i) for i in range(4)]
    B_oTa = [Buf("oTa%d" % i) for i in range(4)]
    B_oTb = [Buf("oTb%d" % i) for i in range(4)]
    B_mT = [Buf("mT%d" % i) for i in range(8)]
    B_rA = [Buf("rA0"), Buf("rA1")]
    B_rB = [Buf("rB0"), Buf("rB1")]
    B_s5b = Buf("s5b")
    B_aden, B_arec, B_aotmp = Buf("aden"), Buf("arec"), Buf("aotmp")
    HB = [B_aden, B_arec, B_s5b]
    B_tab = Buf("tab")
    B_tabhi = Buf("tabhi")
    B_biasA = Buf("biasA")
    B_bvbc = Buf("bvbc")

    sem_xh = [newsem("xh%d" % i) for i in range(2)]
    win_v = win_d.rearrange("(k p) n -> p k n", p=128)
    s_wkv = newsem("wkv")
    S.dma("pool", w_kv[:, :, 0:64], win_v[:, :, C_KA:C_KA + 64], s_wkv, w=[B_wkv])
    S.dma("pool", w_kv[:, :, 64:128], win_v[:, :, C_KA:C_KA + 64], s_wkv, w=[B_wkv])
    S.dma("pool", w_kv[:, :, 128:192], win_v[:, :, C_KA + 64:C_KA + 128], s_wkv, w=[B_wkv])
    S.dma("pool", w_kv[:, :, 192:256], win_v[:, :, C_KA + 64:C_KA + 128], s_wkv, w=[B_wkv])
    S.dma("pool", w_kv[:, :, 256:384], win_v[:, :, C_VA:C_VA + 128], s_wkv, w=[B_wkv])
    S.dma("pool", w_kv[:, :, 384:896], win_v[:, :, C_KB:C_KB + 512], s_wkv, w=[B_wkv])
    s_misc = newsem("misc")
    s_bv = newsem("bv")

    def issue_setup():
        load_x(0, 0)
        S.dma("sp", lnp0, lnp_d[:, 0:2, :], s_misc, w=[B_lnp0])
        issue_consts()
        S.dma("sp", bvbc, bv_d, s_bv, w=[B_bvbc])
        S.op("pool", lambda e: e.memset(va[:, :, :, 64:128], 1.0), w=[B_vaones])
        for tab, k in ((costab, 0), (sintab, 1)):
            S.op("pool", lambda e, tab=tab, k=k: e.tensor_copy(
                out=tab[64:128, :].rearrange("p (r c) -> p r c", r=4),
                in_=roperc[64:128, k, :].unsqueeze(1).broadcast_to([64, 4, 64])), r=[B_const], w=[B_tabhi])

    def issue_w2():
        s_w2 = newsem("w2")
        for (dst, c0, n) in ((0, C_QA, 512), (512, C_QB, 512), (1024, C_GA, 1024), (2048, C_GB, 1024)):
            S.dma("pool", w_qg[:, :, dst:dst + n], win_v[:, :, c0:c0 + n], s_w2, w=[B_wqg])
        s_woa, s_wob, s_wout = newsem("woa"), newsem("wob"), newsem("wout")
        S.dma("pool", w_oa, woa_d.rearrange("(k p) n -> p k n", p=128), s_woa, w=[B_woa])
        S.dma("pool", w_ob, wob_d.rearrange("(k p) n -> p k n", p=128), s_wob, w=[B_wob])
        S.dma("pool", w_out, wout_d.rearrange("(k p) n -> p k n", p=128), s_wout, w=[B_wout])

    def load_x(c, slot):
        src = x_d[c * CH2:(c + 1) * CH2, :].rearrange("(j p) d -> p j d", p=128)
        S.dma("sp", xh[slot], src, sem_xh[slot], w=B_xh[slot])

    def ln_in_tile(slot, j):
        xt = xh[slot][:, j, :]
        Bx = B_xh[slot][j]
        ln_stats(xt, j, Bx)
        S.op("dve", lambda e: e.scalar_tensor_tensor(
            out=xt, in0=xt, scalar=mv[:, j, 0:1], in1=lnp0[:, 0, :], op0=ALU.subtract, op1=ALU.mult),
            r=[Bx, B_mv[j], B_lnp0], w=[Bx])
        S.op("dve", lambda e: e.scalar_tensor_tensor(
            out=xt, in0=xt, scalar=rstd[:, j:j + 1], in1=lnp0[:, 1, :], op0=ALU.mult, op1=ALU.add),
            r=[Bx, B_rstd[j], B_lnp0], w=[Bx])

    def transpose_tile(slot, j, par, bank):
        xt = xh[slot][:, j, :]
        Bx = B_xh[slot][j]
        S.op("act", lambda e: e.activation(out=hb, in_=xt, func=AF.Copy), r=[Bx], w=HB)
        pb, pq = pbank(bank)
        pbb = pb.bitcast(BF16)
        for kc in range(8):
            S.op("pe", lambda e, kc=kc: e.transpose(out=pbb[:, kc * 128:(kc + 1) * 128],
                                                    in_=hb[:, kc * 128:(kc + 1) * 128], identity=ident),
                 r=HB + [B_const], w=pq, track=(kc == 7))
        S.op("act", lambda e: e.activation(
            out=h0T_[par][:, :, j * 128:(j + 1) * 128], in_=pbb.rearrange("p (k t) -> p k t", k=8), func=AF.Copy),
            r=pq, w=[B_h0T_[par][j]])

    def front_stages(slot, j, par, bank, offload=False):
        xt = xh[slot][:, j, :]
        Bx = B_xh[slot][j]
        pb, pq = pbank(bank)
        pbb = pb.bitcast(BF16)

        def f0():
            for hh in range(2):
                S.op("dve", lambda e, hh=hh: e.bn_stats(out=stat[:, (2 * j + hh) * 6:(2 * j + hh + 1) * 6],
                                                        in_=xt[:, hh * 512:(hh + 1) * 512]),
                     r=[Bx], w=[B_stat[j]], track=(hh == 1))
            S.op("dve", lambda e: e.bn_aggr(out=mv[:, j, :], in_=stat[:, 12 * j:12 * j + 12]), r=[B_stat[j]], w=[B_mv[j]])
            S.op("dve", lambda e: e.tensor_scalar(out=rstd[:, j:j + 1], in0=mv[:, j, 1:2], scalar1=LN_EPS, scalar2=None,
                                                  op0=ALU.add), r=[B_mv[j]], w=[B_rstd[j]])

        def f1():
            S.op("pool", lambda e: e.tensor_tensor(out=rstd[:, j:j + 1], in0=rstd[:, j:j + 1], in1=nhalf, op=ALU.pow),
                 r=[B_rstd[j], B_c2], w=[B_rstd[j]])

        def f2():
            S.op("dve", lambda e: e.scalar_tensor_tensor(
                out=xt, in0=xt, scalar=mv[:, j, 0:1], in1=lnp0[:, 0, :], op0=ALU.subtract, op1=ALU.mult),
                r=[Bx, B_mv[j], B_lnp0], w=[Bx])
            S.op("dve", lambda e: e.scalar_tensor_tensor(
                out=xt, in0=xt, scalar=rstd[:, j:j + 1], in1=lnp0[:, 1, :], op0=ALU.mult, op1=ALU.add),
                r=[Bx, B_rstd[j], B_lnp0], w=[Bx])

        def f3():
            if offload:
                S.op("dve", lambda e: e.tensor_copy(out=hb, in_=xt), r=[Bx], w=HB)
            else:
                S.op("act", lambda e: e.activation(out=hb, in_=xt, func=AF.Copy), r=[Bx], w=HB)

        def f4():
            for kc in range(8):
                S.op("pe", lambda e, kc=kc: e.transpose(out=pbb[:, kc * 128:(kc + 1) * 128],
                                                        in_=hb[:, kc * 128:(kc + 1) * 128], identity=ident),
                     r=HB + [B_const], w=pq, track=(kc == 7))

        def f5():
            if offload:
                S.op("dve", lambda e: e.tensor_copy(
                    out=h0T_[par][:, :, j * 128:(j + 1) * 128], in_=pbb.rearrange("p (k t) -> p k t", k=8)),
                    r=pq, w=[B_h0T_[par][j]])
            else:
                S.op("act", lambda e: e.activation(
                    out=h0T_[par][:, :, j * 128:(j + 1) * 128], in_=pbb.rearrange("p (k t) -> p k t", k=8), func=AF.Copy),
                    r=pq, w=[B_h0T_[par][j]])
        return [f0, f1, f2, f3, f4, f5]

    def ln_in_and_transpose(c, slot, trb):
        for j in range(2):
            ln_in_tile(slot, j)
            transpose_tile(slot, j, c % 2, trb[j])

    class BG:
        def __init__(self):
            self.q = []

        def push(self, stages, stride=1, offset=0):
            for k_, f_ in enumerate(stages):
                pos = offset + k_ * stride
                while len(self.q) <= pos:
                    self.q.append([])
                self.q[pos].append(f_)

        def push_at(self, pos, f_):
            while len(self.q) <= pos:
                self.q.append([])
            self.q[pos].append(f_)

        def tick(self):
            if self.q:
                for f_ in self.q.pop(0):
                    f_()

        def flush(self):
            while self.q:
                self.tick()
    bg = BG()

    def mm_group(out_ap, pq, pairs, r):
        n = len(pairs)
        for i, (l, rr) in enumerate(pairs):
            S.op("pe", lambda e, l=l, rr=rr, i=i: e.matmul(out_ap, lhsT=l, rhs=rr, start=(i == 0), stop=(i == n - 1)),
                 r=r, w=pq, track=(i == n - 1))

    def update_rope_tab(c):
        for tab, k in ((costab, 0), (sintab, 1)):
            S.op("pool", lambda e, tab=tab, k=k: e.tensor_copy(
                out=tab[0:64, :].rearrange("p (r c) -> p r c", r=4),
                in_=roperc[0:64, k, c * 4:c * 4 + 4].unsqueeze(2).broadcast_to([64, 4, 64])),
                r=[B_const], w=[B_tab])

    def rope_stages(st, u_ps, u_pq, bias_ap, g_ap, dst_ap, B_dst, ss_slot, rq_slot):
        tA, tB, tC = rA[st], rB[st], dst_ap
        BA, BB, BC = B_rA[st], B_rB[st], B_dst
        ss_ap, ss_pq = ss_slot
        rq_ap, rq_pq = rq_slot

        def t1():
            S.op("act", lambda e: e.activation(out=tA, in_=u_ps, func=AF.Identity, bias=bias_ap, scale=1.0),
                 r=u_pq + CONST, w=[BA])
            S.op("act", lambda e: e.activation(out=tC, in_=u_ps, func=AF.Square, bias=bias_ap, scale=1.0),
                 r=u_pq + CONST, w=[BC])

        def t2():
            mm_group(ss_ap, ss_pq, [(ones, tC)], r=[BC, B_const])

        def t3():
            S.op("act", lambda e: e.activation(out=tB, in_=ss_ap, func=AF.Ln, bias=epsrms, scale=1.0 / 128.0),
                 r=ss_pq + [B_c2], w=[BB])
            S.op("act", lambda e: e.activation(out=tB, in_=tB, func=AF.Exp, scale=-0.5), r=[BB], w=[BB])

        def t4():
            S.op("dve", lambda e: e.scalar_tensor_tensor(out=tA, in0=tA, scalar=g_ap, in1=tB, op0=ALU.mult, op1=ALU.mult),
                 r=[BA, BB] + CONST, w=[BA])
            S.op("dve", lambda e: e.tensor_copy(out=tC, in_=tA), r=[BA], w=[BC])

        def t5():
            mm_group(rq_ap, rq_pq, [(rt, tC)], r=[BC, B_const])

        def t6():
            S.op("dve", lambda e: e.tensor_tensor(out=tB, in0=rq_ap, in1=sintab, op=ALU.mult),
                 r=rq_pq + [B_tab, B_tabhi], w=[BB])
            S.op("dve", lambda e: e.tensor_tensor(out=tA, in0=tA, in1=costab, op=ALU.mult),
                 r=[BA, B_tab, B_tabhi], w=[BA])
            S.op("dve", lambda e: e.tensor_tensor(out=dst_ap, in0=tA, in1=tB, op=ALU.add),
                 r=[BA, BB], w=[B_dst])
        return [t1, t2, t3, t4, t5, t6]

    def rms_rope(u_ps, u_pq, bias_ap, g_ap, dst_ap, B_dst, ss_slot, rq_slot, st=0):
        for f_ in rope_stages(st, u_ps, u_pq, bias_ap, g_ap, dst_ap, B_dst, ss_slot, rq_slot):
            f_()

    def push_front_p1(c):
        slot, par = c % 2, c % 2
        s0 = front_stages(slot, 0, par, 6)
        s1 = front_stages(slot, 1, par, 7)
        bg.push(s0, stride=1, offset=0)
        for k_, f_ in enumerate(s1):
            bg.push_at([1, 2, 3, 5, 6, 7][k_], f_)

    def phase1_chunk(c):
        slot, par = c % 2, c % 2
        if c == 0:
            s0_ = front_stages(slot, 0, par, 6)
            s1_ = front_stages(slot, 1, par, 7)
            for f_ in s0_[:3] + s1_[:3]:
                f_()
            const_ops()
            for f_ in s0_[3:] + s1_[3:]:
                f_()
        if c + 1 < NCH2:
            load_x(c + 1, 1 - slot)
            push_front_p1(c + 1)
        h0T = h0T_[par]
        hr = [B_h0T_[par][0], B_h0T_[par][1], B_wkv]
        for g in range(2):
            bk = 2 + 2 * par + g
            ap, pq = phalf(bk, 0)
            mm_group(ap, pq, [(w_kv[:, kc, 384 + g * 128:384 + (g + 1) * 128], h0T[:, kc, :]) for kc in range(8)], r=hr)
            stages = rope_stages(g, ap, pq, binfm[:, 10 + g:11 + g], gqk[:, 1:2], kbT[:, g, c * CH2:(c + 1) * CH2],
                                 B_kbT[c], phalf(bk, 1), phalf(bk, 0))
            if g == 0:
                stages = stages[:5] + [lambda: update_rope_tab(c)] + stages[5:]
                pos = [1, 2, 3, 4, 6, 7, 8]
            else:
                pos = [1, 2, 3, 4, 6, 8]
            for k_, f_ in enumerate(stages):
                bg.push_at(pos[k_], f_)
            bg.tick()
        for g in range(2):
            ap, pq = phalf(g, 0)
            mm_group(ap, pq, [(w_kv[:, kc, g * 128:(g + 1) * 128], h0T[:, kc, :]) for kc in range(8)], r=hr)
            S.op("act", lambda e, ap=ap, g=g: e.activation(out=kaT[:, g, c * CH2:(c + 1) * CH2], in_=ap,
                                                           func=AF.Identity, bias=bka[:, g:g + 1], scale=1.0),
                 r=pq + CONST, w=[B_kaT[c]])
            bg.tick()
        for j in range(2):
            t = 2 * c + j
            ap, pq = pbank(j)
            for kc in range(8):
                S.op("pe", lambda e, kc=kc, j=j, ap=ap: e.matmul(ap[:, 0:128], lhsT=h0T[:, kc, j * 128:(j + 1) * 128],
                                                                 rhs=w_kv[:, kc, 256:384], start=(kc == 0), stop=(kc == 7)),
                     r=hr, w=pq, track=(kc == 7))
            bg.tick()
            for kc in range(8):
                S.op("pe", lambda e, kc=kc, j=j, ap=ap: e.matmul(ap[:, 128:384], lhsT=h0T[:, kc, j * 128:(j + 1) * 128],
                                                                 rhs=w_kv[:, kc, 640:896], start=(kc == 0), stop=(kc == 7)),
                     r=hr, w=pq, track=(kc == 7))
            bg.tick()
            S.op("dve", lambda e, t=t, ap=ap: e.tensor_tensor(
                out=va[:, t, :, 0:64], in0=ap[:, 0:128].rearrange("p (g d) -> p g d", g=2),
                in1=bvbc[:, 0:128].rearrange("p (g d) -> p g d", g=2), op=ALU.add),
                r=pq + [B_bvbc], w=[B_va[t]])
            S.op("dve", lambda e, t=t, ap=ap: e.tensor_tensor(out=vb[:, t, :], in0=ap[:, 128:384], in1=bvbc[:, 128:384],
                                                              op=ALU.add), r=pq + [B_bvbc], w=[B_vb[t]])

    issue_setup()
    for c_ in range(NCH2):
        phase1_chunk(c_)
        if c_ == 1:
            issue_w2()
    bg.flush()
    S.barrier(dsems)
    if stage == 1:
        sdbg = newsem("dbg")
        for nm, ap_, n_ in (("dbg_kbT", kbT, 2 * S_LEN), ("dbg_vb", vb, NT * 256), ("dbg_kaT", kaT, 2 * S_LEN), ("dbg_va", va, NT * 256)):
            d_ = nc.dram_tensor(nm, [128, n_], BF16, kind="ExternalOutput").ap()
            flat = ap_.rearrange("p a b -> p (a b)") if len(ap_.shape) == 3 else ap_.rearrange("p a b c -> p (a b c)")
            S.dma("sp", d_, flat, sdbg)
        S.wait_all("sp", dsems)
        S.emit()
        es.close()
        return nc
    S.op("pool", lambda e: e.memset(qbd, 0.0), w=B_qbd)
    s_bias = newsem("biasA")
    S.dma("sp", identS, idents_d, s_bias, w=[B_biasA])
    S.dma("sp", distT, distt_d, s_bias, w=[B_biasA])

    KV_ALL_B = B_kbT + B_vb
    sem_h1 = [newsem("h1st%d" % i) for i in range(2)]
    SC_B = 1.0 / float(np.sqrt(128.0))
    def s5_s6(c):
        slot = c % 2
        h0T = h0T_[c % 2]
        hr = [B_h0T_[c % 2][0], B_h0T_[c % 2][1], B_wqg]
        tmp0, tmp1 = s5a, s5b
        Bt0, Bt1 = [B_aden, B_arec], [B_s5b]
        for fo in range(8):
            p_ = fo % 2
            pa, pa_q = phalf(2 * p_, 0)
            pb_, pb_q = phalf(2 * p_, 1)
            ga, ga_q = phalf(2 * p_ + 1, 0)
            gb, gb_q = phalf(2 * p_ + 1, 1)
            fs = slice(fo * 128, (fo + 1) * 128)
            mm_group(ga, ga_q, [(w_qg[:, kc, 1024 + fo * 128:1024 + (fo + 1) * 128], h0T[:, kc, :]) for kc in range(8)], r=hr)
            bg.tick()
            mm_group(gb, gb_q, [(w_qg[:, kc, 2048 + fo * 128:2048 + (fo + 1) * 128], h0T[:, kc, :]) for kc in range(8)], r=hr)
            bg.tick()
            mm_group(pa, pa_q, [(w_oa[:, k, fs], oTa[:, k, :]) for k in range(4)], r=B_oTa + [B_woa])
            bg.tick()
            mm_group(pb_, pb_q, [(w_ob[:, k, fs], oTb[:, k, :]) for k in range(4)], r=B_oTb + [B_wob])
            bg.tick()
            S.op("act", lambda e, ga=ga, fo=fo: e.activation(out=tmp0, in_=ga, func=AF.Tanh,
                                                             bias=binh[:, 14 + fo:15 + fo], scale=0.5),
                 r=ga_q + CONST, w=Bt0)
            S.op("act", lambda e, gb=gb, fo=fo: e.activation(out=tmp1, in_=gb, func=AF.Tanh,
                                                             bias=binh[:, 22 + fo:23 + fo], scale=0.5),
                 r=gb_q + CONST, w=Bt1)
            S.op("dve", lambda e, pa=pa: e.scalar_tensor_tensor(out=tmp0, in0=tmp0, scalar=1.0, in1=pa,
                                                                op0=ALU.add, op1=ALU.mult),
                 r=pa_q + Bt0, w=Bt0)
            S.op("dve", lambda e, pb_=pb_: e.scalar_tensor_tensor(out=tmp1, in0=tmp1, scalar=1.0, in1=pb_,
                                                                  op0=ALU.add, op1=ALU.mult),
                 r=pb_q + Bt1, w=Bt1)
            S.op("dve", lambda e, fo=fo: e.tensor_tensor(out=mT[:, fo, :], in0=tmp0, in1=tmp1, op=ALU.add),
                 r=Bt0 + Bt1, w=[B_mT[fo]])
        for j in range(2):
            xt = xh[slot][:, j, :]
            Bx = B_xh[slot][j]
            for nh in range(2):
                yp, yq = pbank((2 * j + nh) % 4)
                for kc in range(8):
                    S.op("pe", lambda e, kc=kc, yp=yp, j=j, nh=nh: e.matmul(
                        yp, lhsT=mT[:, kc, j * 128:(j + 1) * 128], rhs=w_out[:, kc, nh * 512:(nh + 1) * 512],
                        start=(kc == 0), stop=(kc == 7)), r=[B_mT[kc], B_wout], w=yq, track=(kc == 7))
                bg.tick()
                S.op("dve", lambda e, xt=xt, yp=yp, nh=nh: e.scalar_tensor_tensor(
                    out=xt[:, nh * 512:(nh + 1) * 512], in0=xt[:, nh * 512:(nh + 1) * 512], scalar=ALPHA, in1=yp,
                    op0=ALU.mult, op1=ALU.add), r=yq + [Bx], w=[Bx])
            ln_stats(xt, j, Bx)
            S.op("dve", lambda e, xt=xt, j=j: e.tensor_scalar(out=xt, in0=xt, scalar1=mv[:, j, 0:1],
                                                              scalar2=rstd[:, j:j + 1], op0=ALU.subtract, op1=ALU.mult),
                 r=[Bx, B_mv[j], B_rstd[j]], w=[Bx])
        dst = h1_d[c * CH2:(c + 1) * CH2, :].rearrange("(j p) d -> p j d", p=128)
        S.dma("sp", dst, xh[slot], sem_xh[slot], r=B_xh[slot])

    def push_rope(c):
        par = c % 2
        h0T = h0T_[par]
        hr = [B_h0T_[par][0], B_h0T_[par][1], B_wqg]
        bg.push_at(0, lambda: update_rope_tab(c))
        for h in range(4):
            u_ps, u_pq = phalf(4 + h, 0)

            def t0(h=h, u_ps=u_ps, u_pq=u_pq):
                mm_group(u_ps, u_pq, [(w_qg[:, kc, 512 + h * 128:512 + (h + 1) * 128], h0T[:, kc, :]) for kc in range(8)], r=hr)
            stages = [t0] + rope_stages(h % 2, u_ps, u_pq, binfm[:, 6 + h:7 + h], gqk[:, 0:1], qT[:, h, :],
                                        B_qT[h], phalf(4 + h, 1), phalf(4 + h, 0))
            base = 1 + (h % 2) + 17 * (h // 2)
            for k_, f_ in enumerate(stages):
                bg.push_at(base + [0, 2, 4, 6, 8, 12, 14][k_], f_)

    def push_front(c):
        slot, par = c % 2, c % 2
        s0 = front_stages(slot, 0, par, 6, offload=True)
        s1 = front_stages(slot, 1, par, 7, offload=True)
        bg.push(s0, stride=4, offset=2)
        for k_, f_ in enumerate(s1):
            bg.push_at([4, 8, 12, 20, 24, 28][k_], f_)

    def phase2_chunk(c, first, last):
        slot = c % 2
        par = c % 2
        h0T = h0T_[par]
        if first:
            push_front(c)
            bg.flush()
            push_rope(c)
            bg.flush()
        if not last:
            load_x(c + 1, 1 - slot)
        hr = [B_h0T_[par][0], B_h0T_[par][1], B_wqg]
        for fo in range(4):
            ap, pq = phalf(fo, 0)
            mm_group(ap, pq, [(w_qg[:, kc, fo * 128:(fo + 1) * 128], h0T[:, kc, :]) for kc in range(8)], r=hr)
            bg.tick()
            S.op("act", lambda e, ap=ap, fo=fo: e.activation(
                out=qbd[0:64, fo, :, 0:128], in_=ap[0:64, :].rearrange("p (j t) -> p j t", j=2), func=AF.Identity,
                bias=binfm[0:64, fo:fo + 1], scale=1.0), r=pq + CONST, w=[B_qbd[fo]])
            S.op("act", lambda e, ap=ap, fo=fo: e.activation(
                out=qbd[64:128, fo, :, 128:256], in_=ap[64:128, :].rearrange("p (j t) -> p j t", j=2), func=AF.Identity,
                bias=binfm[64:128, fo:fo + 1], scale=1.0), r=pq + CONST, w=[B_qbd[fo]])
        bg.flush()
        units = []
        for j in range(2):
            i = 2 * c + j
            rels = [r_ for r_ in range(3) if 0 <= i + r_ - 1 < NT]
            for cc in range(4):
                units.append((j, i, rels, cc))
        NU = len(units)

        def a_banks(u):
            bx, bxq = pbank(4 + 2 * (u % 2))
            by, byq = pbank(5 + 2 * (u % 2))
            return bx, bxq, by, byq

        def a_qk(u):
            j, i, rels, cc = units[u]
            g = cc // 2
            bx, bxq, by, byq = a_banks(u)
            kdeps = [B_kaT[(i + r_ - 1) // 2] for r_ in rels]
            for r_ in rels:
                kb = i + r_ - 1
                if r_ < 2:
                    o_ap, oq = bx[:, r_ * 256:(r_ + 1) * 256], bxq
                else:
                    o_ap, oq = by[:, 0:256], byq
                S.op("pe", lambda e, o_ap=o_ap, kb=kb: e.matmul(
                    o_ap, lhsT=kaT[:, g, kb * 128:(kb + 1) * 128], rhs=qbd[:, cc, j, :],
                    start=True, stop=False), r=kdeps + [B_qbd[cc]], w=oq, track=False)
                for hh in range(2):
                    S.op("pe", lambda e, o_ap=o_ap, r_=r_, hh=hh: e.matmul(
                        o_ap[:, hh * 128:(hh + 1) * 128], lhsT=identS[:, 2 * cc + hh, :], rhs=distT[:, r_, :],
                        start=False, stop=(hh == 1)), r=[B_biasA], w=oq, track=(hh == 1))

        def a_exp(u):
            j, i, rels, cc = units[u]
            bx, bxq, by, byq = a_banks(u)
            pt = PTT[:, (u % 2) * 1024:(u % 2) * 1024 + 768]
            bp = [B_PTU[2 * (u % 2)], B_PTU[2 * (u % 2) + 1]]
            rx = [r_ for r_ in rels if r_ < 2]
            lo, hi = rx[0] * 256, (rx[-1] + 1) * 256
            S.op("act", lambda e: e.activation(out=pt[:, lo:hi], in_=bx[:, lo:hi], func=AF.Exp, scale=0.125),
                 r=bxq, w=bp)
            if 2 in rels:
                S.op("act", lambda e: e.activation(out=pt[:, 512:768], in_=by[:, 0:256], func=AF.Exp, scale=0.125),
                     r=byq, w=bp)

        def a_pv(u):
            j, i, rels, cc = units[u]
            g = cc // 2
            pt = PTT[:, (u % 2) * 1024:(u % 2) * 1024 + 768]
            bp = [B_PTU[2 * (u % 2)], B_PTU[2 * (u % 2) + 1]]
            ob, opq_all = pbank(u % 4)
            o_ap = ob[:, 0:256]
            opq = opq_all[0:2]
            vdeps = [B_va[i + r_ - 1] for r_ in rels] + [B_vaones]
            nr = len(rels)
            for n_, r_ in enumerate(rels):
                kb = i + r_ - 1
                S.op("pe", lambda e, kb=kb, r_=r_, n_=n_: e.matmul(
                    o_ap, lhsT=va[:, kb, g, :], rhs=pt[:, r_ * 256:(r_ + 1) * 256],
                    start=(n_ == 0), stop=False),
                    r=vdeps + bp, w=opq, track=False)
            for hh in range(2):
                for part, es in enumerate((esink_hi, esink_lo)):
                    lastm = (hh == 1 and part == 1)
                    S.op("pe", lambda e, hh=hh, es=es, lastm=lastm: e.matmul(
                        o_ap[:, hh * 128:(hh + 1) * 128], lhsT=sinkL[0:1, :],
                        rhs=es[0:1, 2 * cc + hh:2 * cc + hh + 1].broadcast_to([1, 128]),
                        start=False, stop=lastm), r=[B_eshl], w=opq, track=lastm)

        def a_norm(u):
            j, i, rels, cc = units[u]
            ob, opq_all = pbank(u % 4)
            o_ap = ob[:, 0:256]
            opq = opq_all[0:2]
            rec0 = s5b[0:64, :]
            tln = s5a[64:128, :]
            S.op("act", lambda e: e.activation(out=tln, in_=o_ap[64:128, :], func=AF.Ln), r=opq, w=[B_aden, B_arec])
            S.op("act", lambda e: e.activation(out=tln, in_=tln, func=AF.Exp, scale=-1.0), r=[B_aden, B_arec], w=[B_aden, B_arec])
            S.op("dve", lambda e: e.tensor_copy(out=rec0, in_=tln), r=[B_aden, B_arec], w=[B_s5b])
            S.op("dve", lambda e: e.scalar_tensor_tensor(
                out=oTa[0:64, cc, j * 128:(j + 1) * 128], in0=o_ap[0:64, 0:128], scalar=0.5, in1=rec0[:, 0:128],
                op0=ALU.mult, op1=ALU.mult), r=opq + [B_s5b], w=[B_oTa[cc]])
            S.op("dve", lambda e: e.scalar_tensor_tensor(
                out=oTa[64:128, cc, j * 128:(j + 1) * 128], in0=o_ap[0:64, 128:256], scalar=0.5, in1=rec0[:, 128:256],
                op0=ALU.mult, op1=ALU.mult), r=opq + [B_s5b], w=[B_oTa[cc]])

        a_qk(0)
        for u in range(NU):
            a_exp(u)
            if u + 1 < NU:
                a_qk(u + 1)
            a_pv(u)
            if u >= 1:
                a_norm(u - 1)
        a_norm(NU - 1)
        if not last:
            push_front(c + 1)
        for g in range(2):
            o_ap, o_pq = pbank(4 + 2 * g)
            s_ap, s_pq = pbank(5 + 2 * g)
            qrhs = qT[:, 2 * g:2 * g + 2, :]

            def qk(kt, g=g, qrhs=qrhs):
                ap, pq = pbank(kt % 4)
                S.op("pe", lambda e: e.matmul(ap, lhsT=kbT[:, g, kt * 128:(kt + 1) * 128], rhs=qrhs,
                                              start=True, stop=True),
                     r=[B_kbT[kt // 2], B_qT[2 * g], B_qT[2 * g + 1]], w=pq)

            def expo(kt):
                ap, pq = pbank(kt % 4)
                S.op("act", lambda e: e.activation(out=PTU[kt % 4], in_=ap, func=AF.Exp, scale=SC_B),
                     r=pq, w=[B_PTU[kt % 4]])

            def pv(kt, g=g, o_ap=o_ap, s_ap=s_ap, o_pq=o_pq, s_pq=s_pq):
                S.op("pe", lambda e: e.matmul(o_ap, lhsT=vb[:, kt, g * 128:(g + 1) * 128], rhs=PTU[kt % 4],
                                              start=(kt == 0), stop=(kt == NT - 1)),
                     r=[B_vb[kt], B_PTU[kt % 4]], w=o_pq, track=False)
                S.op("pe", lambda e: e.matmul(s_ap, lhsT=ones, rhs=PTU[kt % 4],
                                              start=(kt == 0), stop=(kt == NT - 1)),
                     r=[B_PTU[kt % 4], B_const], w=s_pq)
            qk(0)
            qk(1)
            for kt in range(NT):
                expo(kt)
                if kt + 2 < NT:
                    qk(kt + 2)
                pv(kt)
                bg.tick()
            rec = rAA
            S.op("act", lambda e, s_ap=s_ap: e.activation(out=rec, in_=s_ap, func=AF.Ln), r=s_pq, w=B_rA)
            S.op("act", lambda e: e.activation(out=rec, in_=rec, func=AF.Exp, scale=-1.0), r=B_rA, w=B_rA)
            S.op("dve", lambda e, o_ap=o_ap, g=g: e.scalar_tensor_tensor(
                out=oTb[:, 2 * g:2 * g + 2, :], in0=o_ap.rearrange("p (h t) -> p h t", h=2), scalar=0.5,
                in1=rec.rearrange("p (h t) -> p h t", h=2), op0=ALU.mult, op1=ALU.mult),
                r=o_pq + B_rA, w=[B_oTb[2 * g], B_oTb[2 * g + 1]])
        bg.flush()
        if not last:
            push_rope(c + 1)
        s5_s6(c)

    load_x(0, 0)
    for c_ in range(nch):
        phase2_chunk(c_, c_ == 0, c_ == nch - 1)
    bg.flush()

    S.barrier(dsems)
    if stage == 2:
        sdbg = newsem("dbg")
        d_ = nc.dram_tensor("dbg_h1", [S_LEN, D], F32, kind="ExternalOutput").ap()
        if sub >= 5:
            for i_ in range(nch):
                S.dma("sp", d_[i_ * 256:(i_ + 1) * 256, :], h1_d[i_ * 256:(i_ + 1) * 256, :], sdbg)
        for nm, ap_, n_ in (("dbg_qT", qT, 4 * CH2), ("dbg_oTa", oTa, 4 * CH2), ("dbg_oTb", oTb, 4 * CH2)):
            dd_ = nc.dram_tensor(nm, [128, n_], BF16, kind="ExternalOutput").ap()
            S.dma("sp", dd_, ap_.rearrange("p a b -> p (a b)"), sdbg)
        S.wait_all("sp", dsems)
        S.emit()
        es.close()
        return nc
    A3 = Alloc(P12_BASE, SBYTES)
    w_g = A3.get([128, 8, DFF], BF16)
    w_v = A3.get([128, 8, DFF], BF16)
    w_dn = A3.get([128, NFC, D], BF16)
    lnp3 = A3.get([128, 4, D], F32)
    hh = A3.get([128, 4, D], F32)
    h1T = A3.get([128, 8, CH3 + 2], BF16)
    halo = A3.get([128, 2, 8], F32)
    gexx = A3.get([128, 1056], F32)
    gext_ = [gexx[:, 0:CH3 + 2], gexx[:, 528:528 + CH3 + 2]]
    alt2 = gexx[:, 0:2 * CH3]
    t3aa = A3.get([128, 2 * CH3], F32)
    t3a_ = [t3aa[:, 0:CH3], t3aa[:, CH3:2 * CH3]]
    hb3_off = A3.off
    t3b_ = [A3.get([128, CH3], F32)] * 2
    hb3 = Alloc(hb3_off, A3.off).get([128, D], BF16)
    actT = A3.get([128, NFC, CH3], BF16)
    B_wg, B_wv, B_wd, B_lnp3 = Buf("wg"), Buf("wv"), Buf("wd"), Buf("lnp3")
    B_hh = [Buf("hh%d" % j) for j in range(4)]
    B_h1Th, B_halo = Buf("h1Th"), Buf("halo")
    B_gext_ = [Buf("gext0"), Buf("gext1")]
    B_gexth_ = [Buf("gexth0"), Buf("gexth1")]
    B_t3a_ = [Buf("t3a0"), Buf("t3a1")]
    B_t3b_ = [Buf("t3b0")] * 2
    B_hb3 = B_t3b_[0]
    B_actT = [Buf("actT%d" % i) for i in range(NFC)]
    wg_v = wg_d.rearrange("(k p) n -> p k n", p=128)
    wv_v = wv_d.rearrange("(k p) n -> p k n", p=128)
    WPC = [(0, 256), (256, 768), (768, 1408), (1408, 2176), (2176, 2816)]
    B_wgp = [Buf("wg%d" % i) for i in range(len(WPC))]
    B_wvp = [Buf("wv%d" % i) for i in range(len(WPC))]
    for i_, (c0, c1) in enumerate(WPC):
        S.dma("pool", w_g[:, :, c0:c1], wg_v[:, :, c0:c1], newsem("wg%d" % i_), w=[B_wgp[i_]])
        S.dma("pool", w_v[:, :, c0:c1], wv_v[:, :, c0:c1], newsem("wv%d" % i_), w=[B_wvp[i_]])

    def wpiece(fc):
        for i_, (c0, c1) in enumerate(WPC):
            if c0 <= fc * 128 < c1:
                return i_
    wd_v = wd_d.rearrange("(k p) n -> p k n", p=128)
    s_w3d = newsem("w3d")
    for k0 in range(0, NFC, 11):
        S.dma("pool", w_dn[:, k0:k0 + 11, :], wd_v[:, k0:k0 + 11, :], s_w3d, w=[B_wd])
    s_m3 = newsem("m3")
    S.dma("sp", lnp3, lnp_d[:, 2:6, :], s_m3, w=[B_lnp3])
    sem_halo = newsem("halo")
    sem_hhj = [newsem("hh%d" % j) for j in range(4)]
    B_h1Tj = [Buf("h1T%d" % j) for j in range(4)]

    sem_alt = newsem("hhalt")
    sem_alt2 = newsem("hhalt2")

    def prep_stages(c, j, alt=False):
        t0 = c * CH3
        if alt and j == 3:
            ht, Bh, sem_l = t3aa, [B_t3a_[0], B_t3a_[1]], sem_alt
        elif alt:
            ht, Bh, sem_l = alt2, [B_gext_[0], B_gext_[1], B_gexth_[0], B_gexth_[1]], sem_alt2
        else:
            ht, Bh, sem_l = hh[:, j, :], [B_hh[j]], sem_hhj[j]
        pb, pq = pbank(6 + j % 2)
        pbb = pb.bitcast(BF16)

        def p0():
            S.dma("sp", ht, h1_d[t0 + j * 128:t0 + (j + 1) * 128, :], sem_l, w=Bh)
            if j == 0:
                for side, tk in ((0, t0 - 1), (1, t0 + CH3)):
                    if 0 <= tk < S_LEN:
                        S.dma("sp", halo[:, side, :], h1_d[tk:tk + 1, :].rearrange("o (k p) -> p (o k)", p=128),
                              sem_halo, w=[B_halo], allow_slow_non_contiguous=True)

        def p1():
            S.op("dve", lambda e: e.tensor_tensor(out=ht, in0=ht, in1=lnp3[:, 0, :], op=ALU.mult),
                 r=Bh + [B_lnp3], w=Bh)

        def p2():
            S.op("dve", lambda e: e.tensor_tensor(out=ht, in0=ht, in1=lnp3[:, 1, :], op=ALU.add),
                 r=Bh + [B_lnp3], w=Bh)

        def p3():
            S.op("act", lambda e: e.activation(out=hb3, in_=ht, func=AF.Copy), r=Bh, w=[B_hb3])

        def p4():
            for kc in range(8):
                S.op("pe", lambda e, kc=kc: e.transpose(out=pbb[:, kc * 128:(kc + 1) * 128],
                                                        in_=hb3[:, kc * 128:(kc + 1) * 128], identity=ident),
                     r=[B_hb3, B_const], w=pq, track=(kc == 7))

        def p5():
            S.op("act", lambda e: e.activation(
                out=h1T[:, :, 1 + j * 128:1 + (j + 1) * 128], in_=pbb.rearrange("p (k t) -> p k t", k=8), func=AF.Copy),
                r=pq, w=[B_h1Tj[j]])
        return [p0, p1, p2, p3, p4, p5]

    def halo_finish(c):
        t0 = c * CH3
        for side, tk, col in ((0, t0 - 1, 0), (1, t0 + CH3, CH3 + 1)):
            if 0 <= tk < S_LEN:
                S.op("dve", lambda e, side=side: e.tensor_tensor(out=halo[:, side, :], in0=halo[:, side, :],
                                                                 in1=g1fm[:, 0, :], op=ALU.mult),
                     r=[B_halo, B_const], w=[B_halo])
                S.op("dve", lambda e, side=side, col=col: e.tensor_tensor(out=h1T[:, :, col], in0=halo[:, side, :],
                                                                          in1=g1fm[:, 1, :], op=ALU.add),
                     r=[B_halo, B_const], w=[B_h1Th])
            else:
                S.op("dve", lambda e, col=col: e.memset(h1T[:, :, col], 0.0), w=[B_h1Th])

    def phase3_chunk(c):
        t0 = c * CH3
        if c == 0:
            for j in range(4):
                for f_ in prep_stages(c, j):
                    f_()
        bg.flush()
        halo_finish(c)
        hr3 = B_h1Tj + [B_h1Th]
        for fc in range(NFC):
            fs = slice(fc * 128, (fc + 1) * 128)
            gext, t3a, t3b = gext_[fc % 2], t3a_[fc % 2], t3b_[fc % 2]
            B_gext, B_gexth, B_t3a, B_t3b = B_gext_[fc % 2], B_gexth_[fc % 2], B_t3a_[fc % 2], B_t3b_[fc % 2]
            gp, gq_ = pbank(fc % 2)
            vp, vq_ = pbank(2 + fc % 2)
            ghp_full, ghq_full = pbank(4 + fc % 2)
            ghp = ghp_full[:, 0:2]
            for kc in range(8):
                S.op("pe", lambda e, kc=kc, gp=gp, fs=fs: e.matmul(gp, lhsT=w_g[:, kc, fs], rhs=h1T[:, kc, 1:CH3 + 1],
                                                                   start=(kc == 0), stop=(kc == 7)),
                     r=hr3 + [B_wgp[wpiece(fc)]], w=gq_, track=(kc == 7))
                S.op("pe", lambda e, kc=kc, ghp=ghp, fs=fs: e.matmul(ghp, lhsT=w_g[:, kc, fs],
                                                                     rhs=h1T[:, kc, 0:CH3 + 2:CH3 + 1],
                                                                     start=(kc == 0), stop=(kc == 7)),
                     r=hr3 + [B_wgp[wpiece(fc)]], w=[ghq_full[0]], track=(kc == 7))
            mm_group(vp, vq_, [(w_v[:, kc, fs], h1T[:, kc, 1:CH3 + 1]) for kc in range(8)], r=hr3 + [B_wvp[wpiece(fc)]])
            S.op("act", lambda e, gp=gp, gext=gext: e.activation(out=gext[:, 1:CH3 + 1], in_=gp, func=AF.Copy), r=gq_, w=[B_gext])
            S.op("dve", lambda e, ghp=ghp, gext=gext: e.tensor_copy(out=gext[:, 0:CH3 + 2:CH3 + 1], in_=ghp),
                 r=[ghq_full[0]], w=[B_gexth])
            ge = [B_gext, B_gexth]
            S.op("dve", lambda e, fc=fc, gext=gext, t3a=t3a, t3b=t3b: e.tensor_scalar(out=t3a, in0=gext[:, 0:CH3], scalar1=cwfm[:, fc, 0:1],
                                                         scalar2=None, op0=ALU.mult), r=ge + [B_const], w=[B_t3a])
            S.op("dve", lambda e, fc=fc, gext=gext, t3a=t3a, t3b=t3b: e.scalar_tensor_tensor(out=t3a, in0=gext[:, 1:CH3 + 1], scalar=cwfm[:, fc, 1:2],
                                                                in1=t3a, op0=ALU.mult, op1=ALU.add),
                 r=ge + [B_const, B_t3a], w=[B_t3a])
            S.op("dve", lambda e, fc=fc, gext=gext, t3a=t3a, t3b=t3b: e.scalar_tensor_tensor(out=t3a, in0=gext[:, 2:CH3 + 2], scalar=cwfm[:, fc, 2:3],
                                                                in1=t3a, op0=ALU.mult, op1=ALU.add),
                 r=ge + [B_const, B_t3a], w=[B_t3a])
            S.op("act", lambda e, fc=fc, gext=gext, t3a=t3a, t3b=t3b: e.activation(out=t3b, in_=t3a, func=AF.Gelu, bias=cwfm[:, fc, 3:4], scale=1.0),
                 r=[B_t3a, B_const], w=[B_t3b])
            S.op("dve", lambda e, fc=fc, vp=vp, t3b=t3b: e.tensor_tensor(out=actT[:, fc, :], in0=t3b, in1=vp, op=ALU.mult),
                 r=vq_ + [B_t3b], w=[B_actT[fc]])
        if c + 1 < NCH3:
            bg.push(prep_stages(c + 1, 2, alt=True), stride=1, offset=0)
            bg.push(prep_stages(c + 1, 3, alt=True), stride=1, offset=1)
        for j in range(4):
            ht = hh[:, j, :]
            for nh in range(2):
                yp, yq = pbank((2 * j + nh) % 4)
                for fc in range(NFC):
                    S.op("pe", lambda e, fc=fc, yp=yp, j=j, nh=nh: e.matmul(
                        yp, lhsT=actT[:, fc, j * 128:(j + 1) * 128], rhs=w_dn[:, fc, nh * 512:(nh + 1) * 512],
                        start=(fc == 0), stop=(fc == NFC - 1)), r=[B_actT[fc], B_wd], w=yq, track=(fc == NFC - 1))
                bg.tick()
                S.op("dve", lambda e, ht=ht, yp=yp, nh=nh: e.scalar_tensor_tensor(
                    out=ht[:, nh * 512:(nh + 1) * 512], in0=ht[:, nh * 512:(nh + 1) * 512], scalar=ALPHA, in1=yp,
                    op0=ALU.mult, op1=ALU.add), r=yq + [B_hh[j]], w=[B_hh[j]])
            ln_stats(ht, j, B_hh[j])
            S.op("dve", lambda e, ht=ht, j=j: e.scalar_tensor_tensor(
                out=ht, in0=ht, scalar=mv[:, j, 0:1], in1=lnp3[:, 2, :], op0=ALU.subtract, op1=ALU.mult),
                r=[B_hh[j], B_mv[j], B_lnp3], w=[B_hh[j]])
            S.op("dve", lambda e, ht=ht, j=j: e.scalar_tensor_tensor(
                out=ht, in0=ht, scalar=rstd[:, j:j + 1], in1=lnp3[:, 3, :], op0=ALU.mult, op1=ALU.add),
                r=[B_hh[j], B_rstd[j], B_lnp3], w=[B_hh[j]])
            S.dma("sp", out_d[t0 + j * 128:t0 + (j + 1) * 128, :], ht, sem_hhj[j], r=[B_hh[j]])
            if c + 1 < NCH3:
                if j < 2:
                    bg.push(prep_stages(c + 1, j), stride=1, offset=1)
                elif j == 2:
                    bg.push_at(1, lambda: S.op("dve", lambda e: e.tensor_copy(out=hh[:, 2, :], in_=alt2),
                                               r=[B_gext_[0], B_gext_[1], B_gexth_[0], B_gexth_[1]], w=[B_hh[2]]))
                else:
                    bg.push_at(1, lambda: S.op("dve", lambda e: e.tensor_copy(out=hh[:, 3, :], in_=t3aa),
                                               r=[B_t3a_[0], B_t3a_[1]], w=[B_hh[3]]))
    for c_ in range(NCH3):
        phase3_chunk(c_)
    bg.flush()
    S.wait_all("sp", dsems)
    S.emit()
    es.close()
    return nc


def _host_consts():
    bf = ml_dtypes.bfloat16
    ident = np.eye(128, dtype=np.float32).astype(bf)
    ones = np.ones((128, 128), dtype=np.float32).astype(bf)
    R = np.zeros((128, 128), dtype=np.float32)
    for d in range(128):
        if d % 64 < 32:
            R[d, d + 32] = -1.0
        else:
            R[d, d - 32] = 1.0
    rt = np.ascontiguousarray(R.T).astype(bf)
    freqs = (np.float32(10000.0) ** (-(np.arange(32, dtype=np.float32) / np.float32(32)))).astype(np.float32)
    pos = np.arange(64, dtype=np.float32)
    ang = (pos[None, :] * freqs[:, None]).astype(np.float32)
    ang128 = np.tile(ang, (4, 1))
    rope = np.stack([np.cos(ang128), np.sin(ang128)], axis=1).astype(np.float32)
    a = np.arange(128)[:, None]
    b = np.arange(128)[None, :]
    dist_t = np.zeros((128, 3, 128), dtype=np.float32)
    for rel in range(3):
        dist = np.abs(b - a - (rel - 1) * 128)
        dist_t[:, rel, :] = np.where(dist <= 128, -dist, -30000.0)
    dist_t = dist_t.astype(bf)
    ident_s = np.zeros((128, 8, 128), dtype=np.float32)
    for h in range(8):
        ident_s[:, h, :] = np.eye(128, dtype=np.float32) * (8.0 * 2.0 ** (-(h + 1)))
    ident_s = ident_s.astype(bf)
    bias = (ident_s, dist_t)
    return ident, ones, rt, rope, bias


_NC_CACHE = {}


def make_shared(x, ln_in_g, ln_in_b, w_in, b_in, a_sinks, b_q_norm, b_k_norm, w_o_a, w_o_b, w_out,
           ln1_g, ln1_b, w_ffn_gate, w_ffn_val, ffn_conv_w, ffn_conv_b, w_ffn_down, ln2_g, ln2_b):
    f = lambda t: np.ascontiguousarray(np.asarray(t, dtype=np.float32))
    x = f(x)
    b_in0 = f(b_in)[0]
    ident, ones, rt, rope, bias = _host_consts()
    lnp = np.stack([f(ln_in_g), f(ln_in_b), f(ln1_g)[0], f(ln1_b)[0], f(ln2_g)[0], f(ln2_b)[0]], 0)
    lnp = np.ascontiguousarray(np.broadcast_to(lnp[None], (128, 6, D)))
    bin_fm = np.ascontiguousarray(b_in0.reshape(30, 128).T)
    bka = np.stack([np.tile(b_in0[C_KA:C_KA + 64], 2), np.tile(b_in0[C_KA + 64:C_KA + 128], 2)], 1)
    bv = np.concatenate([b_in0[C_VA:C_VA + 128], b_in0[C_VB:C_VB + 256]])
    bv_bc = np.ascontiguousarray(np.broadcast_to(bv[None], (128, 384)))
    ln1_fm = np.stack([f(ln1_g)[0].reshape(8, 128).T, f(ln1_b)[0].reshape(8, 128).T], 1)
    cw = np.concatenate([f(ffn_conv_w)[0], f(ffn_conv_b)], 0)
    cw_fm = np.ascontiguousarray(cw.reshape(4, NFC, 128).transpose(2, 1, 0))
    sinks_bc = np.ascontiguousarray(np.broadcast_to(f(a_sinks)[0][None], (128, 8)))
    gqk = np.stack([f(b_q_norm)[0], f(b_k_norm)[0]], 1)
    shared = {
        "w_in": f(w_in)[0], "w_o_a": f(w_o_a)[0], "w_o_b": f(w_o_b)[0], "w_out": f(w_out)[0],
        "w_g": f(w_ffn_gate)[0], "w_v": f(w_ffn_val)[0], "w_d": f(w_ffn_down)[0],
        "lnp": lnp, "bin_fm": bin_fm, "bka_dup": np.ascontiguousarray(bka), "bv_bc": bv_bc,
        "ln1_fm": np.ascontiguousarray(ln1_fm), "cw_fm": cw_fm, "sinks_bc": sinks_bc,
        "gqk_fm": np.ascontiguousarray(gqk), "ident": ident, "ones": ones, "rt": rt,
        "rope_rc": np.ascontiguousarray(rope), "ident_s": bias[0], "dist_t": bias[1],
    }
    return x, shared


def kernel(x, ln_in_g, ln_in_b, w_in, b_in, a_sinks, b_q_norm, b_k_norm, w_o_a, w_o_b, w_out,
           ln1_g, ln1_b, w_ffn_gate, w_ffn_val, ffn_conv_w, ffn_conv_b, w_ffn_down, ln2_g, ln2_b):
    x, shared = make_shared(x, ln_in_g, ln_in_b, w_in, b_in, a_sinks, b_q_norm, b_k_norm, w_o_a, w_o_b, w_out,
                            ln1_g, ln1_b, w_ffn_gate, w_ffn_val, ffn_conv_w, ffn_conv_b, w_ffn_down, ln2_g, ln2_b)
    if "nc" not in _NC_CACHE:
        _NC_CACHE["nc"] = build_nc()
    nc = _NC_CACHE["nc"]
    in_maps = []
    for b in range(8):
        m = dict(shared)
        m["x"] = x[b]
        in_maps.append(m)
    res = run_bass_kernel_spmd(nc, in_maps, core_ids=list(range(8)))
    return np.stack([np.asarray(r["out"], dtype=np.float32) for r in res.results], 0)
```

```python
import numpy as np
import ml_dtypes
from contextlib import ExitStack
import concourse.bass as bass
import concourse.mybir as mybir
from concourse.bass_utils import run_bass_kernel_spmd

F32 = mybir.dt.float32
BF16 = mybir.dt.bfloat16
AF = mybir.ActivationFunctionType
ALU = mybir.AluOpType

S_LEN = 4096
D = 1024
NT = 32
DFF = 2816
NFC = 22
ALPHA = float(2.0 ** 0.25)
LN_EPS = 1e-5
RMS_EPS = 1e-6
CH2 = 256
NCH2 = S_LEN // CH2
CH3 = 512
NCH3 = S_LEN // CH3
C_QA, C_KA, C_VA, C_QB, C_KB, C_VB, C_GA, C_GB = 0, 512, 640, 768, 1280, 1536, 1792, 2816


class Buf:
    __slots__ = ("name", "last_w", "readers", "is_bank")

    def __init__(self, name, is_bank=False):
        self.name = name
        self.last_w = None
        self.readers = {}
        self.is_bank = is_bank


class Sem:
    def __init__(self, h):
        self.h = h
        self.count = 0


class Sched:
    ENG = ("pe", "act", "dve", "pool", "sp")

    def __init__(self, nc, es):
        self.nc = nc
        self.es = es
        self.ops = {e: [] for e in self.ENG}
        self.esem = {e: Sem(es.enter_context(nc.semaphore("sem_" + e))) for e in self.ENG if e != "sp"}
        self.seen = {e: {} for e in self.ENG}
        self.nsem = 0
        self.bankof = {}

    def new_sem(self, name):
        self.nsem += 1
        return Sem(self.es.enter_context(self.nc.semaphore("d_%s_%d" % (name, self.nsem))))

    def _waits(self, eng, r, w):
        need = {}

        def add(dep, war, bank=False):
            if dep is None:
                return
            sem, val = dep
            if war and sem is self.esem.get(eng) and (eng == "pe" or bank):
                return
            if need.get(sem, 0) < val:
                need[sem] = val
        for b in r:
            add(b.last_w, False)
        for b in w:
            add(b.last_w, True, b.is_bank)
            for sem, val in b.readers.items():
                add((sem, val), True, b.is_bank)
        out = []
        seen = self.seen[eng]
        for sem, val in need.items():
            if seen.get(sem, 0) < val:
                seen[sem] = val
                out.append((sem.h, val))
        return out

    def op(self, eng, fn, r=(), w=(), track=True):
        banks = []
        for b in list(r) + list(w):
            bk = self.bankof.get(b)
            if bk is not None and bk not in banks:
                banks.append(bk)
        if banks:
            w = list(w) + banks
        waits = self._waits(eng, r, w)
        sem = self.esem[eng]
        val = sem.count + 1
        if eng != "pe":
            track = True
        if track:
            sem.count = val
        self.ops[eng].append((waits, fn, (sem.h, 1) if track else None))
        for b in w:
            b.last_w = (sem, val)
            b.readers = {}
        for b in r:
            if b.readers.get(sem, 0) < val:
                b.readers[sem] = val

    def dma(self, q, out, in_, sem, r=(), w=(), **kw):
        waits = self._waits(q, r, w)
        sem.count += 16
        val = sem.count
        self.ops[q].append((waits, lambda e, o=out, i=in_: e.dma_start(out=o, in_=i, **kw), (sem.h, 16)))
        for b in w:
            b.last_w = (sem, val)
            b.readers = {}
        for b in r:
            if b.readers.get(sem, 0) < val:
                b.readers[sem] = val

    def wait_all(self, eng, sems):
        waits = []
        for s in sems:
            if s.count > 0 and self.seen[eng].get(s, 0) < s.count:
                self.seen[eng][s] = s.count
                waits.append((s.h, s.count))
        self.ops[eng].append((waits, None, None))

    def barrier(self, dsems):
        allsems = list(self.esem.values()) + list(dsems)
        for e in self.ENG:
            self.wait_all(e, allsems)

    def emit(self):
        nc = self.nc
        block = self.es.enter_context(nc.Block())

        def run(eng_name):
            def f(e):
                for waits, fn, inc in self.ops[eng_name]:
                    for h, v in waits:
                        e.wait_ge(h, v)
                    if fn is not None:
                        ins = fn(e)
                        if inc is not None:
                            ins.then_inc(inc[0], inc[1])
            return f
        block.tensor(run("pe"))
        block.scalar(run("act"))
        block.vector(run("dve"))
        block.gpsimd(run("pool"))
        block.sync(run("sp"))


def build_nc(stage=3, sub=9, nch=NCH2, asub=9):
    nc = bass.Bass("TRN2", target_bir_lowering=False)
    es = ExitStack()
    dram = lambda n, s, dt=F32, kind="ExternalInput": nc.dram_tensor(n, list(s), dt, kind=kind).ap()
    x_d = dram("x", [S_LEN, D])
    win_d = dram("w_in", [D, 3840])
    woa_d = dram("w_o_a", [512, D])
    wob_d = dram("w_o_b", [512, D])
    wout_d = dram("w_out", [D, D])
    wg_d = dram("w_g", [D, DFF])
    wv_d = dram("w_v", [D, DFF])
    wd_d = dram("w_d", [DFF, D])
    lnp_d = dram("lnp", [128, 6, D])
    binfm_d = dram("bin_fm", [128, 30])
    bka_d = dram("bka_dup", [128, 2])
    bv_d = dram("bv_bc", [128, 384])
    g1fm_d = dram("ln1_fm", [128, 2, 8])
    cw_d = dram("cw_fm", [128, NFC, 4])
    sink_d = dram("sinks_bc", [128, 8])
    gqk_d = dram("gqk_fm", [128, 2])
    ident_d = dram("ident", [128, 128], BF16)
    ones_d = dram("ones", [128, 128], BF16)
    rt_d = dram("rt", [128, 128], BF16)
    rope_d = dram("rope_rc", [128, 2, 64])
    idents_d = dram("ident_s", [128, 8, 128], BF16)
    distt_d = dram("dist_t", [128, 3, 128], BF16)
    h1_d = dram("h1_scratch", [S_LEN, D], F32, kind="Internal")
    out_d = dram("out", [S_LEN, D], F32, kind="ExternalOutput")

    S = Sched(nc, es)
    SBYTES = 212800
    sb = es.enter_context(nc.sbuf_tensor("SB", [128, SBYTES // 2], BF16))
    psum = [es.enter_context(nc.psum_tensor("ps%d" % i, [128, 512], F32)) for i in range(8)]
    PQ = [[Buf("ps%d_%d" % (b, q)) for q in range(4)] for b in range(8)]
    for b_ in range(8):
        bk_ = Buf("bank%d" % b_, is_bank=True)
        for q_ in PQ[b_]:
            S.bankof[q_] = bk_

    def pbank(b):
        return psum[b][:, :], PQ[b]

    def phalf(b, s):
        return psum[b][:, s * 256:(s + 1) * 256], PQ[b][2 * s:2 * s + 2]

    class Alloc:
        def __init__(self, base, limit):
            self.off = base
            self.limit = limit

        def get(self, shape, dt):
            esz = 4 if dt == F32 else 2
            n = 1
            for s in shape[1:]:
                n *= s
            nb = (n * esz + 63) // 64 * 64
            assert self.off + nb <= self.limit, ("SBUF overflow", self.off, nb, self.limit)
            ap = sb[:, self.off // 2:(self.off + nb) // 2]
            if dt == F32:
                ap = ap.bitcast(F32)
            ap = ap[:, 0:n]
            if len(shape) == 3:
                ap = ap.rearrange("p (a b) -> p a b", a=shape[1])
            elif len(shape) == 4:
                ap = ap.rearrange("p (a b c) -> p a b c", a=shape[1], b=shape[2])
            self.off += nb
            return ap

    A0 = Alloc(0, SBYTES)
    ident = A0.get([128, 128], BF16)
    ones = A0.get([128, 128], BF16)
    rt = A0.get([128, 128], BF16)
    binfm = A0.get([128, 30], F32)
    binh = A0.get([128, 30], F32)
    bka = A0.get([128, 2], F32)
    g1fm = A0.get([128, 2, 8], F32)
    cwfm = A0.get([128, NFC, 4], F32)
    esink = A0.get([128, 8], F32)
    esink_hi = A0.get([128, 8], BF16)
    esink_lo = A0.get([128, 8], BF16)
    sinkL = A0.get([128, 128], BF16)
    gqk = A0.get([128, 2], F32)
    roperc = A0.get([128, 2, 64], F32)
    neghalf = A0.get([128, 3], F32)
    epsln = neghalf[:, 0:1]
    epsrms = neghalf[:, 1:2]
    nhalf = neghalf[:, 2:3]
    stat = A0.get([128, 48], F32)
    mv = A0.get([128, 4, 2], F32)
    rstd = A0.get([128, 4], F32)
    B_const = Buf("const")
    B_stat = [Buf("stat%d" % j) for j in range(4)]
    B_mv = [Buf("mv%d" % j) for j in range(4)]
    B_rstd = [Buf("rstd%d" % j) for j in range(4)]
    dsems = []

    def newsem(n):
        s = S.new_sem(n)
        dsems.append(s)
        return s
    sem_c = newsem("const")
    B_c2 = Buf("neghalf")
    B_binh = Buf("binh")
    B_esink = Buf("esink")
    B_eshl = Buf("esink_hl")
    S.op("pool", lambda e: e.memset(epsln, LN_EPS), w=[B_c2], track=False)
    S.op("pool", lambda e: e.memset(nhalf, -0.5), w=[B_c2], track=False)
    S.op("pool", lambda e: e.memset(epsrms, RMS_EPS), w=[B_c2])

    def issue_consts():
        for ap, d in ((ident, ident_d), (ones, ones_d), (rt, rt_d), (binfm, binfm_d), (bka, bka_d),
                      (g1fm, g1fm_d), (cwfm, cw_d), (esink, sink_d), (gqk, gqk_d), (roperc, rope_d)):
            S.dma("sp", ap, d, sem_c, w=[B_const])

    def const_ops():
        S.op("dve", lambda e: e.tensor_scalar(out=binh, in0=binfm, scalar1=0.5, scalar2=None, op0=ALU.mult),
             r=[B_const], w=[B_binh])
        S.op("act", lambda e: e.activation(out=esink, in_=esink, func=AF.Exp), r=[B_const], w=[B_esink])
        S.op("dve", lambda e: e.tensor_copy(out=esink_hi, in_=esink), r=[B_esink], w=[B_eshl])
        S.op("dve", lambda e: e.tensor_tensor(out=esink_lo, in0=esink, in1=esink_hi, op=ALU.subtract), r=[B_esink, B_eshl], w=[B_eshl])
        S.op("dve", lambda e: e.memset(sinkL[:, 0:64], 0.0), w=[B_eshl], track=False)
        S.op("dve", lambda e: e.memset(sinkL[:, 64:128], 1.0), w=[B_eshl])
    CONST = [B_const, B_binh, B_esink]
    P12_BASE = A0.off

    def ln_stats(xt, j, B_x):
        for hh in range(2):
            S.op("dve", lambda e, hh=hh: e.bn_stats(out=stat[:, (2 * j + hh) * 6:(2 * j + hh + 1) * 6], in_=xt[:, hh * 512:(hh + 1) * 512]),
                 r=[B_x], w=[B_stat[j]], track=(hh == 1))
        S.op("dve", lambda e: e.bn_aggr(out=mv[:, j, :], in_=stat[:, 12 * j:12 * j + 12]), r=[B_stat[j]], w=[B_mv[j]])
        S.op("dve", lambda e: e.tensor_scalar(out=rstd[:, j:j + 1], in0=mv[:, j, 1:2], scalar1=LN_EPS, scalar2=None,
                                              op0=ALU.add), r=[B_mv[j]], w=[B_rstd[j]])
        S.op("pool", lambda e: e.tensor_tensor(out=rstd[:, j:j + 1], in0=rstd[:, j:j + 1], in1=nhalf, op=ALU.pow),
             r=[B_rstd[j], B_c2], w=[B_rstd[j]])

    A = Alloc(P12_BASE, SBYTES)
    kbT = A.get([128, 2, S_LEN], BF16)
    vb = A.get([128, NT, 256], BF16)
    kaT = A.get([128, 2, S_LEN], BF16)
    va = A.get([128, NT, 2, 128], BF16)
    w_qg = A.get([128, 8, 3072], BF16)
    w_oa = A.get([128, 4, D], BF16)
    w_ob = A.get([128, 4, D], BF16)
    w_out = A.get([128, 8, D], BF16)
    lnp0 = A.get([128, 2, D], F32)
    xh = [A.get([128, 2, D], F32) for _ in range(2)]
    h0T_ = [A.get([128, 8, CH2], BF16) for _ in range(2)]
    rAA = A.get([128, 512], F32)
    rBB = A.get([128, 512], F32)
    rA = [rAA[:, 0:256], rAA[:, 256:512]]
    rB = [rBB[:, 0:256], rBB[:, 256:512]]
    s5ab = A.get([128, 512], F32)
    s5a = s5ab[:, 0:256]
    s5b = s5ab[:, 256:512]
    hb = s5ab.bitcast(BF16)
    aden = s5a[:, 0:128]
    arec = s5a[:, 128:256]
    costab = A.get([128, 256], F32)
    sintab = A.get([128, 256], F32)
    P2ONLY = A.off
    qT = A.get([128, 4, CH2], BF16)
    qbd = A.get([128, 4, 2, 256], BF16)
    PTT = A.get([128, 2048], BF16)
    PTU = [PTT[:, i * 512:(i + 1) * 512] for i in range(4)]
    oTa = A.get([128, 4, CH2], BF16)
    oTb = A.get([128, 4, CH2], BF16)
    identS = A.get([128, 8, 128], BF16)
    distT = A.get([128, 3, 128], BF16)
    mT = A.get([128, 8, CH2], BF16)
    Akv = Alloc(P2ONLY, A.off)
    print('phase12 sbuf used', A.off)
    w_kv = Akv.get([128, 8, 896], BF16)
    bvbc = Akv.get([128, 384], F32)

    B_kbT = [Buf("kbT%d" % c) for c in range(NCH2)]
    B_kaT = [Buf("kaT%d" % c) for c in range(NCH2)]
    B_vb = [Buf("vb%d" % t) for t in range(NT)]
    B_va = [Buf("va%d" % t) for t in range(NT)]
    B_vaones = Buf("vaones")
    B_wqg, B_woa, B_wob, B_wout, B_wkv = Buf("wqg"), Buf("woa"), Buf("wob"), Buf("wout"), Buf("wkv")
    B_lnp0 = Buf("lnp0")
    B_xh = [[Buf("xh%d_%d" % (i, j)) for j in range(2)] for i in range(2)]
    B_h0T_ = [[Buf("h0T%d_%d" % (p_, j)) for j in range(2)] for p_ in range(2)]
    B_qT = [Buf("qT%d" % i) for i in range(4)]
    B_qbd = [Buf("qbd%d" % i) for i in range(4)]
    B_PTU = [Buf("PTU%d" % i) for i in range(4)]
    B_oTa = [Buf("oTa%d" % i) for i in range(4)]
    B_oTb = [Buf("oTb%d" % i) for i in range(4)]
    B_mT = [Buf("mT%d" % i) for i in range(8)]
    B_rA = [Buf("rA0"), Buf("rA1")]
    B_rB = [Buf("rB0"), Buf("rB1")]
    B_s5b = Buf("s5b")
    B_aden, B_arec, B_aotmp = Buf("aden"), Buf("arec"), Buf("aotmp")
    HB = [B_aden, B_arec, B_s5b]
    B_tab = Buf("tab")
    B_tabhi = Buf("tabhi")
    B_biasA = Buf("biasA")
    B_bvbc = Buf("bvbc")

    sem_xh = [newsem("xh%d" % i) for i in range(2)]
    win_v = win_d.rearrange("(k p) n -> p k n", p=128)
    s_wkv = newsem("wkv")
    S.dma("pool", w_kv[:, :, 0:64], win_v[:, :, C_KA:C_KA + 64], s_wkv, w=[B_wkv])
    S.dma("pool", w_kv[:, :, 64:128], win_v[:, :, C_KA:C_KA + 64], s_wkv, w=[B_wkv])
    S.dma("pool", w_kv[:, :, 128:192], win_v[:, :, C_KA + 64:C_KA + 128], s_wkv, w=[B_wkv])
    S.dma("pool", w_kv[:, :, 192:256], win_v[:, :, C_KA + 64:C_KA + 128], s_wkv, w=[B_wkv])
    S.dma("pool", w_kv[:, :, 256:384], win_v[:, :, C_VA:C_VA + 128], s_wkv, w=[B_wkv])
    S.dma("pool", w_kv[:, :, 384:896], win_v[:, :, C_KB:C_KB + 512], s_wkv, w=[B_wkv])
    s_misc = newsem("misc")
    s_bv = newsem("bv")

    def issue_setup():
        load_x(0, 0)
        S.dma("sp", lnp0, lnp_d[:, 0:2, :], s_misc, w=[B_lnp0])
        issue_consts()
        S.dma("sp", bvbc, bv_d, s_bv, w=[B_bvbc])
        S.op("pool", lambda e: e.memset(va[:, :, :, 64:128], 1.0), w=[B_vaones])
        for tab, k in ((costab, 0), (sintab, 1)):
            S.op("pool", lambda e, tab=tab, k=k: e.tensor_copy(
                out=tab[64:128, :].rearrange("p (r c) -> p r c", r=4),
                in_=roperc[64:128, k, :].unsqueeze(1).broadcast_to([64, 4, 64])), r=[B_const], w=[B_tabhi])

    def issue_w2():
        s_w2 = newsem("w2")
        for (dst, c0, n) in ((0, C_QA, 512), (512, C_QB, 512), (1024, C_GA, 1024), (2048, C_GB, 1024)):
            S.dma("pool", w_qg[:, :, dst:dst + n], win_v[:, :, c0:c0 + n], s_w2, w=[B_wqg])
        s_woa, s_wob, s_wout = newsem("woa"), newsem("wob"), newsem("wout")
        S.dma("pool", w_oa, woa_d.rearrange("(k p) n -> p k n", p=128), s_woa, w=[B_woa])
        S.dma("pool", w_ob, wob_d.rearrange("(k p) n -> p k n", p=128), s_wob, w=[B_wob])
        S.dma("pool", w_out, wout_d.rearrange("(k p) n -> p k n", p=128), s_wout, w=[B_wout])

    def load_x(c, slot):
        src = x_d[c * CH2:(c + 1) * CH2, :].rearrange("(j p) d -> p j d", p=128)
        S.dma("sp", xh[slot], src, sem_xh[slot], w=B_xh[slot])

    def ln_in_tile(slot, j):
        xt = xh[slot][:, j, :]
        Bx = B_xh[slot][j]
        ln_stats(xt, j, Bx)
        S.op("dve", lambda e: e.scalar_tensor_tensor(
            out=xt, in0=xt, scalar=mv[:, j, 0:1], in1=lnp0[:, 0, :], op0=ALU.subtract, op1=ALU.mult),
            r=[Bx, B_mv[j], B_lnp0], w=[Bx])
        S.op("dve", lambda e: e.scalar_tensor_tensor(
            out=xt, in0=xt, scalar=rstd[:, j:j + 1], in1=lnp0[:, 1, :], op0=ALU.mult, op1=ALU.add),
            r=[Bx, B_rstd[j], B_lnp0], w=[Bx])

    def transpose_tile(slot, j, par, bank):
        xt = xh[slot][:, j, :]
        Bx = B_xh[slot][j]
        S.op("act", lambda e: e.activation(out=hb, in_=xt, func=AF.Copy), r=[Bx], w=HB)
        pb, pq = pbank(bank)
        pbb = pb.bitcast(BF16)
        for kc in range(8):
            S.op("pe", lambda e, kc=kc: e.transpose(out=pbb[:, kc * 128:(kc + 1) * 128],
                                                    in_=hb[:, kc * 128:(kc + 1) * 128], identity=ident),
                 r=HB + [B_const], w=pq, track=(kc == 7))
        S.op("act", lambda e: e.activation(
            out=h0T_[par][:, :, j * 128:(j + 1) * 128], in_=pbb.rearrange("p (k t) -> p k t", k=8), func=AF.Copy),
            r=pq, w=[B_h0T_[par][j]])

    def front_stages(slot, j, par, bank, offload=False):
        xt = xh[slot][:, j, :]
        Bx = B_xh[slot][j]
        pb, pq = pbank(bank)
        pbb = pb.bitcast(BF16)

        def f0():
            for hh in range(2):
                S.op("dve", lambda e, hh=hh: e.bn_stats(out=stat[:, (2 * j + hh) * 6:(2 * j + hh + 1) * 6],
                                                        in_=xt[:, hh * 512:(hh + 1) * 512]),
                     r=[Bx], w=[B_stat[j]], track=(hh == 1))
            S.op("dve", lambda e: e.bn_aggr(out=mv[:, j, :], in_=stat[:, 12 * j:12 * j + 12]), r=[B_stat[j]], w=[B_mv[j]])
            S.op("dve", lambda e: e.tensor_scalar(out=rstd[:, j:j + 1], in0=mv[:, j, 1:2], scalar1=LN_EPS, scalar2=None,
                                                  op0=ALU.add), r=[B_mv[j]], w=[B_rstd[j]])

        def f1():
            S.op("pool", lambda e: e.tensor_tensor(out=rstd[:, j:j + 1], in0=rstd[:, j:j + 1], in1=nhalf, op=ALU.pow),
                 r=[B_rstd[j], B_c2], w=[B_rstd[j]])

        def f2():
            S.op("dve", lambda e: e.scalar_tensor_tensor(
                out=xt, in0=xt, scalar=mv[:, j, 0:1], in1=lnp0[:, 0, :], op0=ALU.subtract, op1=ALU.mult),
                r=[Bx, B_mv[j], B_lnp0], w=[Bx])
            S.op("dve", lambda e: e.scalar_tensor_tensor(
                out=xt, in0=xt, scalar=rstd[:, j:j + 1], in1=lnp0[:, 1, :], op0=ALU.mult, op1=ALU.add),
                r=[Bx, B_rstd[j], B_lnp0], w=[Bx])

        def f3():
            if offload:
                S.op("dve", lambda e: e.tensor_copy(out=hb, in_=xt), r=[Bx], w=HB)
            else:
                S.op("act", lambda e: e.activation(out=hb, in_=xt, func=AF.Copy), r=[Bx], w=HB)

        def f4():
            for kc in range(8):
                S.op("pe", lambda e, kc=kc: e.transpose(out=pbb[:, kc * 128:(kc + 1) * 128],
                                                        in_=hb[:, kc * 128:(kc + 1) * 128], identity=ident),
                     r=HB + [B_const], w=pq, track=(kc == 7))

        def f5():
            if offload:
                S.op("dve", lambda e: e.tensor_copy(
                    out=h0T_[par][:, :, j * 128:(j + 1) * 128], in_=pbb.rearrange("p (k t) -> p k t", k=8)),
                    r=pq, w=[B_h0T_[par][j]])
            else:
                S.op("act", lambda e: e.activation(
                    out=h0T_[par][:, :, j * 128:(j + 1) * 128], in_=pbb.rearrange("p (k t) -> p k t", k=8), func=AF.Copy),
                    r=pq, w=[B_h0T_[par][j]])
        return [f0, f1, f2, f3, f4, f5]

    def ln_in_and_transpose(c, slot, trb):
        for j in range(2):
            ln_in_tile(slot, j)
            transpose_tile(slot, j, c % 2, trb[j])

    class BG:
        def __init__(self):
            self.q = []

        def push(self, stages, stride=1, offset=0):
            for k_, f_ in enumerate(stages):
                pos = offset + k_ * stride
                while len(self.q) <= pos:
                    self.q.append([])
                self.q[pos].append(f_)

        def push_at(self, pos, f_):
            while len(self.q) <= pos:
                self.q.append([])
            self.q[pos].append(f_)

        def tick(self):
            if self.q:
                for f_ in self.q.pop(0):
                    f_()

        def flush(self):
            while self.q:
                self.tick()
    bg = BG()

    def mm_group(out_ap, pq, pairs, r):
        n = len(pairs)
        for i, (l, rr) in enumerate(pairs):
            S.op("pe", lambda e, l=l, rr=rr, i=i: e.matmul(out_ap, lhsT=l, rhs=rr, start=(i == 0), stop=(i == n - 1)),
                 r=r, w=pq, track=(i == n - 1))

    def update_rope_tab(c):
        for tab, k in ((costab, 0), (sintab, 1)):
            S.op("pool", lambda e, tab=tab, k=k: e.tensor_copy(
                out=tab[0:64, :].rearrange("p (r c) -> p r c", r=4),
                in_=roperc[0:64, k, c * 4:c * 4 + 4].unsqueeze(2).broadcast_to([64, 4, 64])),
                r=[B_const], w=[B_tab])

    def rope_stages(st, u_ps, u_pq, bias_ap, g_ap, dst_ap, B_dst, ss_slot, rq_slot):
        tA, tB, tC = rA[st], rB[st], dst_ap
        BA, BB, BC = B_rA[st], B_rB[st], B_dst
        ss_ap, ss_pq = ss_slot
        rq_ap, rq_pq = rq_slot

        def t1():
            S.op("act", lambda e: e.activation(out=tA, in_=u_ps, func=AF.Identity, bias=bias_ap, scale=1.0),
                 r=u_pq + CONST, w=[BA])
            S.op("act", lambda e: e.activation(out=tC, in_=u_ps, func=AF.Square, bias=bias_ap, scale=1.0),
                 r=u_pq + CONST, w=[BC])

        def t2():
            mm_group(ss_ap, ss_pq, [(ones, tC)], r=[BC, B_const])

        def t3():
            S.op("act", lambda e: e.activation(out=tB, in_=ss_ap, func=AF.Ln, bias=epsrms, scale=1.0 / 128.0),
                 r=ss_pq + [B_c2], w=[BB])
            S.op("act", lambda e: e.activation(out=tB, in_=tB, func=AF.Exp, scale=-0.5), r=[BB], w=[BB])

        def t4():
            S.op("dve", lambda e: e.scalar_tensor_tensor(out=tA, in0=tA, scalar=g_ap, in1=tB, op0=ALU.mult, op1=ALU.mult),
                 r=[BA, BB] + CONST, w=[BA])
            S.op("dve", lambda e: e.tensor_copy(out=tC, in_=tA), r=[BA], w=[BC])

        def t5():
            mm_group(rq_ap, rq_pq, [(rt, tC)], r=[BC, B_const])

        def t6():
            S.op("dve", lambda e: e.tensor_tensor(out=tB, in0=rq_ap, in1=sintab, op=ALU.mult),
                 r=rq_pq + [B_tab, B_tabhi], w=[BB])
            S.op("dve", lambda e: e.tensor_tensor(out=tA, in0=tA, in1=costab, op=ALU.mult),
                 r=[BA, B_tab, B_tabhi], w=[BA])
            S.op("dve", lambda e: e.tensor_tensor(out=dst_ap, in0=tA, in1=tB, op=ALU.add),
                 r=[BA, BB], w=[B_dst])
        return [t1, t2, t3, t4, t5, t6]

    def rms_rope(u_ps, u_pq, bias_ap, g_ap, dst_ap, B_dst, ss_slot, rq_slot, st=0):
        for f_ in rope_stages(st, u_ps, u_pq, bias_ap, g_ap, dst_ap, B_dst, ss_slot, rq_slot):
            f_()

    def push_front_p1(c):
        slot, par = c % 2, c % 2
        s0 = front_stages(slot, 0, par, 6)
        s1 = front_stages(slot, 1, par, 7)
        bg.push(s0, stride=1, offset=0)
        for k_, f_ in enumerate(s1):
            bg.push_at([1, 2, 3, 5, 6, 7][k_], f_)

    def phase1_chunk(c):
        slot, par = c % 2, c % 2
        if c == 0:
            s0_ = front_stages(slot, 0, par, 6)
            s1_ = front_stages(slot, 1, par, 7)
            for f_ in s0_[:3] + s1_[:3]:
                f_()
            const_ops()
            for f_ in s0_[3:] + s1_[3:]:
                f_()
        if c + 1 < NCH2:
            load_x(c + 1, 1 - slot)
            push_front_p1(c + 1)
        h0T = h0T_[par]
        hr = [B_h0T_[par][0], B_h0T_[par][1], B_wkv]
        for g in range(2):
            bk = 2 + 2 * par + g
            ap, pq = phalf(bk, 0)
            mm_group(ap, pq, [(w_kv[:, kc, 384 + g * 128:384 + (g + 1) * 128], h0T[:, kc, :]) for kc in range(8)], r=hr)
            stages = rope_stages(g, ap, pq, binfm[:, 10 + g:11 + g], gqk[:, 1:2], kbT[:, g, c * CH2:(c + 1) * CH2],
                                 B_kbT[c], phalf(bk, 1), phalf(bk, 0))
            if g == 0:
                stages = stages[:5] + [lambda: update_rope_tab(c)] + stages[5:]
                pos = [1, 2, 3, 4, 6, 7, 8]
            else:
                pos = [1, 2, 3, 4, 6, 8]
            for k_, f_ in enumerate(stages):
                bg.push_at(pos[k_], f_)
            bg.tick()
        for g in range(2):
            ap, pq = phalf(g, 0)
            mm_group(ap, pq, [(w_kv[:, kc, g * 128:(g + 1) * 128], h0T[:, kc, :]) for kc in range(8)], r=hr)
            S.op("act", lambda e, ap=ap, g=g: e.activation(out=kaT[:, g, c * CH2:(c + 1) * CH2], in_=ap,
                                                           func=AF.Identity, bias=bka[:, g:g + 1], scale=1.0),
                 r=pq + CONST, w=[B_kaT[c]])
            bg.tick()
        for j in range(2):
            t = 2 * c + j
            ap, pq = pbank(j)
            for kc in range(8):
                S.op("pe", lambda e, kc=kc, j=j, ap=ap: e.matmul(ap[:, 0:128], lhsT=h0T[:, kc, j * 128:(j + 1) * 128],
                                                                 rhs=w_kv[:, kc, 256:384], start=(kc == 0), stop=(kc == 7)),
                     r=hr, w=pq, track=(kc == 7))
            bg.tick()
            for kc in range(8):
                S.op("pe", lambda e, kc=kc, j=j, ap=ap: e.matmul(ap[:, 128:384], lhsT=h0T[:, kc, j * 128:(j + 1) * 128],
                                                                 rhs=w_kv[:, kc, 640:896], start=(kc == 0), stop=(kc == 7)),
                     r=hr, w=pq, track=(kc == 7))
            bg.tick()
            S.op("dve", lambda e, t=t, ap=ap: e.tensor_tensor(
                out=va[:, t, :, 0:64], in0=ap[:, 0:128].rearrange("p (g d) -> p g d", g=2),
                in1=bvbc[:, 0:128].rearrange("p (g d) -> p g d", g=2), op=ALU.add),
                r=pq + [B_bvbc], w=[B_va[t]])
            S.op("dve", lambda e, t=t, ap=ap: e.tensor_tensor(out=vb[:, t, :], in0=ap[:, 128:384], in1=bvbc[:, 128:384],
                                                              op=ALU.add), r=pq + [B_bvbc], w=[B_vb[t]])

    issue_setup()
    for c_ in range(NCH2):
        phase1_chunk(c_)
        if c_ == 1:
            issue_w2()
    bg.flush()
    S.barrier(dsems)
    if stage == 1:
        sdbg = newsem("dbg")
        for nm, ap_, n_ in (("dbg_kbT", kbT, 2 * S_LEN), ("dbg_vb", vb, NT * 256), ("dbg_kaT", kaT, 2 * S_LEN), ("dbg_va", va, NT * 256)):
            d_ = nc.dram_tensor(nm, [128, n_], BF16, kind="ExternalOutput").ap()
            flat = ap_.rearrange("p a b -> p (a b)") if len(ap_.shape) == 3 else ap_.rearrange("p a b c -> p (a b c)")
            S.dma("sp", d_, flat, sdbg)
        S.wait_all("sp", dsems)
        S.emit()
        es.close()
        return nc
    S.op("pool", lambda e: e.memset(qbd, 0.0), w=B_qbd)
    s_bias = newsem("biasA")
    S.dma("sp", identS, idents_d, s_bias, w=[B_biasA])
    S.dma("sp", distT, distt_d, s_bias, w=[B_biasA])

    KV_ALL_B = B_kbT + B_vb
    sem_h1 = [newsem("h1st%d" % i) for i in range(2)]
    SC_B = 1.0 / float(np.sqrt(128.0))
    def s5_s6(c):
        slot = c % 2
        h0T = h0T_[c % 2]
        hr = [B_h0T_[c % 2][0], B_h0T_[c % 2][1], B_wqg]
        tmp0, tmp1 = s5a, s5b
        Bt0, Bt1 = [B_aden, B_arec], [B_s5b]
        for fo in range(8):
            p_ = fo % 2
            pa, pa_q = phalf(2 * p_, 0)
            pb_, pb_q = phalf(2 * p_, 1)
            ga, ga_q = phalf(2 * p_ + 1, 0)
            gb, gb_q = phalf(2 * p_ + 1, 1)
            fs = slice(fo * 128, (fo + 1) * 128)
            mm_group(ga, ga_q, [(w_qg[:, kc, 1024 + fo * 128:1024 + (fo + 1) * 128], h0T[:, kc, :]) for kc in range(8)], r=hr)
            bg.tick()
            mm_group(gb, gb_q, [(w_qg[:, kc, 2048 + fo * 128:2048 + (fo + 1) * 128], h0T[:, kc, :]) for kc in range(8)], r=hr)
            bg.tick()
            mm_group(pa, pa_q, [(w_oa[:, k, fs], oTa[:, k, :]) for k in range(4)], r=B_oTa + [B_woa])
            bg.tick()
            mm_group(pb_, pb_q, [(w_ob[:, k, fs], oTb[:, k, :]) for k in range(4)], r=B_oTb + [B_wob])
            bg.tick()
            S.op("act", lambda e, ga=ga, fo=fo: e.activation(out=tmp0, in_=ga, func=AF.Tanh,
                                                             bias=binh[:, 14 + fo:15 + fo], scale=0.5),
                 r=ga_q + CONST, w=Bt0)
            S.op("act", lambda e, gb=gb, fo=fo: e.activation(out=tmp1, in_=gb, func=AF.Tanh,
                                                             bias=binh[:, 22 + fo:23 + fo], scale=0.5),
                 r=gb_q + CONST, w=Bt1)
            S.op("dve", lambda e, pa=pa: e.scalar_tensor_tensor(out=tmp0, in0=tmp0, scalar=1.0, in1=pa,
                                                                op0=ALU.add, op1=ALU.mult),
                 r=pa_q + Bt0, w=Bt0)
            S.op("dve", lambda e, pb_=pb_: e.scalar_tensor_tensor(out=tmp1, in0=tmp1, scalar=1.0, in1=pb_,
                                                                  op0=ALU.add, op1=ALU.mult),
                 r=pb_q + Bt1, w=Bt1)
            S.op("dve", lambda e, fo=fo: e.tensor_tensor(out=mT[:, fo, :], in0=tmp0, in1=tmp1, op=ALU.add),
                 r=Bt0 + Bt1, w=[B_mT[fo]])
        for j in range(2):
            xt = xh[slot][:, j, :]
            Bx = B_xh[slot][j]
            for nh in range(2):
                yp, yq = pbank((2 * j + nh) % 4)
                for kc in range(8):
                    S.op("pe", lambda e, kc=kc, yp=yp, j=j, nh=nh: e.matmul(
                        yp, lhsT=mT[:, kc, j * 128:(j + 1) * 128], rhs=w_out[:, kc, nh * 512:(nh + 1) * 512],
                        start=(kc == 0), stop=(kc == 7)), r=[B_mT[kc], B_wout], w=yq, track=(kc == 7))
                bg.tick()
                S.op("dve", lambda e, xt=xt, yp=yp, nh=nh: e.scalar_tensor_tensor(
                    out=xt[:, nh * 512:(nh + 1) * 512], in0=xt[:, nh * 512:(nh + 1) * 512], scalar=ALPHA, in1=yp,
                    op0=ALU.mult, op1=ALU.add), r=yq + [Bx], w=[Bx])
            ln_stats(xt, j, Bx)
            S.op("dve", lambda e, xt=xt, j=j: e.tensor_scalar(out=xt, in0=xt, scalar1=mv[:, j, 0:1],
                                                              scalar2=rstd[:, j:j + 1], op0=ALU.subtract, op1=ALU.mult),
                 r=[Bx, B_mv[j], B_rstd[j]], w=[Bx])
        dst = h1_d[c * CH2:(c + 1) * CH2, :].rearrange("(j p) d -> p j d", p=128)
        S.dma("sp", dst, xh[slot], sem_xh[slot], r=B_xh[slot])

    def push_rope(c):
        par = c % 2
        h0T = h0T_[par]
        hr = [B_h0T_[par][0], B_h0T_[par][1], B_wqg]
        bg.push_at(0, lambda: update_rope_tab(c))
        for h in range(4):
            u_ps, u_pq = phalf(4 + h, 0)

            def t0(h=h, u_ps=u_ps, u_pq=u_pq):
                mm_group(u_ps, u_pq, [(w_qg[:, kc, 512 + h * 128:512 + (h + 1) * 128], h0T[:, kc, :]) for kc in range(8)], r=hr)
            stages = [t0] + rope_stages(h % 2, u_ps, u_pq, binfm[:, 6 + h:7 + h], gqk[:, 0:1], qT[:, h, :],
                                        B_qT[h], phalf(4 + h, 1), phalf(4 + h, 0))
            base = 1 + (h % 2) + 17 * (h // 2)
            for k_, f_ in enumerate(stages):
                bg.push_at(base + [0, 2, 4, 6, 8, 12, 14][k_], f_)

    def push_front(c):
        slot, par = c % 2, c % 2
        s0 = front_stages(slot, 0, par, 6, offload=True)
        s1 = front_stages(slot, 1, par, 7, offload=True)
        bg.push(s0, stride=4, offset=2)
        for k_, f_ in enumerate(s1):
            bg.push_at([4, 8, 12, 20, 24, 28][k_], f_)

    def phase2_chunk(c, first, last):
        slot = c % 2
        par = c % 2
        h0T = h0T_[par]
        if first:
            push_front(c)
            bg.flush()
            push_rope(c)
            bg.flush()
        if not last:
            load_x(c + 1, 1 - slot)
        hr = [B_h0T_[par][0], B_h0T_[par][1], B_wqg]
        for fo in range(4):
            ap, pq = phalf(fo, 0)
            mm_group(ap, pq, [(w_qg[:, kc, fo * 128:(fo + 1) * 128], h0T[:, kc, :]) for kc in range(8)], r=hr)
            bg.tick()
            S.op("act", lambda e, ap=ap, fo=fo: e.activation(
                out=qbd[0:64, fo, :, 0:128], in_=ap[0:64, :].rearrange("p (j t) -> p j t", j=2), func=AF.Identity,
                bias=binfm[0:64, fo:fo + 1], scale=1.0), r=pq + CONST, w=[B_qbd[fo]])
            S.op("act", lambda e, ap=ap, fo=fo: e.activation(
                out=qbd[64:128, fo, :, 128:256], in_=ap[64:128, :].rearrange("p (j t) -> p j t", j=2), func=AF.Identity,
                bias=binfm[64:128, fo:fo + 1], scale=1.0), r=pq + CONST, w=[B_qbd[fo]])
        bg.flush()
        units = []
        for j in range(2):
            i = 2 * c + j
            rels = [r_ for r_ in range(3) if 0 <= i + r_ - 1 < NT]
            for cc in range(4):
                units.append((j, i, rels, cc))
        NU = len(units)

        def a_banks(u):
            bx, bxq = pbank(4 + 2 * (u % 2))
            by, byq = pbank(5 + 2 * (u % 2))
            return bx, bxq, by, byq

        def a_qk(u):
            j, i, rels, cc = units[u]
            g = cc // 2
            bx, bxq, by, byq = a_banks(u)
            kdeps = [B_kaT[(i + r_ - 1) // 2] for r_ in rels]
            for r_ in rels:
                kb = i + r_ - 1
                if r_ < 2:
                    o_ap, oq = bx[:, r_ * 256:(r_ + 1) * 256], bxq
                else:
                    o_ap, oq = by[:, 0:256], byq
                S.op("pe", lambda e, o_ap=o_ap, kb=kb: e.matmul(
                    o_ap, lhsT=kaT[:, g, kb * 128:(kb + 1) * 128], rhs=qbd[:, cc, j, :],
                    start=True, stop=False), r=kdeps + [B_qbd[cc]], w=oq, track=False)
                for hh in range(2):
                    S.op("pe", lambda e, o_ap=o_ap, r_=r_, hh=hh: e.matmul(
                        o_ap[:, hh * 128:(hh + 1) * 128], lhsT=identS[:, 2 * cc + hh, :], rhs=distT[:, r_, :],
                        start=False, stop=(hh == 1)), r=[B_biasA], w=oq, track=(hh == 1))

        def a_exp(u):
            j, i, rels, cc = units[u]
            bx, bxq, by, byq = a_banks(u)
            pt = PTT[:, (u % 2) * 1024:(u % 2) * 1024 + 768]
            bp = [B_PTU[2 * (u % 2)], B_PTU[2 * (u % 2) + 1]]
            rx = [r_ for r_ in rels if r_ < 2]
            lo, hi = rx[0] * 256, (rx[-1] + 1) * 256
            S.op("act", lambda e: e.activation(out=pt[:, lo:hi], in_=bx[:, lo:hi], func=AF.Exp, scale=0.125),
                 r=bxq, w=bp)
            if 2 in rels:
                S.op("act", lambda e: e.activation(out=pt[:, 512:768], in_=by[:, 0:256], func=AF.Exp, scale=0.125),
                     r=byq, w=bp)

        def a_pv(u):
            j, i, rels, cc = units[u]
            g = cc // 2
            pt = PTT[:, (u % 2) * 1024:(u % 2) * 1024 + 768]
            bp = [B_PTU[2 * (u % 2)], B_PTU[2 * (u % 2) + 1]]
            ob, opq_all = pbank(u % 4)
            o_ap = ob[:, 0:256]
            opq = opq_all[0:2]
            vdeps = [B_va[i + r_ - 1] for r_ in rels] + [B_vaones]
            nr = len(rels)
            for n_, r_ in enumerate(rels):
                kb = i + r_ - 1
                S.op("pe", lambda e, kb=kb, r_=r_, n_=n_: e.matmul(
                    o_ap, lhsT=va[:, kb, g, :], rhs=pt[:, r_ * 256:(r_ + 1) * 256],
                    start=(n_ == 0), stop=False),
                    r=vdeps + bp, w=opq, track=False)
            for hh in range(2):
                for part, es in enumerate((esink_hi, esink_lo)):
                    lastm = (hh == 1 and part == 1)
                    S.op("pe", lambda e, hh=hh, es=es, lastm=lastm: e.matmul(
                        o_ap[:, hh * 128:(hh + 1) * 128], lhsT=sinkL[0:1, :],
                        rhs=es[0:1, 2 * cc + hh:2 * cc + hh + 1].broadcast_to([1, 128]),
                        start=False, stop=lastm), r=[B_eshl], w=opq, track=lastm)

        def a_norm(u):
            j, i, rels, cc = units[u]
            ob, opq_all = pbank(u % 4)
            o_ap = ob[:, 0:256]
            opq = opq_all[0:2]
            rec0 = s5b[0:64, :]
            tln = s5a[64:128, :]
            S.op("act", lambda e: e.activation(out=tln, in_=o_ap[64:128, :], func=AF.Ln), r=opq, w=[B_aden, B_arec])
            S.op("act", lambda e: e.activation(out=tln, in_=tln, func=AF.Exp, scale=-1.0), r=[B_aden, B_arec], w=[B_aden, B_arec])
            S.op("dve", lambda e: e.tensor_copy(out=rec0, in_=tln), r=[B_aden, B_arec], w=[B_s5b])
            S.op("dve", lambda e: e.scalar_tensor_tensor(
                out=oTa[0:64, cc, j * 128:(j + 1) * 128], in0=o_ap[0:64, 0:128], scalar=0.5, in1=rec0[:, 0:128],
                op0=ALU.mult, op1=ALU.mult), r=opq + [B_s5b], w=[B_oTa[cc]])
            S.op("dve", lambda e: e.scalar_tensor_tensor(
                out=oTa[64:128, cc, j * 128:(j + 1) * 128], in0=o_ap[0:64, 128:256], scalar=0.5, in1=rec0[:, 128:256],
                op0=ALU.mult, op1=ALU.mult), r=opq + [B_s5b], w=[B_oTa[cc]])

        a_qk(0)
        for u in range(NU):
            a_exp(u)
            if u + 1 < NU:
                a_qk(u + 1)
            a_pv(u)
            if u >= 1:
                a_norm(u - 1)
        a_norm(NU - 1)
        if not last:
            push_front(c + 1)
        for g in range(2):
            o_ap, o_pq = pbank(4 + 2 * g)
            s_ap, s_pq = pbank(5 + 2 * g)
            qrhs = qT[:, 2 * g:2 * g + 2, :]

            def qk(kt, g=g, qrhs=qrhs):
                ap, pq = pbank(kt % 4)
                S.op("pe", lambda e: e.matmul(ap, lhsT=kbT[:, g, kt * 128:(kt + 1) * 128], rhs=qrhs,
                                              start=True, stop=True),
                     r=[B_kbT[kt // 2], B_qT[2 * g], B_qT[2 * g + 1]], w=pq)

            def expo(kt):
                ap, pq = pbank(kt % 4)
                S.op("act", lambda e: e.activation(out=PTU[kt % 4], in_=ap, func=AF.Exp, scale=SC_B),
                     r=pq, w=[B_PTU[kt % 4]])

            def pv(kt, g=g, o_ap=o_ap, s_ap=s_ap, o_pq=o_pq, s_pq=s_pq):
                S.op("pe", lambda e: e.matmul(o_ap, lhsT=vb[:, kt, g * 128:(g + 1) * 128], rhs=PTU[kt % 4],
                                              start=(kt == 0), stop=(kt == NT - 1)),
                     r=[B_vb[kt], B_PTU[kt % 4]], w=o_pq, track=False)
                S.op("pe", lambda e: e.matmul(s_ap, lhsT=ones, rhs=PTU[kt % 4],
                                              start=(kt == 0), stop=(kt == NT - 1)),
                     r=[B_PTU[kt % 4], B_const], w=s_pq)
            qk(0)
            qk(1)
            for kt in range(NT):
                expo(kt)
                if kt + 2 < NT:
                    qk(kt + 2)
                pv(kt)
                bg.tick()
            rec = rAA
            S.op("act", lambda e, s_ap=s_ap: e.activation(out=rec, in_=s_ap, func=AF.Ln), r=s_pq, w=B_rA)
            S.op("act", lambda e: e.activation(out=rec, in_=rec, func=AF.Exp, scale=-1.0), r=B_rA, w=B_rA)
            S.op("dve", lambda e, o_ap=o_ap, g=g: e.scalar_tensor_tensor(
                out=oTb[:, 2 * g:2 * g + 2, :], in0=o_ap.rearrange("p (h t) -> p h t", h=2), scalar=0.5,
                in1=rec.rearrange("p (h t) -> p h t", h=2), op0=ALU.mult, op1=ALU.mult),
                r=o_pq + B_rA, w=[B_oTb[2 * g], B_oTb[2 * g + 1]])
        bg.flush()
        if not last:
            push_rope(c + 1)
        s5_s6(c)

    load_x(0, 0)
    for c_ in range(nch):
        phase2_chunk(c_, c_ == 0, c_ == nch - 1)
    bg.flush()

    S.barrier(dsems)
    if stage == 2:
        sdbg = newsem("dbg")
        d_ = nc.dram_tensor("dbg_h1", [S_LEN, D], F32, kind="ExternalOutput").ap()
        if sub >= 5:
            for i_ in range(nch):
                S.dma("sp", d_[i_ * 256:(i_ + 1) * 256, :], h1_d[i_ * 256:(i_ + 1) * 256, :], sdbg)
        for nm, ap_, n_ in (("dbg_qT", qT, 4 * CH2), ("dbg_oTa", oTa, 4 * CH2), ("dbg_oTb", oTb, 4 * CH2)):
            dd_ = nc.dram_tensor(nm, [128, n_], BF16, kind="ExternalOutput").ap()
            S.dma("sp", dd_, ap_.rearrange("p a b -> p (a b)"), sdbg)
        S.wait_all("sp", dsems)
        S.emit()
        es.close()
        return nc
    A3 = Alloc(P12_BASE, SBYTES)
    w_g = A3.get([128, 8, DFF], BF16)
    w_v = A3.get([128, 8, DFF], BF16)
    w_dn = A3.get([128, NFC, D], BF16)
    lnp3 = A3.get([128, 4, D], F32)
    hh = A3.get([128, 4, D], F32)
    h1T = A3.get([128, 8, CH3 + 2], BF16)
    halo = A3.get([128, 2, 8], F32)
    gexx = A3.get([128, 1056], F32)
    gext_ = [gexx[:, 0:CH3 + 2], gexx[:, 528:528 + CH3 + 2]]
    alt2 = gexx[:, 0:2 * CH3]
    t3aa = A3.get([128, 2 * CH3], F32)
    t3a_ = [t3aa[:, 0:CH3], t3aa[:, CH3:2 * CH3]]
    hb3_off = A3.off
    t3b_ = [A3.get([128, CH3], F32)] * 2
    hb3 = Alloc(hb3_off, A3.off).get([128, D], BF16)
    actT = A3.get([128, NFC, CH3], BF16)
    B_wg, B_wv, B_wd, B_lnp3 = Buf("wg"), Buf("wv"), Buf("wd"), Buf("lnp3")
    B_hh = [Buf("hh%d" % j) for j in range(4)]
    B_h1Th, B_halo = Buf("h1Th"), Buf("halo")
    B_gext_ = [Buf("gext0"), Buf("gext1")]
    B_gexth_ = [Buf("gexth0"), Buf("gexth1")]
    B_t3a_ = [Buf("t3a0"), Buf("t3a1")]
    B_t3b_ = [Buf("t3b0")] * 2
    B_hb3 = B_t3b_[0]
    B_actT = [Buf("actT%d" % i) for i in range(NFC)]
    wg_v = wg_d.rearrange("(k p) n -> p k n", p=128)
    wv_v = wv_d.rearrange("(k p) n -> p k n", p=128)
    WPC = [(0, 256), (256, 768), (768, 1408), (1408, 2176), (2176, 2816)]
    B_wgp = [Buf("wg%d" % i) for i in range(len(WPC))]
    B_wvp = [Buf("wv%d" % i) for i in range(len(WPC))]
    for i_, (c0, c1) in enumerate(WPC):
        S.dma("pool", w_g[:, :, c0:c1], wg_v[:, :, c0:c1], newsem("wg%d" % i_), w=[B_wgp[i_]])
        S.dma("pool", w_v[:, :, c0:c1], wv_v[:, :, c0:c1], newsem("wv%d" % i_), w=[B_wvp[i_]])

    def wpiece(fc):
        for i_, (c0, c1) in enumerate(WPC):
            if c0 <= fc * 128 < c1:
                return i_
    wd_v = wd_d.rearrange("(k p) n -> p k n", p=128)
    s_w3d = newsem("w3d")
    for k0 in range(0, NFC, 11):
        S.dma("pool", w_dn[:, k0:k0 + 11, :], wd_v[:, k0:k0 + 11, :], s_w3d, w=[B_wd])
    s_m3 = newsem("m3")
    S.dma("sp", lnp3, lnp_d[:, 2:6, :], s_m3, w=[B_lnp3])
    sem_halo = newsem("halo")
    sem_hhj = [newsem("hh%d" % j) for j in range(4)]
    B_h1Tj = [Buf("h1T%d" % j) for j in range(4)]

    sem_alt = newsem("hhalt")
    sem_alt2 = newsem("hhalt2")

    def prep_stages(c, j, alt=False):
        t0 = c * CH3
        if alt and j == 3:
            ht, Bh, sem_l = t3aa, [B_t3a_[0], B_t3a_[1]], sem_alt
        elif alt:
            ht, Bh, sem_l = alt2, [B_gext_[0], B_gext_[1], B_gexth_[0], B_gexth_[1]], sem_alt2
        else:
            ht, Bh, sem_l = hh[:, j, :], [B_hh[j]], sem_hhj[j]
        pb, pq = pbank(6 + j % 2)
        pbb = pb.bitcast(BF16)

        def p0():
            S.dma("sp", ht, h1_d[t0 + j * 128:t0 + (j + 1) * 128, :], sem_l, w=Bh)
            if j == 0:
                for side, tk in ((0, t0 - 1), (1, t0 + CH3)):
                    if 0 <= tk < S_LEN:
                        S.dma("sp", halo[:, side, :], h1_d[tk:tk + 1, :].rearrange("o (k p) -> p (o k)", p=128),
                              sem_halo, w=[B_halo], allow_slow_non_contiguous=True)

        def p1():
            S.op("dve", lambda e: e.tensor_tensor(out=ht, in0=ht, in1=lnp3[:, 0, :], op=ALU.mult),
                 r=Bh + [B_lnp3], w=Bh)

        def p2():
            S.op("dve", lambda e: e.tensor_tensor(out=ht, in0=ht, in1=lnp3[:, 1, :], op=ALU.add),
                 r=Bh + [B_lnp3], w=Bh)

        def p3():
            S.op("act", lambda e: e.activation(out=hb3, in_=ht, func=AF.Copy), r=Bh, w=[B_hb3])

        def p4():
            for kc in range(8):
                S.op("pe", lambda e, kc=kc: e.transpose(out=pbb[:, kc * 128:(kc + 1) * 128],
                                                        in_=hb3[:, kc * 128:(kc + 1) * 128], identity=ident),
                     r=[B_hb3, B_const], w=pq, track=(kc == 7))

        def p5():
            S.op("act", lambda e: e.activation(
                out=h1T[:, :, 1 + j * 128:1 + (j + 1) * 128], in_=pbb.rearrange("p (k t) -> p k t", k=8), func=AF.Copy),
                r=pq, w=[B_h1Tj[j]])
        return [p0, p1, p2, p3, p4, p5]

    def halo_finish(c):
        t0 = c * CH3
        for side, tk, col in ((0, t0 - 1, 0), (1, t0 + CH3, CH3 + 1)):
            if 0 <= tk < S_LEN:
                S.op("dve", lambda e, side=side: e.tensor_tensor(out=halo[:, side, :], in0=halo[:, side, :],
                                                                 in1=g1fm[:, 0, :], op=ALU.mult),
                     r=[B_halo, B_const], w=[B_halo])
                S.op("dve", lambda e, side=side, col=col: e.tensor_tensor(out=h1T[:, :, col], in0=halo[:, side, :],
                                                                          in1=g1fm[:, 1, :], op=ALU.add),
                     r=[B_halo, B_const], w=[B_h1Th])
            else:
                S.op("dve", lambda e, col=col: e.memset(h1T[:, :, col], 0.0), w=[B_h1Th])

    def phase3_chunk(c):
        t0 = c * CH3
        if c == 0:
            for j in range(4):
                for f_ in prep_stages(c, j):
                    f_()
        halo_finish(c)
        bg.flush()
        hr3 = B_h1Tj + [B_h1Th]
        for fc in range(NFC):
            fs = slice(fc * 128, (fc + 1) * 128)
            gext, t3a, t3b = gext_[fc % 2], t3a_[fc % 2], t3b_[fc % 2]
            B_gext, B_gexth, B_t3a, B_t3b = B_gext_[fc % 2], B_gexth_[fc % 2], B_t3a_[fc % 2], B_t3b_[fc % 2]
            gp, gq_ = pbank(fc % 2)
            vp, vq_ = pbank(2 + fc % 2)
            ghp_full, ghq_full = pbank(4 + fc % 2)
            ghp = ghp_full[:, 0:2]
            for kc in range(8):
                S.op("pe", lambda e, kc=kc, gp=gp, fs=fs: e.matmul(gp, lhsT=w_g[:, kc, fs], rhs=h1T[:, kc, 1:CH3 + 1],
                                                                   start=(kc == 0), stop=(kc == 7)),
                     r=hr3 + [B_wgp[wpiece(fc)]], w=gq_, track=(kc == 7))
                S.op("pe", lambda e, kc=kc, ghp=ghp, fs=fs: e.matmul(ghp, lhsT=w_g[:, kc, fs],
                                                                     rhs=h1T[:, kc, 0:CH3 + 2:CH3 + 1],
                                                                     start=(kc == 0), stop=(kc == 7)),
                     r=hr3 + [B_wgp[wpiece(fc)]], w=[ghq_full[0]], track=(kc == 7))
            mm_group(vp, vq_, [(w_v[:, kc, fs], h1T[:, kc, 1:CH3 + 1]) for kc in range(8)], r=hr3 + [B_wvp[wpiece(fc)]])
            S.op("act", lambda e, gp=gp, gext=gext: e.activation(out=gext[:, 1:CH3 + 1], in_=gp, func=AF.Copy), r=gq_, w=[B_gext])
            S.op("dve", lambda e, ghp=ghp, gext=gext: e.tensor_copy(out=gext[:, 0:CH3 + 2:CH3 + 1], in_=ghp),
                 r=[ghq_full[0]], w=[B_gexth])
            ge = [B_gext, B_gexth]
            S.op("dve", lambda e, fc=fc, gext=gext, t3a=t3a, t3b=t3b: e.tensor_scalar(out=t3a, in0=gext[:, 0:CH3], scalar1=cwfm[:, fc, 0:1],
                                                         scalar2=None, op0=ALU.mult), r=ge + [B_const], w=[B_t3a])
            S.op("dve", lambda e, fc=fc, gext=gext, t3a=t3a, t3b=t3b: e.scalar_tensor_tensor(out=t3a, in0=gext[:, 1:CH3 + 1], scalar=cwfm[:, fc, 1:2],
                                                                in1=t3a, op0=ALU.mult, op1=ALU.add),
                 r=ge + [B_const, B_t3a], w=[B_t3a])
            S.op("dve", lambda e, fc=fc, gext=gext, t3a=t3a, t3b=t3b: e.scalar_tensor_tensor(out=t3a, in0=gext[:, 2:CH3 + 2], scalar=cwfm[:, fc, 2:3],
                                                                in1=t3a, op0=ALU.mult, op1=ALU.add),
                 r=ge + [B_const, B_t3a], w=[B_t3a])
            S.op("act", lambda e, fc=fc, gext=gext, t3a=t3a, t3b=t3b: e.activation(out=t3b, in_=t3a, func=AF.Gelu, bias=cwfm[:, fc, 3:4], scale=1.0),
                 r=[B_t3a, B_const], w=[B_t3b])
            S.op("dve", lambda e, fc=fc, vp=vp, t3b=t3b: e.tensor_tensor(out=actT[:, fc, :], in0=t3b, in1=vp, op=ALU.mult),
                 r=vq_ + [B_t3b], w=[B_actT[fc]])
        if c + 1 < NCH3:
            bg.push(prep_stages(c + 1, 2, alt=True), stride=1, offset=0)
            bg.push(prep_stages(c + 1, 3, alt=True), stride=1, offset=1)
        for j in range(4):
            ht = hh[:, j, :]
            for nh in range(2):
                yp, yq = pbank((2 * j + nh) % 4)
                for fc in range(NFC):
                    S.op("pe", lambda e, fc=fc, yp=yp, j=j, nh=nh: e.matmul(
                        yp, lhsT=actT[:, fc, j * 128:(j + 1) * 128], rhs=w_dn[:, fc, nh * 512:(nh + 1) * 512],
                        start=(fc == 0), stop=(fc == NFC - 1)), r=[B_actT[fc], B_wd], w=yq, track=(fc == NFC - 1))
                bg.tick()
                S.op("dve", lambda e, ht=ht, yp=yp, nh=nh: e.scalar_tensor_tensor(
                    out=ht[:, nh * 512:(nh + 1) * 512], in0=ht[:, nh * 512:(nh + 1) * 512], scalar=ALPHA, in1=yp,
                    op0=ALU.mult, op1=ALU.add), r=yq + [B_hh[j]], w=[B_hh[j]])
            ln_stats(ht, j, B_hh[j])
            S.op("dve", lambda e, ht=ht, j=j: e.scalar_tensor_tensor(
                out=ht, in0=ht, scalar=mv[:, j, 0:1], in1=lnp3[:, 2, :], op0=ALU.subtract, op1=ALU.mult),
                r=[B_hh[j], B_mv[j], B_lnp3], w=[B_hh[j]])
            S.op("dve", lambda e, ht=ht, j=j: e.scalar_tensor_tensor(
                out=ht, in0=ht, scalar=rstd[:, j:j + 1], in1=lnp3[:, 3, :], op0=ALU.mult, op1=ALU.add),
                r=[B_hh[j], B_rstd[j], B_lnp3], w=[B_hh[j]])
            S.dma("sp", out_d[t0 + j * 128:t0 + (j + 1) * 128, :], ht, sem_hhj[j], r=[B_hh[j]])
            if c + 1 < NCH3:
                if j < 2:
                    bg.push(prep_stages(c + 1, j), stride=1, offset=1)
                elif j == 2:
                    bg.push_at(1, lambda: S.op("dve", lambda e: e.tensor_copy(out=hh[:, 2, :], in_=alt2),
                                               r=[B_gext_[0], B_gext_[1], B_gexth_[0], B_gexth_[1]], w=[B_hh[2]]))
                else:
                    bg.push_at(1, lambda: S.op("dve", lambda e: e.tensor_copy(out=hh[:, 3, :], in_=t3aa),
                                               r=[B_t3a_[0], B_t3a_[1]], w=[B_hh[3]]))
    for c_ in range(NCH3):
        phase3_chunk(c_)
    bg.flush()
    S.wait_all("sp", dsems)
    S.emit()
    es.close()
    return nc


def _host_consts():
    bf = ml_dtypes.bfloat16
    ident = np.eye(128, dtype=np.float32).astype(bf)
    ones = np.ones((128, 128), dtype=np.float32).astype(bf)
    R = np.zeros((128, 128), dtype=np.float32)
    for d in range(128):
        if d % 64 < 32:
            R[d, d + 32] = -1.0
        else:
            R[d, d - 32] = 1.0
    rt = np.ascontiguousarray(R.T).astype(bf)
    freqs = (np.float32(10000.0) ** (-(np.arange(32, dtype=np.float32) / np.float32(32)))).astype(np.float32)
    pos = np.arange(64, dtype=np.float32)
    ang = (pos[None, :] * freqs[:, None]).astype(np.float32)
    ang128 = np.tile(ang, (4, 1))
    rope = np.stack([np.cos(ang128), np.sin(ang128)], axis=1).astype(np.float32)
    a = np.arange(128)[:, None]
    b = np.arange(128)[None, :]
    dist_t = np.zeros((128, 3, 128), dtype=np.float32)
    for rel in range(3):
        dist = np.abs(b - a - (rel - 1) * 128)
        dist_t[:, rel, :] = np.where(dist <= 128, -dist, -30000.0)
    dist_t = dist_t.astype(bf)
    ident_s = np.zeros((128, 8, 128), dtype=np.float32)
    for h in range(8):
        ident_s[:, h, :] = np.eye(128, dtype=np.float32) * (8.0 * 2.0 ** (-(h + 1)))
    ident_s = ident_s.astype(bf)
    bias = (ident_s, dist_t)
    return ident, ones, rt, rope, bias


_NC_CACHE = {}


def make_shared(x, ln_in_g, ln_in_b, w_in, b_in, a_sinks, b_q_norm, b_k_norm, w_o_a, w_o_b, w_out,
           ln1_g, ln1_b, w_ffn_gate, w_ffn_val, ffn_conv_w, ffn_conv_b, w_ffn_down, ln2_g, ln2_b):
    f = lambda t: np.ascontiguousarray(np.asarray(t, dtype=np.float32))
    x = f(x)
    b_in0 = f(b_in)[0]
    ident, ones, rt, rope, bias = _host_consts()
    lnp = np.stack([f(ln_in_g), f(ln_in_b), f(ln1_g)[0], f(ln1_b)[0], f(ln2_g)[0], f(ln2_b)[0]], 0)
    lnp = np.ascontiguousarray(np.broadcast_to(lnp[None], (128, 6, D)))
    bin_fm = np.ascontiguousarray(b_in0.reshape(30, 128).T)
    bka = np.stack([np.tile(b_in0[C_KA:C_KA + 64], 2), np.tile(b_in0[C_KA + 64:C_KA + 128], 2)], 1)
    bv = np.concatenate([b_in0[C_VA:C_VA + 128], b_in0[C_VB:C_VB + 256]])
    bv_bc = np.ascontiguousarray(np.broadcast_to(bv[None], (128, 384)))
    ln1_fm = np.stack([f(ln1_g)[0].reshape(8, 128).T, f(ln1_b)[0].reshape(8, 128).T], 1)
    cw = np.concatenate([f(ffn_conv_w)[0], f(ffn_conv_b)], 0)
    cw_fm = np.ascontiguousarray(cw.reshape(4, NFC, 128).transpose(2, 1, 0))
    sinks_bc = np.ascontiguousarray(np.broadcast_to(f(a_sinks)[0][None], (128, 8)))
    gqk = np.stack([f(b_q_norm)[0], f(b_k_norm)[0]], 1)
    shared = {
        "w_in": f(w_in)[0], "w_o_a": f(w_o_a)[0], "w_o_b": f(w_o_b)[0], "w_out": f(w_out)[0],
        "w_g": f(w_ffn_gate)[0], "w_v": f(w_ffn_val)[0], "w_d": f(w_ffn_down)[0],
        "lnp": lnp, "bin_fm": bin_fm, "bka_dup": np.ascontiguousarray(bka), "bv_bc": bv_bc,
        "ln1_fm": np.ascontiguousarray(ln1_fm), "cw_fm": cw_fm, "sinks_bc": sinks_bc,
        "gqk_fm": np.ascontiguousarray(gqk), "ident": ident, "ones": ones, "rt": rt,
        "rope_rc": np.ascontiguousarray(rope), "ident_s": bias[0], "dist_t": bias[1],
    }
    return x, shared


def kernel(x, ln_in_g, ln_in_b, w_in, b_in, a_sinks, b_q_norm, b_k_norm, w_o_a, w_o_b, w_out,
           ln1_g, ln1_b, w_ffn_gate, w_ffn_val, ffn_conv_w, ffn_conv_b, w_ffn_down, ln2_g, ln2_b):
    x, shared = make_shared(x, ln_in_g, ln_in_b, w_in, b_in, a_sinks, b_q_norm, b_k_norm, w_o_a, w_o_b, w_out,
                            ln1_g, ln1_b, w_ffn_gate, w_ffn_val, ffn_conv_w, ffn_conv_b, w_ffn_down, ln2_g, ln2_b)
    if "nc" not in _NC_CACHE:
        _NC_CACHE["nc"] = build_nc()
    nc = _NC_CACHE["nc"]
    in_maps = []
    for b in range(8):
        m = dict(shared)
        m["x"] = x[b]
        in_maps.append(m)
    res = run_bass_kernel_spmd(nc, in_maps, core_ids=list(range(8)))
    return np.stack([np.asarray(r["out"], dtype=np.float32) for r in res.results], 0)
```
